# Optimizing a Trainium2 kernel written in Bass

```python
import math
import jax, jax.numpy as jnp
from jax import lax
import numpy as np

D_MODEL = 2048
BATCH = 4
SEQ = 2048
DEPTH = 4

HEAD_DIM = 128
MEM_LEN = 256
MEM_HEADS = 4
A_HEADS = 6
A_PATTERNS = ((128, 1), (512, 4), (2048, 16))
B_HEADS = 6
MOBA_BLOCK = 256
MOBA_TOPK = 3
MOBA_QCHUNK = 32
C_HEADS = 24
C_KV_HEADS = 3
C_HEAD_DIM = 64
C_WINDOW = 128
BAND_BLOCK = 128
ROPE_THETA = 10000.0
EPS = 1e-6
NEG = -1e30

EVEN_WIDTH = (A_HEADS + B_HEADS + MEM_HEADS) * HEAD_DIM
ODD_WIDTH = C_HEADS * C_HEAD_DIM + MEM_HEADS * HEAD_DIM
EVEN_SPLITS = [A_HEADS * HEAD_DIM] * 3 + [B_HEADS * HEAD_DIM] * 3 + [MEM_HEADS * HEAD_DIM, EVEN_WIDTH]
ODD_SPLITS = [C_HEADS * C_HEAD_DIM, C_KV_HEADS * C_HEAD_DIM, C_KV_HEADS * C_HEAD_DIM, MEM_HEADS * HEAD_DIM, ODD_WIDTH]
EVEN_IN = sum(EVEN_SPLITS)
ODD_IN = sum(ODD_SPLITS)
EVEN_OFFSETS = [sum(EVEN_SPLITS[:i + 1]) for i in range(len(EVEN_SPLITS) - 1)]
ODD_OFFSETS = [sum(ODD_SPLITS[:i + 1]) for i in range(len(ODD_SPLITS) - 1)]

kernel_name = "hybrid_dilated_moba_swa_sink_decoder"


def rmsnorm(x, g):
    xf = x.astype(jnp.float32)
    y = xf * lax.rsqrt(jnp.mean(xf * xf, axis=-1, keepdims=True) + EPS)
    return (y * g.astype(jnp.float32)).astype(x.dtype)


def rope(x, pos):
    d = x.shape[-1]
    half = d // 2
    inv_freq = jnp.exp(jnp.arange(half, dtype=jnp.float32) * (-2.0 * math.log(ROPE_THETA) / d))
    ang = pos.astype(jnp.float32)[:, None, :, None] * inv_freq
    cos, sin = jnp.cos(ang), jnp.sin(ang)
    xf = x.astype(jnp.float32)
    x1, x2 = xf[..., :half], xf[..., half:]
    return jnp.concatenate([x1 * cos - x2 * sin, x2 * cos + x1 * sin], axis=-1).astype(x.dtype)


def split_heads(t, n):
    b, s, _ = t.shape
    return t.reshape(b, s, n, -1).transpose(0, 2, 1, 3)


def merge_heads(t):
    b, h, s, d = t.shape
    return t.transpose(0, 2, 1, 3).reshape(b, s, h * d)


def banded_attention(q, k, v, max_dist, sink=None):
    n, r, L, d = q.shape
    blk = BAND_BLOCK
    nb = -(-L // blk)
    pad = nb * blk - L
    qp = jnp.pad(q, ((0, 0), (0, 0), (0, pad), (0, 0))).reshape(n, r, nb, blk, d)
    kc = jnp.pad(k, ((0, 0), (0, pad), (0, 0))).reshape(n, nb, blk, d)
    vc = jnp.pad(v, ((0, 0), (0, pad), (0, 0))).reshape(n, nb, blk, d)
    kb = jnp.concatenate([jnp.pad(kc[:, :-1], ((0, 0), (1, 0), (0, 0), (0, 0))), kc], axis=2)
    vb = jnp.concatenate([jnp.pad(vc[:, :-1], ((0, 0), (1, 0), (0, 0), (0, 0))), vc], axis=2)
    s = jnp.einsum('nrbqd,nbkd->nrbqk', qp, kb, preferred_element_type=jnp.float32) * (d ** -0.5)
    qi = jnp.arange(blk)[:, None]
    ki = jnp.arange(2 * blk)[None, :]
    dist = qi + blk - ki
    k_abs = jnp.arange(nb)[:, None, None] * blk - blk + ki[None]
    mask = ((dist >= 0) & (dist <= max_dist))[None] & (k_abs >= 0)
    s = jnp.where(mask, s, NEG)
    lse = jax.nn.logsumexp(s, axis=-1)
    if sink is not None:
        lse = jnp.logaddexp(lse, sink[:, :, None, None])
    p = jnp.exp(s - lse[..., None])
    out = jnp.einsum('nrbqk,nbkd->nrbqd', p.astype(v.dtype), vb)
    return out.reshape(n, r, nb * blk, d)[:, :, :L], lse.reshape(n, r, nb * blk)[:, :, :L]


def dilated_mixture_attention(q, k, v):
    b, h, s, d = q.shape
    outs, lses = [], []
    for window, dil in A_PATTERNS:
        L = s // dil

        def fold(t):
            return t.reshape(b, h, L, dil, d).transpose(0, 1, 3, 2, 4).reshape(b * h * dil, L, d)

        o, l = banded_attention(fold(q)[:, None], fold(k), fold(v), window // dil)
        outs.append(o[:, 0].reshape(b, h, dil, L, d).transpose(0, 1, 3, 2, 4).reshape(b, h, s, d))
        lses.append(l[:, 0].reshape(b, h, dil, L).transpose(0, 1, 3, 2).reshape(b, h, s))
    w = jax.nn.softmax(jnp.stack(lses), axis=0)
    out = jnp.einsum('gbhs,gbhsd->bhsd', w, jnp.stack(outs).astype(jnp.float32))
    return out.astype(q.dtype)


def moba_attention(q, k, v):
    b, h, s, d = q.shape
    scale = d ** -0.5
    nblk = -(-s // MOBA_BLOCK)
    sp = nblk * MOBA_BLOCK
    kp = jnp.pad(k, ((0, 0), (0, 0), (0, sp - s), (0, 0)))
    vp = jnp.pad(v, ((0, 0), (0, 0), (0, sp - s), (0, 0)))
    kblk = kp.reshape(b, h, nblk, MOBA_BLOCK, d)
    vblk = vp.reshape(b, h, nblk, MOBA_BLOCK, d)
    kmean = jnp.mean(kblk.astype(jnp.float32), axis=3)
    gate = jnp.einsum('bhsd,bhnd->bhsn', q.astype(jnp.float32), kmean)
    own = jnp.arange(s) // MOBA_BLOCK
    past = jnp.arange(nblk)[None, :] < own[:, None]
    gate = jnp.where(past, gate, NEG)
    topk = min(MOBA_TOPK, nblk)
    _, sel = lax.top_k(gate, topk)
    sel_valid = jnp.arange(topk)[None, :] < own[:, None]
    qc = MOBA_QCHUNK
    nq = s // qc
    nsel = topk * MOBA_BLOCK
    bi = jnp.arange(b)[:, None, None]
    hi = jnp.arange(h)[None, :, None]

    def chunk(args):
        c, q_c, sel_c, valid_c = args
        flat = sel_c.reshape(b, h, qc * topk)
        k_sel = kblk[bi, hi, flat].reshape(b, h, qc, nsel, d)
        v_sel = vblk[bi, hi, flat].reshape(b, h, qc, nsel, d)
        s_sel = jnp.einsum('bhqd,bhqkd->bhqk', q_c, k_sel, preferred_element_type=jnp.float32) * scale
        s_sel = jnp.where(jnp.repeat(valid_c, MOBA_BLOCK, axis=1), s_sel, NEG)
        start = (c * qc) // MOBA_BLOCK * MOBA_BLOCK
        k_own = lax.dynamic_slice_in_dim(kp, start, MOBA_BLOCK, axis=2)
        v_own = lax.dynamic_slice_in_dim(vp, start, MOBA_BLOCK, axis=2)
        s_own = jnp.einsum('bhqd,bhkd->bhqk', q_c, k_own, preferred_element_type=jnp.float32) * scale
        qpos = c * qc + jnp.arange(qc)
        kpos = start + jnp.arange(MOBA_BLOCK)
        s_own = jnp.where(kpos[None, :] <= qpos[:, None], s_own, NEG)
        p = jax.nn.softmax(jnp.concatenate([s_sel, s_own], axis=-1), axis=-1).astype(v.dtype)
        return (jnp.einsum('bhqk,bhqkd->bhqd', p[..., :nsel], v_sel)
                + jnp.einsum('bhqk,bhkd->bhqd', p[..., nsel:], v_own))

    xs = (jnp.arange(nq),
          q.reshape(b, h, nq, qc, d).transpose(2, 0, 1, 3, 4),
          sel.reshape(b, h, nq, qc, topk).transpose(2, 0, 1, 3, 4),
          sel_valid.reshape(nq, qc, topk))
    out = lax.map(chunk, xs)
    return out.transpose(1, 2, 0, 3, 4).reshape(b, h, s, d)


def swa_sink_attention(q, k, v, sinks):
    b, hq, s, d = q.shape
    g = k.shape[1]
    r = hq // g
    sink = jnp.broadcast_to(sinks.astype(jnp.float32).reshape(1, g, r), (b, g, r)).reshape(b * g, r)
    o, _ = banded_attention(q.reshape(b * g, r, s, d), k.reshape(b * g, s, d), v.reshape(b * g, s, d),
                            C_WINDOW - 1, sink)
    return o.reshape(b, hq, s, d)


def memory_attention(q, mk, mv):
    s = jnp.einsum('bhsd,bhmd->bhsm', q, mk, preferred_element_type=jnp.float32) * (q.shape[-1] ** -0.5)
    p = jax.nn.softmax(s, axis=-1).astype(mv.dtype)
    return jnp.einsum('bhsm,bhmd->bhsd', p, mv)


def memory_kv(mem_n, w_mem_kv):
    mk, mv = jnp.split(mem_n @ w_mem_kv, 2, axis=-1)
    return split_heads(mk, MEM_HEADS), split_heads(mv, MEM_HEADS)


def even_mixer(h, pos, mem_n, w_in, w_mem_kv, w_out):
    qa, ka, va, qb, kb, vb, qm, gate = jnp.split(h @ w_in, EVEN_OFFSETS, axis=-1)
    mk, mv = memory_kv(mem_n, w_mem_kv)
    ya = dilated_mixture_attention(rope(split_heads(qa, A_HEADS), pos), rope(split_heads(ka, A_HEADS), pos),
                                   split_heads(va, A_HEADS))
    yb = moba_attention(rope(split_heads(qb, B_HEADS), pos), rope(split_heads(kb, B_HEADS), pos),
                        split_heads(vb, B_HEADS))
    ym = memory_attention(split_heads(qm, MEM_HEADS), mk, mv)
    y = jnp.concatenate([merge_heads(ya), merge_heads(yb), merge_heads(ym)], axis=-1) * jax.nn.silu(gate)
    return y @ w_out


def odd_mixer(h, pos, mem_n, w_in, w_mem_kv, w_out, sinks):
    qc, kc, vc, qm, gate = jnp.split(h @ w_in, ODD_OFFSETS, axis=-1)
    mk, mv = memory_kv(mem_n, w_mem_kv)
    yc = swa_sink_attention(rope(split_heads(qc, C_HEADS), pos), rope(split_heads(kc, C_KV_HEADS), pos),
                            split_heads(vc, C_KV_HEADS), sinks)
    ym = memory_attention(split_heads(qm, MEM_HEADS), mk, mv)
    y = jnp.concatenate([merge_heads(yc), merge_heads(ym)], axis=-1) * jax.nn.silu(gate)
    return y @ w_out


def setup_inputs(seed: int = 0) -> dict:
    key = jax.random.key(seed)
    ks = jax.random.split(key, 16)
    n_even = (DEPTH + 1) // 2
    n_odd = DEPTH // 2
    f32 = jnp.float32
    x = jax.random.normal(ks[0], (BATCH, SEQ, D_MODEL), f32)
    mem = jax.random.normal(ks[1], (BATCH, MEM_LEN, D_MODEL), f32)
    offset = jax.random.randint(ks[2], (BATCH, 1), 0, 4096, dtype=jnp.int32)
    positions = (jnp.arange(SEQ, dtype=jnp.int32)[None, :] + offset).astype(jnp.int32)
    din = D_MODEL ** -0.5
    return {
        'x': x,
        'mem': mem,
        'positions': positions,
        'even_norm': 1.0 + 0.02 * jax.random.normal(ks[3], (n_even, D_MODEL), f32),
        'even_w_in': jax.random.normal(ks[4], (n_even, D_MODEL, EVEN_IN), f32) * din,
        'even_w_mem_kv': jax.random.normal(ks[5], (n_even, D_MODEL, 2 * MEM_HEADS * HEAD_DIM), f32) * din,
        'even_w_out': jax.random.normal(ks[6], (n_even, EVEN_WIDTH, D_MODEL), f32) * EVEN_WIDTH ** -0.5,
        'odd_norm': 1.0 + 0.02 * jax.random.normal(ks[7], (n_odd, D_MODEL), f32),
        'odd_w_in': jax.random.normal(ks[8], (n_odd, D_MODEL, ODD_IN), f32) * din,
        'odd_w_mem_kv': jax.random.normal(ks[9], (n_odd, D_MODEL, 2 * MEM_HEADS * HEAD_DIM), f32) * din,
        'odd_w_out': jax.random.normal(ks[10], (n_odd, ODD_WIDTH, D_MODEL), f32) * ODD_WIDTH ** -0.5,
        'odd_sinks': 0.5 * jax.random.normal(ks[11], (n_odd, C_HEADS), f32),
        'mem_norm': 1.0 + 0.02 * jax.random.normal(ks[12], (D_MODEL,), f32),
        'final_norm': 1.0 + 0.02 * jax.random.normal(ks[13], (D_MODEL,), f32),
    }


def reference(x, mem, positions, even_norm, even_w_in, even_w_mem_kv, even_w_out,
              odd_norm, odd_w_in, odd_w_mem_kv, odd_w_out, odd_sinks, mem_norm, final_norm):
    mem_n = rmsnorm(mem, mem_norm)
    for layer in range(DEPTH):
        i = layer // 2
        if layer % 2 == 0:
            x = x + even_mixer(rmsnorm(x, even_norm[i]), positions, mem_n,
                               even_w_in[i], even_w_mem_kv[i], even_w_out[i])
        else:
            x = x + odd_mixer(rmsnorm(x, odd_norm[i]), positions, mem_n,
                              odd_w_in[i], odd_w_mem_kv[i], odd_w_out[i], odd_sinks[i])
    return rmsnorm(x, final_norm)
```

```python
import math
from contextlib import ExitStack
import numpy as np
import concourse.bass as bass
import concourse.mybir as mybir
from concourse.bass_utils import run_bass_kernel_spmd

F32 = mybir.dt.float32
BF16 = mybir.dt.bfloat16
I32 = mybir.dt.int32
AF = mybir.ActivationFunctionType
ALU = mybir.AluOpType
AX = mybir.AxisListType

S = 2048
DM = 2048
EPS = 1e-6
EVEN_OFF = dict(qa=0, ka=384, va=768, qb=1152, kb=1536, vb=1920, qm=2304, gate=2560)
ODD_OFF = dict(qc=0, kc=768, vc=896, qm=1024, gate=1280)
EVEN_ORDER = [0, 1, 2, 6, 7, 8, 12, 13, 3, 4, 5, 9, 10, 11, 14, 15]
ODD_ORDER = [0, 1, 2, 3, 4, 5, 12, 13, 8, 9, 10, 11, 6, 7, 14, 15]
PAIRS = [[0, 1], [2, 3], [4, 5], [6, 7]]
EVEN_PROC = [6, 7, 0, 1, 2, 3, 4, 5]
TWO_PI = 2.0 * math.pi
CW1 = 6.28125
CW2 = TWO_PI - CW1


class Buf:
    __slots__ = ("name", "w", "r")

    def __init__(self, name=""):
        self.name = name
        self.w = None
        self.r = {}


class DSem:
    def __init__(self, key):
        self.key = key
        self.count = 0


class Eng:
    def __init__(self, name):
        self.name = name
        self.ops = []
        self.count = 0
        self.waited = {}
        self.key = "E_" + name


class Emitter:
    def __init__(self):
        self.engs = {n: Eng(n) for n in ("pe", "act", "dve", "pool", "sp")}
        self.dsems = []

    def new_dsem(self):
        d = DSem("D_%d" % len(self.dsems))
        self.dsems.append(d)
        return d

    def _deps(self, eng, reads, writes, is_dma):
        deps = {}

        def add(semkey, val, ename, kind):
            if (not is_dma) and ename == eng.name and kind != "raw":
                return
            if eng.waited.get(semkey, 0) >= val:
                return
            if deps.get(semkey, 0) < val:
                deps[semkey] = val

        for b in reads:
            if b.w is not None:
                add(b.w[0], b.w[1], b.w[2], "raw")
        for b in writes:
            if b.w is not None:
                add(b.w[0], b.w[1], b.w[2], "waw")
            for k, (v, en) in b.r.items():
                add(k, v, en, "war")
        return deps

    def _wait(self, eng, deps):
        for k, v in deps.items():
            eng.ops.append(("wait", k, v))
            eng.waited[k] = v

    def _commit(self, tok, reads, writes):
        k, v, en = tok
        for b in reads:
            b.r[k] = (v, en)
        for b in writes:
            b.w = tok
            b.r = {}

    def op(self, engname, fn, reads=(), writes=()):
        eng = self.engs[engname]
        self._wait(eng, self._deps(eng, reads, writes, False))
        eng.count += 1
        eng.ops.append(("op", fn))
        self._commit((eng.key, eng.count, eng.name), reads, writes)

    def dma(self, parts, dsem, reads=(), writes=()):
        for (q, o, i) in parts:
            eng = self.engs[q]
            self._wait(eng, self._deps(eng, reads, writes, True))
            eng.ops.append(("dma", o, i, dsem.key))
            dsem.count += 16
        self._commit((dsem.key, dsem.count, "dma"), reads, writes)

    def cc(self, ins, outs, dsem, reads=(), writes=()):
        eng = self.engs["pool"]
        self._wait(eng, self._deps(eng, reads, writes, True))
        eng.ops.append(("cc", ins, outs, dsem.key))
        dsem.count += 1
        self._commit((dsem.key, dsem.count, "dma"), reads, writes)

    def fence(self):
        for e in self.engs.values():
            deps = {}
            for e2 in self.engs.values():
                if e2 is not e and e2.count > e.waited.get(e2.key, 0):
                    deps[e2.key] = e2.count
            for d in self.dsems:
                if d.count > e.waited.get(d.key, 0):
                    deps[d.key] = d.count
            self._wait(e, deps)

    def replay(self, nc):
        with ExitStack() as es:
            sems = {}
            for e in self.engs.values():
                sems[e.key] = es.enter_context(nc.semaphore("s_" + e.name))
            for d in self.dsems:
                sems[d.key] = es.enter_context(nc.semaphore("s_" + d.key))
            block = es.enter_context(nc.Block())

            def run(eng, h):
                for o in eng.ops:
                    if o[0] == "wait":
                        h.wait_ge(sems[o[1]], o[2])
                    elif o[0] == "op":
                        o[1](h).then_inc(sems[eng.key], 1)
                    elif o[0] == "cc":
                        h.collective_compute("AllGather", ALU.bypass, replica_groups=PAIRS, ins=[o[1]],
                                             outs=[o[2]]).then_inc(sems[o[3]], 1)
                    else:
                        h.dma_start(out=o[1], in_=o[2]).then_inc(sems[o[3]], 16)

            @block.tensor
            def _(h):
                run(self.engs["pe"], h)

            @block.scalar
            def _(h):
                run(self.engs["act"], h)

            @block.vector
            def _(h):
                run(self.engs["dve"], h)

            @block.gpsimd
            def _(h):
                run(self.engs["pool"], h)

            @block.sync
            def _(h):
                run(self.engs["sp"], h)


class Ring:
    def __init__(self, items):
        self.items = items
        self.i = 0

    def get(self):
        it = self.items[self.i % len(self.items)]
        self.i += 1
        return it


def host_consts():
    kk = np.arange(128)[:, None]
    c = np.arange(2432)[None, :]
    d = c - kk - 384
    mA = (((d >= 0) & (d <= 128)).astype(np.float32) + ((d % 4 == 0) & (d >= 0) & (d <= 512)).astype(np.float32)
          + ((d % 16 == 0) & (d >= 0) & (d <= 2048)).astype(np.float32))
    c = np.arange(512)[None, :]
    d = c - 128 - kk
    mB = ((d >= 0) & (d <= 127)).astype(np.float32)
    c = np.arange(896)[None, :]
    d = c - 384 - kk
    mC = (d >= 0).astype(np.float32)
    cmask = np.concatenate([mA, mB, mC], axis=1).astype(np.float32)
    ident = np.eye(128, dtype=np.float32)
    esel = np.zeros((8, 8, 128), np.float32)
    for b in range(8):
        esel[b, b, :] = 1.0
    esel = esel.reshape(8, 1024)
    t = np.arange(16)[:, None]
    blk = np.arange(8)[None, :]
    past = blk < (t // 2)
    own = blk == (t // 2)
    pn = np.stack([np.where(past, 0.0, np.where(own, 1e30, -1e30)),
                   np.where(past | own, -30000.0, -60000.0)]).astype(np.float32).reshape(1, 256)
    p = np.arange(128)
    inv128 = np.exp(np.arange(64, dtype=np.float32) * np.float32(-2.0 * math.log(10000.0) / 128)).astype(np.float32)
    inv64 = np.exp(np.arange(32, dtype=np.float32) * np.float32(-2.0 * math.log(10000.0) / 64)).astype(np.float32)
    invf = np.stack([inv128[p % 64], inv64[p % 32]], axis=1).astype(np.float32)
    return dict(cmask=cmask, ident=ident, esel=esel, pastc=pn, invf=invf)


def build_program(n_layers=4):
    nc = bass.Bass("TRN2", target_bir_lowering=False)
    em = Emitter()
    es = ExitStack()

    def dram(name, shape, dt, kind="ExternalInput"):
        return nc.dram_tensor(name, shape, dt, kind=kind).ap()

    x_in = dram("x", [S, DM], F32)
    mem_in = dram("mem", [256, DM], F32)
    pos_in = dram("pos", [1, S], I32)
    w_in_d = [dram("even_w_in", [2, DM, 3584], F32), dram("odd_w_in", [2, DM, 2304], F32)]
    w_mem_d = [dram("even_w_mem_kv", [2, DM, 512], F32), dram("odd_w_mem_kv", [2, DM, 512], F32)]
    w_out_d = [dram("even_w_out", [2, DM, DM], F32), dram("odd_w_out", [2, DM, DM], F32)]
    ncols_d = dram("ncols", [5, 128, 16], F32)
    fin_d = dram("final_norm", [1, DM], F32)
    sinks_d = dram("sinks2", [2, 2, 6], F32)
    cmask_d = dram("cmask", [128, 3840], F32)
    ident_d = dram("ident", [128, 128], F32)
    esel_d = dram("esel", [8, 1024], F32)
    pastc_d = dram("pastc", [1, 256], F32)
    invf_d = dram("invf", [128, 2], F32)
    out_d = dram("out", [S, DM], F32, kind="ExternalOutput")
    xs_d = dram("xs_scratch", [S, DM], F32, kind="Internal")
    memn_d = dram("memn_scratch", [128, 4096], BF16, kind="Internal")
    ybounce_d = [[nc.dram_tensor("ybounce%d_%d" % (i, j), [256, S], BF16).ap() for j in range(4)]
                 for i in range(n_layers)]
    ygath_d = [[nc.dram_tensor("ygath%d_%d" % (i, j), [512, S], BF16).ap() for j in range(4)]
               for i in range(n_layers)]
    cc_ds = [[em.new_dsem() for j in range(4)] for _ in range(n_layers)]
    gin_ds = [[em.new_dsem() for r in range(2)] for j in range(4)]
    gout_ds = [em.new_dsem() for j in range(4)]

    def sb(name, shape, dt):
        return es.enter_context(nc.sbuf_tensor(name, shape, dt))

    R1 = sb("R1", [128, 32768], BF16)
    R2 = sb("R2", [128, 32768], BF16)
    R3 = sb("R3", [128, 16896], BF16)
    cosT = sb("cosT", [128, S], F32)
    sinS = sb("sinS", [128, S], F32)
    mkT = sb("mkT", [128, 2, 256], BF16)
    mvT = sb("mvT", [128, 2, 2, 128], BF16)
    wslots = [sb("wslot%d" % i, [128, 16, 128], BF16) for i in range(3)]
    cmask = sb("cmaskb", [128, 3840], BF16)
    identf = sb("identf", [128, 128], F32)
    identb = sb("identb", [128, 128], BF16)
    ones = sb("ones", [128, 128], BF16)
    ones_lo = sb("ones_lo", [128, 128], BF16)
    ones_hi = sb("ones_hi", [128, 128], BF16)
    esel = sb("eselb", [8, 1024], BF16)
    pastc = sb("pastcb", [128, 256], F32)
    invf = sb("invfb", [128, 2], F32)
    ncols = sb("ncolsb", [128, 5, 16], F32)
    esink = sb("esink", [128, 6], F32)
    small = sb("small", [128, 640], F32)
    smallb = sb("smallb", [128, 160], BF16)

    banks = [es.enter_context(nc.psum_tensor("bank%d" % i, [128, 512], F32)) for i in range(8)]
    PA = Ring([(banks[i], Buf("pa%d" % i)) for i in (0, 1)])
    PS = Ring([(banks[i], Buf("ps%d" % i)) for i in (2, 3, 4)])
    POD = Ring([(banks[i], Buf("pod%d" % i)) for i in (5, 6, 7)])

    hT = R1[:, :].rearrange("p (c n) -> p c n", c=16)
    wout = hT
    yT = R2[:, :].rearrange("p (c n) -> p c n", c=16)
    R2f = R2[:, :].bitcast(F32)
    NG = 2
    xstage = [R2f[:, g * 4096:(g + 1) * 4096].rearrange("p (j d) -> p j d", j=NG) for g in range(4)]
    R3f = R3[:, :].bitcast(F32)

    def r3b(kib_lo, kib_hi):
        return R3[:, kib_lo * 512:kib_hi * 512]

    def r3f(kib_lo, kib_hi):
        return R3f[:, kib_lo * 256:kib_hi * 256]

    qT = r3b(0, 4)
    kT0 = r3b(4, 8)
    kT1 = r3b(8, 12)
    biasT = r3b(8, 12)
    V0 = r3b(12, 16).rearrange("p (t d) -> p t d", t=16)
    V1 = r3b(16, 20).rearrange("p (t d) -> p t d", t=16)
    gT = r3b(20, 24)
    t1 = r3f(24, 26)
    t2 = r3f(26, 28)
    rD = t2
    ytmp = t1
    Pt = [r3b(28 + i, 29 + i) for i in range(3)] + [r3b(32, 33)]
    vTtmp = r3b(31, 32)
    junk = r3b(28, 32)
    posi = R2f[:, 0:2048].bitcast(I32)
    angf = R2f[:, 2048:4096]
    kff = R2f[:, 4096:6144]
    rr = R2f[:, 6144:8192]
    memnT = r3b(16, 24).rearrange("p (c n) -> p c n", c=16)
    xpieces = [r3f(2 * i, 2 * i + 2) for i in range(4)]
    xrow = [r3f(8, 16), r3f(16, 24)]
    gfin = r3f(24, 32)
    junkO = r3b(0, 4)

    B = {}

    def buf(name):
        if name not in B:
            B[name] = Buf(name)
        return B[name]

    PP = Ring([(Pt[i], buf("P%d" % i)) for i in range(4)])
    XP = Ring([(xpieces[i], buf("xp%d" % i), em.new_dsem(), em.new_dsem()) for i in range(4)])
    XR = Ring([(xrow[i], buf("xr%d" % i), em.new_dsem(), em.new_dsem()) for i in range(2)])
    XST = Ring([(xstage[i], buf("xst%d" % i), em.new_dsem()) for i in range(4)])
    xs_bufs = [buf("xs%d" % t) for t in range(16)]
    misc_ds = em.new_dsem()
    rope_ds = em.new_dsem()
    memn_ds = em.new_dsem()
    out_buf = buf("out")
    out_ds = em.new_dsem()
    wout_ds = [em.new_dsem() for _ in range(4)]

    def MM(out, lhsT, rhs, start, stop, reads, writes):
        em.op("pe", lambda h: h.matmul(out, lhsT=lhsT, rhs=rhs, start=start, stop=stop), reads, writes)

    def TR(out, in_, ident, reads, writes):
        em.op("pe", lambda h: h.transpose(out, in_, ident), reads, writes)

    def ACT(out, in_, func, reads, writes, bias=0.0, scale=1.0, accum_out=None):
        if accum_out is None:
            em.op("act", lambda h: h.activation(out=out, in_=in_, func=func, bias=bias, scale=scale), reads, writes)
        else:
            em.op("act", lambda h: h.activation(out=out, in_=in_, func=func, bias=bias, scale=scale,
                                                accum_out=accum_out), reads, writes)

    def ACTMUL(out, in_, mul, reads, writes):
        em.op("act", lambda h: h.mul(out=out, in_=in_, mul=mul), reads, writes)

    def ACTCOPY(out, in_, reads, writes):
        em.op("act", lambda h: h.copy(out=out, in_=in_), reads, writes)

    def TT(eng, out, in0, in1, op, reads, writes):
        em.op(eng, lambda h: h.tensor_tensor(out=out, in0=in0, in1=in1, op=op), reads, writes)

    def TS(eng, out, in0, s1, s2, op0, op1, reads, writes):
        if op1 is None:
            em.op(eng, lambda h: h.tensor_scalar(out=out, in0=in0, scalar1=s1, scalar2=None, op0=op0), reads, writes)
        else:
            em.op(eng, lambda h: h.tensor_scalar(out=out, in0=in0, scalar1=s1, scalar2=s2, op0=op0, op1=op1),
                  reads, writes)

    def STT(eng, out, in0, scalar, in1, op0, op1, reads, writes):
        em.op(eng, lambda h: h.scalar_tensor_tensor(out=out, in0=in0, scalar=scalar, in1=in1, op0=op0, op1=op1),
              reads, writes)

    def CP(eng, out, in_, reads, writes):
        em.op(eng, lambda h: h.tensor_copy(out=out, in_=in_), reads, writes)

    def MEMSET(eng, ap, val, writes):
        em.op(eng, lambda h: h.memset(ap, val), (), writes)

    def RECIP(out, in_, reads, writes):
        em.op("dve", lambda h: h.reciprocal(out=out, in_=in_), reads, writes)

    def DMA(q, out, in_, ds, reads, writes):
        em.dma([(q, out, in_)], ds, reads, writes)

    cb = buf("consts")
    DMA("pool", cmask[:], cmask_d, misc_ds, [], [cb])
    DMA("sp", identf[:], ident_d, misc_ds, [], [cb])
    DMA("pool", identb[:], ident_d, misc_ds, [], [cb])
    DMA("pool", esel[:], esel_d, misc_ds, [], [cb])
    DMA("sp", pastc[:], pastc_d.to_broadcast([128, 256]), misc_ds, [], [cb])
    DMA("sp", invf[:], invf_d, misc_ds, [], [cb])
    DMA("sp", ncols[:], ncols_d.rearrange("l p c -> p l c"), misc_ds, [], [cb])
    MEMSET("pool", ones[:], 1.0, [cb])
    MEMSET("pool", ones_lo[:], 0.0, [cb])
    MEMSET("pool", ones_hi[:], 0.0, [cb])
    em.fence()
    MEMSET("pool", ones_lo[:, 0:64], 1.0, [cb])
    MEMSET("pool", ones_hi[:, 64:128], 1.0, [cb])
    mA = cmask[:, 0:2432]
    mB = cmask[:, 2432:2944]
    mC = cmask[:, 2944:3840]
    em.fence()

    wring = [(wslots[i], buf("w%d" % i), em.new_dsem()) for i in range(3)]
    wq = []
    wstate = dict(loaded=0, used=0)
    loaded_items = []

    def w_issue():
        i = wstate["loaded"]
        spec = wq[i]
        ap, bf, ds = wring[i % 3]
        if spec["zero"]:
            MEMSET("pool", ap[:, :, :], 0.0, [bf])
        parts = []
        for (lo, hi, src) in spec["segs"]:
            parts.append(("pool", ap[:, :, lo:hi], src.rearrange("(c p) n -> p c n", p=128)))
        em.dma(parts, ds, [], [bf])
        wstate["loaded"] += 1

    def w_get(tag):
        i = wstate["used"]
        assert wq[i]["tag"] == tag, (wq[i]["tag"], tag)
        while wstate["loaded"] < min(len(wq), i + 3):
            w_issue()
        wstate["used"] += 1
        ap, bf, ds = wring[i % 3]
        return ap, bf

    def spec(tag, segs, zero=False):
        return dict(tag=tag, segs=segs, zero=zero)

    def layer_specs(l):
        par = l % 2
        li = l // 2
        win = w_in_d[par][li]
        wm = w_mem_d[par][li]
        sp_ = []
        for h in range(2):
            sp_.append(spec(("mk", l, h), [(0, 128, wm[:, h * 128:(h + 1) * 128])]))
            sp_.append(spec(("mv", l, h), [(0, 128, wm[:, 256 + h * 128:256 + (h + 1) * 128])]))
        if par == 0:
            E = EVEN_OFF
            for c in EVEN_PROC:
                sp_.append(spec(("g", l, c), [(0, 128, win[:, E["gate"] + c * 128:E["gate"] + (c + 1) * 128])]))
                if c < 3:
                    names = ("qa", "ka", "va")
                    hh = c
                elif c < 6:
                    names = ("qb", "kb", "vb")
                    hh = c - 3
                else:
                    names = ("qm",)
                    hh = c - 6
                for nm in names:
                    sp_.append(spec((nm, l, c), [(0, 128, win[:, E[nm] + hh * 128:E[nm] + (hh + 1) * 128])]))
        else:
            O = ODD_OFF
            for h in range(2):
                c = 6 + h
                sp_.append(spec(("g", l, c), [(0, 128, win[:, O["gate"] + c * 128:O["gate"] + (c + 1) * 128])]))
                sp_.append(spec(("qm", l, c), [(0, 128, win[:, O["qm"] + h * 128:O["qm"] + (h + 1) * 128])]))
            for g, chunks in ((0, (0, 1, 2, 3)), (1, (4, 5))):
                k0 = O["kc"] + g * 64
                v0 = O["vc"] + g * 64
                sp_.append(spec(("k0", l, g), [(0, 32, win[:, k0:k0 + 32]), (64, 96, win[:, k0 + 32:k0 + 64])], True))
                sp_.append(spec(("k1", l, g), [(32, 64, win[:, k0:k0 + 32]), (96, 128, win[:, k0 + 32:k0 + 64])], True))
                sp_.append(spec(("v0", l, g), [(0, 64, win[:, v0:v0 + 64])], True))
                sp_.append(spec(("v1", l, g), [(64, 128, win[:, v0:v0 + 64])], True))
                for c in chunks:
                    sp_.append(spec(("g", l, c), [(0, 128, win[:, O["gate"] + c * 128:O["gate"] + (c + 1) * 128])]))
                    q0 = O["qc"] + c * 128
                    sp_.append(spec(("qc", l, c), [(0, 32, win[:, q0:q0 + 32]), (64, 96, win[:, q0 + 32:q0 + 64]),
                                                   (32, 64, win[:, q0 + 64:q0 + 96]),
                                                   (96, 128, win[:, q0 + 96:q0 + 128])]))
        return sp_

    for l in range(n_layers):
        wq.extend(layer_specs(l))

    hTb = buf("R1")
    yTb = [buf("yT%d" % c) for c in range(16)]
    sm = buf("small")

    def norm_phase(src_rows, ntiles, gidx, dst, dst_buf, src_bufs):
        ngroups = (ntiles + NG - 1) // NG
        for g in range(ngroups):
            nt = min(NG, ntiles - g * NG)
            xg, xb, xds = XST.get()
            em.dma([("sp", xg[:, 0:nt, :], src_rows(g * NG, nt).rearrange("(j p) d -> p j d", p=128))], xds,
                   [src_bufs[g * NG + j] for j in range(nt)], [xb])
            for j in range(nt):
                t = g * NG + j
                ACT(junk, xg[:, j, :], AF.Square, [xb], [buf("junk"), sm], accum_out=small[:, t:t + 1])
            ACT(small[:, 16 + g * NG:16 + g * NG + nt], small[:, g * NG:g * NG + nt], AF.Sqrt, [sm], [sm], bias=EPS,
                scale=1.0 / DM)
            RECIP(small[:, 16 + g * NG:16 + g * NG + nt], small[:, 16 + g * NG:16 + g * NG + nt], [sm], [sm])
            for j in range(nt):
                t = g * NG + j
                TS("dve", xg[:, j, :], xg[:, j, :], small[:, 16 + t:17 + t], None, ALU.mult, None, [sm, xb], [xb])
            for c in range(16):
                ps, pb = PA.get()
                for j in range(nt):
                    TR(ps[:, j * 128:(j + 1) * 128], xg[:, j, c * 128:(c + 1) * 128], identf[:], [xb, cb], [pb])
                ACTMUL(dst[:, c, g * NG * 128:g * NG * 128 + nt * 128], ps[:, 0:nt * 128], ncols[:, gidx, c:c + 1],
                       [pb, cb], [dst_buf])

    tb = buf("tables")

    def rope_tables(col):
        pb_ = buf("ropetmp")
        T0, T1, T2, T3 = r3f(8, 10), r3f(10, 12), r3f(12, 14), r3f(14, 16)
        for ch in range(4):
            cs = slice(ch * 512, (ch + 1) * 512)
            DMA("sp", T0.bitcast(I32), pos_in[:, cs].to_broadcast([128, 512]), rope_ds, [], [pb_])
            CP("dve", T1, T0.bitcast(I32), [pb_], [pb_])
            TS("dve", T1, T1, invf[:, col:col + 1], None, ALU.mult, None, [pb_, cb], [pb_])
            TS("dve", T2, T1, 1.0 / TWO_PI, None, ALU.mult, None, [pb_], [pb_])
            CP("dve", T3.bitcast(I32), T2, [pb_], [pb_])
            CP("dve", T2, T3.bitcast(I32), [pb_], [pb_])
            STT("dve", T1, T2, -CW1, T1, ALU.mult, ALU.add, [pb_], [pb_])
            STT("dve", T1, T2, -CW2, T1, ALU.mult, ALU.add, [pb_], [pb_])
            TS("dve", T2, T1, math.pi, -TWO_PI, ALU.is_gt, ALU.mult, [pb_], [pb_])
            TT("dve", T3, T1, T2, ALU.add, [pb_], [pb_])
            TS("dve", T2, T3, -math.pi, TWO_PI, ALU.is_lt, ALU.mult, [pb_], [pb_])
            TT("dve", T3, T3, T2, ALU.add, [pb_], [pb_])
            ACT(sinS[0:64, cs], T3[0:64, :], AF.Sin, [pb_], [tb])
            ACT(sinS[64:128, cs], T3[64:128, :], AF.Sin, [pb_], [tb], scale=-1.0)
            TS("dve", T0, T1, math.pi / 2, None, ALU.add, None, [pb_], [pb_])
            TS("dve", T2, T0, math.pi, -TWO_PI, ALU.is_gt, ALU.mult, [pb_], [pb_])
            TT("dve", T0, T0, T2, ALU.add, [pb_], [pb_])
            ACT(cosT[:, cs], T0, AF.Sin, [pb_], [tb])

    def proj_fm(tag, evac):
        wp, wb = w_get(tag)
        for tt in range(4):
            ps, pb = PA.get()
            for kc in range(16):
                MM(ps[:, :], wp[:, kc, :], hT[:, kc, tt * 512:(tt + 1) * 512], kc == 0, kc == 15, [wb, hTb], [pb])
            evac(ps, pb, tt)

    def rope_evac(dst, dst_buf):
        def f(ps, pb, tt):
            sl = slice(tt * 512, (tt + 1) * 512)
            TT("dve", t1, ps[:, :], cosT[:, sl], ALU.mult, [pb, tb], [buf("t1")])
            TT("dve", t2[0:64, :], ps[64:128, :], sinS[64:128, sl], ALU.mult, [pb, tb], [buf("t2")])
            TT("dve", t2[64:128, :], ps[0:64, :], sinS[0:64, sl], ALU.mult, [pb, tb], [buf("t2")])
            TT("pool", dst[:, sl], t1, t2, ALU.add, [buf("t1"), buf("t2")], [dst_buf])
        return f

    def copy_evac(dst, dst_buf):
        def f(ps, pb, tt):
            ACTCOPY(dst[:, tt * 512:(tt + 1) * 512], ps[:, :], [pb], [dst_buf])
        return f

    def silu_evac(ps, pb, tt):
        ACT(gT[:, tt * 512:(tt + 1) * 512], ps[:, :], AF.Silu, [pb], [buf("gT")])

    def v_evac(Vt, vbuf):
        def f(ps, pb, tt):
            ACTCOPY(vTtmp, ps[:, :], [pb], [buf("vTtmp")])
            p2, p2b = PA.get()
            p2v = p2[:, :].bitcast(BF16)
            for j in range(4):
                TR(p2v[:, j * 128:(j + 1) * 128], vTtmp[:, j * 128:(j + 1) * 128], identb[:], [buf("vTtmp"), cb], [p2b])
            CP("dve", Vt[:, tt * 4:(tt + 1) * 4, :], p2v[:, 0:512].rearrange("p (t d) -> p t d", t=4), [p2b], [vbuf])
        return f

    def attn_head(ranges):
        flat = [(ri, bj) for ri, r in enumerate(ranges) for bj in range(len(r[0]))]
        Ps = {}
        acc = {}

        def emit_qk(g):
            ri, bj = flat[g]
            blocks, Wq, Ws, scale, finish = ranges[ri]
            b = blocks[bj]
            Sx, Sb = PS.get()
            for (lhsT, rhs, lo, hi) in b["qk"]:
                MM(Sx[:, lo:hi], lhsT, rhs, True, b.get("bias") is None, b["reads"], [Sb])
                if b.get("bias") is not None:
                    bl, br = b["bias"]
                    MM(Sx[:, lo:hi], bl, br, False, True, [cb, buf("biasT")], [Sb])
            P, Pb = PP.get()
            ACT(P[:, 0:Ws], Sx[:, 0:Ws], AF.Exp, [Sb], [Pb], scale=scale)
            for (lo, hi, m) in b.get("masks", ()):
                TT("dve", P[:, lo:hi], P[:, lo:hi], m, ALU.mult, [Pb, cb], [Pb])
            Ps[g] = (P, Pb)

        def emit_pv(g):
            ri, bj = flat[g]
            blocks, Wq, Ws, scale, finish = ranges[ri]
            nb = len(blocks)
            b = blocks[bj]
            if bj == 0:
                Dn, Db = POD.get()
                O, Ob = POD.get()
                acc[ri] = (O, Ob, Dn, Db)
            O, Ob, Dn, Db = acc[ri]
            P, Pb = Ps.pop(g)
            n = len(b["pv"])
            for idx, (vl, ol, lo, hi) in enumerate(b["pv"]):
                first = (bj == 0 and idx == 0)
                last = (bj == nb - 1 and idx == n - 1)
                MM(O[:, 0:Wq], vl, P[:, lo:hi], first, last, [Pb] + b["vreads"], [Ob])
                MM(Dn[:, 0:Wq], ol, P[:, lo:hi], first, last, [Pb, cb], [Db])
            if bj == nb - 1:
                finish(O, Ob, Dn, Db)

        ng = len(flat)
        emit_qk(0)
        if ng > 1:
            emit_qk(1)
        for g in range(ng):
            if g + 2 < ng:
                emit_qk(g + 2)
            emit_pv(g)

    def finish_std(c, q0, Wq, sink_col=None):
        def f(O, Ob, Dn, Db):
            rb, yb = buf("t2"), buf("t1")
            if sink_col is None:
                ACT(rD[:, 0:Wq], Dn[:, 0:Wq], AF.Ln, [Db], [rb])
            else:
                ACT(rD[:, 0:Wq], Dn[:, 0:Wq], AF.Ln, [Db, cb], [rb], bias=sink_col)
            ACT(rD[:, 0:Wq], rD[:, 0:Wq], AF.Exp, [rb], [rb], scale=-1.0)
            TT("dve", ytmp[:, 0:Wq], O[:, 0:Wq], rD[:, 0:Wq], ALU.mult, [Ob, rb], [yb])
            TT("pool", yT[:, c, q0:q0 + Wq], ytmp[:, 0:Wq], gT[:, q0:q0 + Wq], ALU.mult, [yb, buf("gT")], [yTb[c]])
        return f

    SC128 = 128 ** -0.5
    SC64 = 64 ** -0.5
    qb_, kb0_, kb1_, vb0_, vb1_ = buf("qT"), buf("kT0"), buf("kT1"), buf("V0"), buf("V1")

    def mem_head(l, c, h, qtag):
        proj_fm(("g", l, c), silu_evac)
        proj_fm(qtag, copy_evac(qT, qb_))
        rngs = []
        for r in range(4):
            q0 = r * 512
            blocks = []
            for kb in range(2):
                blocks.append(dict(qk=[(mkT[:, h, kb * 128:(kb + 1) * 128], qT[:, q0:q0 + 512], 0, 512)],
                                   reads=[buf("mk"), qb_], pv=[(mvT[:, kb, h, :], ones[:], 0, 512)],
                                   vreads=[buf("mk")]))
            rngs.append((blocks, 512, 512, SC128, finish_std(c, q0, 512)))
        attn_head(rngs)

    def phase_M(l):
        DMA("sp", r3b(16, 24), memn_d, memn_ds, [buf("memn_d")], [buf("memnT")])
        mb = buf("mk")
        for h in range(2):
            wp, wb = w_get(("mk", l, h))
            ps, pb = PA.get()
            for kc in range(16):
                MM(ps[:, 0:256], wp[:, kc, :], memnT[:, kc, :], kc == 0, kc == 15, [wb, buf("memnT")], [pb])
            ACTCOPY(mkT[:, h, :], ps[:, 0:256], [pb], [mb])
            wp, wb = w_get(("mv", l, h))
            ps, pb = PA.get()
            for tl in range(2):
                for kc in range(16):
                    MM(ps[:, tl * 128:(tl + 1) * 128], memnT[:, kc, tl * 128:(tl + 1) * 128], wp[:, kc, :], kc == 0,
                       kc == 15, [wb, buf("memnT")], [pb])
            CP("dve", mvT[:, :, h, :], ps[:, 0:256].rearrange("p (t d) -> p t d", t=2), [pb], [mb])

    def even_layer(l):
        done = set()
        for c in EVEN_PROC:
            if c < 3:
                proj_fm(("g", l, c), silu_evac)
                proj_fm(("qa", l, c), rope_evac(qT, qb_))
                proj_fm(("ka", l, c), rope_evac(kT0, kb0_))
                proj_fm(("va", l, c), v_evac(V0, vb0_))
                rngs = []
                for r in range(4):
                    q0 = r * 512
                    blocks = []
                    for kb in range(4 * r + 4):
                        delta = q0 - 128 * kb
                        blocks.append(dict(qk=[(kT0[:, kb * 128:(kb + 1) * 128], qT[:, q0:q0 + 512], 0, 512)],
                                           reads=[kb0_, qb_], masks=[(0, 512, mA[:, delta + 384:delta + 384 + 512])],
                                           pv=[(V0[:, kb, :], ones[:], 0, 512)], vreads=[vb0_]))
                    rngs.append((blocks, 512, 512, SC128, finish_std(c, q0, 512)))
                attn_head(rngs)
            elif c < 6:
                proj_fm(("g", l, c), silu_evac)
                proj_fm(("qb", l, c), rope_evac(qT, qb_))
                proj_fm(("kb", l, c), rope_evac(kT0, kb0_))
                proj_fm(("vb", l, c), v_evac(V0, vb0_))
                if c == EVEN_PROC[-1]:
                    prefetch_wout(l)
                km = small[:, 32:40]
                kmt = small[:, 40:48]
                kmh = smallb[:, 0:8]
                kml = smallb[:, 8:16]
                gm = small[:, 64:192]
                mx = small[:, 192:320]
                bs = small[:, 320:448]
                bsb = smallb[:, 16:144]
                em.op("dve", lambda h: h.reduce_sum(out=km, in_=kT0.rearrange("p (b n) -> p b n", b=8), axis=AX.X),
                      [kb0_], [sm])
                TS("dve", km, km, 1.0 / 256.0, None, ALU.mult, None, [sm], [sm])
                CP("dve", kmh, km, [sm], [sm])
                TT("dve", kmt, km, kmh, ALU.subtract, [sm], [sm])
                CP("dve", kml, kmt, [sm], [sm])
                gp, gpb = PA.get()
                for t in range(16):
                    MM(gp[:, t * 8:(t + 1) * 8], qT[:, t * 128:(t + 1) * 128], kmh, True, False, [qb_, sm], [gpb])
                    MM(gp[:, t * 8:(t + 1) * 8], qT[:, t * 128:(t + 1) * 128], kml, False, True, [qb_, sm], [gpb])
                TT("dve", gm, gp[:, 0:128], pastc[:, 0:128], ALU.add, [gpb, cb], [sm])
                for t in range(16):
                    em.op("dve", (lambda t_: (lambda h: h.max(out=mx[:, t_ * 8:(t_ + 1) * 8],
                                                             in_=gm[:, t_ * 8:(t_ + 1) * 8])))(t), [sm], [sm])
                for t in range(16):
                    TS("dve", bs[:, t * 8:(t + 1) * 8], gm[:, t * 8:(t + 1) * 8], mx[:, t * 8 + 3:t * 8 + 4], None,
                       ALU.is_ge, None, [sm], [sm])
                STT("dve", bsb, bs, 30000.0, pastc[:, 128:256], ALU.mult, ALU.add, [sm, cb], [sm])
                for g4 in range(4):
                    p2, p2b = PA.get()
                    p2v = p2[:, :].bitcast(BF16)
                    for j in range(4):
                        t = g4 * 4 + j
                        TR(p2v[0:8, j * 128:(j + 1) * 128], bsb[:, t * 8:(t + 1) * 8], identb[:], [sm, cb], [p2b])
                    CP("dve", biasT[0:8, g4 * 512:(g4 + 1) * 512], p2v[0:8, 0:512], [p2b], [buf("biasT")])
                rngs = []
                for r in range(4):
                    q0 = r * 512
                    blocks = []
                    for kb in range(4 * r + 4):
                        b_ = dict(qk=[(kT0[:, kb * 128:(kb + 1) * 128], qT[:, q0:q0 + 512], 0, 512)],
                                  reads=[kb0_, qb_],
                                  bias=(esel[0:8, (kb // 2) * 128:(kb // 2 + 1) * 128], biasT[0:8, q0:q0 + 512]),
                                  pv=[(V0[:, kb, :], ones[:], 0, 512)], vreads=[vb0_])
                        if kb >= 4 * r:
                            delta = q0 - 128 * kb
                            b_["masks"] = [(0, 512, mC[:, delta + 384:delta + 384 + 512])]
                        blocks.append(b_)
                    rngs.append((blocks, 512, 512, SC128, finish_std(c, q0, 512)))
                attn_head(rngs)
            else:
                mem_head(l, c, c - 6, ("qm", l, c))
            done.add(c)
            if (c ^ 1) in done:
                exchange_part(l, c // 2)

    def odd_layer(l):
        li = l // 2
        DMA("sp", esink[0:64, :], sinks_d[li, 0:1, :].to_broadcast([64, 6]), misc_ds, [], [buf("esink")])
        DMA("sp", esink[64:128, :], sinks_d[li, 1:2, :].to_broadcast([64, 6]), misc_ds, [], [buf("esink")])
        ACT(esink[:, :], esink[:, :], AF.Exp, [buf("esink")], [cb])
        for h in range(2):
            mem_head(l, 6 + h, h, ("qm", l, 6 + h))
        exchange_part(l, 3)
        for g, chunks in ((0, (0, 1, 2, 3)), (1, (4, 5))):
            proj_fm(("k0", l, g), rope_evac(kT0, kb0_))
            proj_fm(("k1", l, g), rope_evac(kT1, kb1_))
            proj_fm(("v0", l, g), v_evac(V0, vb0_))
            proj_fm(("v1", l, g), v_evac(V1, vb1_))
            for c in chunks:
                proj_fm(("g", l, c), silu_evac)
                proj_fm(("qc", l, c), rope_evac(qT, qb_))
                if c == 5:
                    prefetch_wout(l)
                rngs = []
                for i in range(8):
                    q0 = i * 256
                    blocks = []
                    for kb in (2 * i - 1, 2 * i, 2 * i + 1):
                        if kb < 0:
                            continue
                        delta = q0 - 128 * kb
                        m = mB[:, delta + 128:delta + 128 + 256]
                        ks = slice(kb * 128, (kb + 1) * 128)
                        blocks.append(dict(qk=[(kT0[:, ks], qT[:, q0:q0 + 256], 0, 256),
                                               (kT1[:, ks], qT[:, q0:q0 + 256], 256, 512)],
                                           reads=[kb0_, kb1_, qb_], masks=[(0, 256, m), (256, 512, m)],
                                           pv=[(V0[:, kb, :], ones_lo[:], 0, 256), (V1[:, kb, :], ones_hi[:], 256, 512)],
                                           vreads=[vb0_, vb1_]))
                    rngs.append((blocks, 256, 512, SC64, finish_std(c, q0, 256, sink_col=esink[:, c:c + 1])))
                attn_head(rngs)
                if c % 2 == 1:
                    exchange_part(l, c // 2)

    def exchange_part(l, j):
        bb, gb = buf("ybounce%d_%d" % (l, j)), buf("ygath%d_%d" % (l, j))
        DMA("sp", ybounce_d[l][j].rearrange("(c p) n -> p c n", p=128), yT[:, 2 * j:2 * j + 2, :], gout_ds[j],
            yTb[2 * j:2 * j + 2], [bb])
        em.cc(ybounce_d[l][j], ygath_d[l][j], cc_ds[l][j], [bb], [gb])
        for r in range(2):
            DMA("sp", yT[:, r * 8 + 2 * j:r * 8 + 2 * j + 2, :],
                ygath_d[l][j][r * 256:(r + 1) * 256, :].rearrange("(c p) n -> p c n", p=128), gin_ds[j][r], [gb],
                yTb[r * 8 + 2 * j:r * 8 + 2 * j + 2])

    def prefetch_wout(l):
        par, li = l % 2, l // 2
        wo = w_out_d[par][li]
        for n in range(4):
            em.dma([("pool", wout[:, :, n * 512:(n + 1) * 512],
                     wo[:, n * 512:(n + 1) * 512].rearrange("(c p) n -> p c n", p=128))], wout_ds[n], [],
                   [hTb, buf("wout%d" % n)])

    def phase_O(l):
        par, li = l % 2, l // 2
        src = x_in if l == 0 else xs_d
        for n in range(4):
            wb = buf("wout%d" % n)
            for t in range(16):
                xp, xb, ds_in, ds_out = XP.get()
                rows = slice(t * 128, (t + 1) * 128)
                cols = slice(n * 512, (n + 1) * 512)
                xsb = buf("xs%d_%d" % (t, n))
                DMA("sp", xp, src[rows, cols], ds_in, [xsb], [xb])
                ps, pb = PA.get()
                for kc in range(16):
                    MM(ps[:, :], yT[:, kc, rows], wout[:, kc, cols], kc == 0, kc == 15, [yTb[kc], wb], [pb])
                TT("dve", xp, ps[:, :], xp, ALU.add, [pb, xb], [xb])
                DMA("sp", xs_d[rows, cols], xp, ds_out, [xb], [xsb, xs_bufs[t]])

    def phase_F():
        DMA("sp", gfin, fin_d.to_broadcast([128, DM]), misc_ds, [], [buf("gfin")])
        for t in range(16):
            xr, xb, ds_in, ds_out = XR.get()
            rows = slice(t * 128, (t + 1) * 128)
            DMA("sp", xr, xs_d[rows, :], ds_in, [xs_bufs[t]], [xb])
            ACT(junkO, xr, AF.Square, [xb], [buf("junkO"), sm], accum_out=small[:, t:t + 1])
            ACT(small[:, 16 + t:17 + t], small[:, t:t + 1], AF.Sqrt, [sm], [sm], bias=EPS, scale=1.0 / DM)
            RECIP(small[:, 16 + t:17 + t], small[:, 16 + t:17 + t], [sm], [sm])
            STT("dve", xr, xr, small[:, 16 + t:17 + t], gfin, ALU.mult, ALU.mult, [sm, xb, buf("gfin")], [xb])
            DMA("sp", out_d[rows, :], xr, ds_out, [xb], [out_buf])

    memb = buf("memn_sb")
    norm_phase(lambda t0, n: mem_in[t0 * 128:(t0 + n) * 128, :], 2, 4, memnT, memb, [buf("memsrc")] * 2)
    DMA("sp", memn_d, r3b(16, 24), misc_ds, [memb], [buf("memn_d")])
    em.fence()

    xin_bufs = [buf("xin")] * 16
    rope_tables(0)
    phase_M(0)
    em.fence()
    for l in range(n_layers):
        par = l % 2
        src = x_in if l == 0 else xs_d
        norm_phase(lambda t0, n, s=src: s[t0 * 128:(t0 + n) * 128, :], 16, l, hT, hTb,
                   xin_bufs if l == 0 else xs_bufs)
        em.fence()
        if par == 0:
            even_layer(l)
        else:
            odd_layer(l)
        em.fence()
        if l + 1 < n_layers:
            rope_tables((l + 1) % 2)
            phase_M(l + 1)
        phase_O(l)
        em.fence()
    phase_F()
    em.fence()
    em.replay(nc)
    es.close()
    return nc


_CACHE = {}


def _slice_even(w, r):
    ids = [3 * r + j for j in range(3)]
    cols = []
    for base in (0, 768, 1536, 2304, 3072, 3840):
        for h in ids:
            cols.append(np.arange(base + h * 128, base + (h + 1) * 128))
    for h in (2 * r, 2 * r + 1):
        cols.append(np.arange(4608 + h * 128, 4608 + (h + 1) * 128))
    for gc in EVEN_ORDER[8 * r:8 * r + 8]:
        cols.append(np.arange(5120 + gc * 128, 5120 + (gc + 1) * 128))
    return np.ascontiguousarray(w[:, :, np.concatenate(cols)])


def _slice_odd(w, r):
    chunks = ODD_ORDER[8 * r:8 * r + 8]
    groups = [0, 1] if r == 0 else [2, 1]
    cols = []
    for gc in chunks[:6]:
        cols.append(np.arange(gc * 128, (gc + 1) * 128))
    for g in groups:
        cols.append(np.arange(1536 + g * 64, 1536 + (g + 1) * 64))
    for g in groups:
        cols.append(np.arange(1728 + g * 64, 1728 + (g + 1) * 64))
    for h in (2 * r, 2 * r + 1):
        cols.append(np.arange(1920 + h * 128, 1920 + (h + 1) * 128))
    for gc in chunks:
        cols.append(np.arange(2432 + gc * 128, 2432 + (gc + 1) * 128))
    return np.ascontiguousarray(w[:, :, np.concatenate(cols)])


def _slice_mem(w, r):
    cols = []
    for base in (0, 512):
        for h in (2 * r, 2 * r + 1):
            cols.append(np.arange(base + h * 128, base + (h + 1) * 128))
    return np.ascontiguousarray(w[:, :, np.concatenate(cols)])


def _perm_rows(w, order):
    rows = np.concatenate([np.arange(gc * 128, (gc + 1) * 128) for gc in order])
    return np.ascontiguousarray(w[:, rows, :])


def kernel(x, mem, positions, even_norm, even_w_in, even_w_mem_kv, even_w_out,
           odd_norm, odd_w_in, odd_w_mem_kv, odd_w_out, odd_sinks, mem_norm, final_norm, _n_layers=4):
    f32 = np.float32
    x = np.asarray(x, f32)
    mem = np.asarray(mem, f32)
    positions = np.asarray(positions, np.int32)
    even_w_in = np.asarray(even_w_in, f32)
    odd_w_in = np.asarray(odd_w_in, f32)
    even_w_mem_kv = np.asarray(even_w_mem_kv, f32)
    odd_w_mem_kv = np.asarray(odd_w_mem_kv, f32)
    consts = host_consts()
    norms = [np.asarray(even_norm, f32)[0], np.asarray(odd_norm, f32)[0], np.asarray(even_norm, f32)[1],
             np.asarray(odd_norm, f32)[1], np.asarray(mem_norm, f32)]
    ncols = np.stack([n.reshape(16, 128).T for n in norms]).astype(f32)
    sk = np.asarray(odd_sinks, f32)
    common = dict(even_w_out=_perm_rows(np.asarray(even_w_out, f32), EVEN_ORDER),
                  odd_w_out=_perm_rows(np.asarray(odd_w_out, f32), ODD_ORDER),
                  ncols=ncols, final_norm=np.asarray(final_norm, f32).reshape(1, DM), **consts)
    role = []
    for r in range(2):
        chunks = ODD_ORDER[8 * r:8 * r + 6]
        sinks2 = np.stack([sk[:, [2 * gc for gc in chunks]], sk[:, [2 * gc + 1 for gc in chunks]]], axis=1)
        role.append(dict(even_w_in=_slice_even(even_w_in, r), odd_w_in=_slice_odd(odd_w_in, r),
                         even_w_mem_kv=_slice_mem(even_w_mem_kv, r), odd_w_mem_kv=_slice_mem(odd_w_mem_kv, r),
                         sinks2=np.ascontiguousarray(sinks2.astype(f32))))
    if _n_layers not in _CACHE:
        _CACHE[_n_layers] = build_program(_n_layers)
    nc = _CACHE[_n_layers]
    in_maps = []
    for core in range(8):
        b, r = core // 2, core % 2
        m = dict(common)
        m.update(role[r])
        m["x"] = np.ascontiguousarray(x[b])
        m["mem"] = np.ascontiguousarray(mem[b])
        m["pos"] = np.ascontiguousarray(positions[b].reshape(1, S))
        in_maps.append(m)
    res = run_bass_kernel_spmd(nc, in_maps, core_ids=list(range(8)))
    out = np.stack([res.results[2 * b]["out"] for b in range(4)]).astype(f32)
    return out
```

```python
import math
from contextlib import ExitStack
import numpy as np
import concourse.bass as bass
import concourse.mybir as mybir
from concourse.bass_utils import run_bass_kernel_spmd

F32 = mybir.dt.float32
BF16 = mybir.dt.bfloat16
I32 = mybir.dt.int32
AF = mybir.ActivationFunctionType
ALU = mybir.AluOpType
AX = mybir.AxisListType

S = 2048
DM = 2048
EPS = 1e-6
EVEN_OFF = dict(qa=0, ka=384, va=768, qb=1152, kb=1536, vb=1920, qm=2304, gate=2560)
ODD_OFF = dict(qc=0, kc=768, vc=896, qm=1024, gate=1280)
EVEN_ORDER = [0, 1, 2, 6, 7, 8, 12, 13, 3, 4, 5, 9, 10, 11, 14, 15]
ODD_ORDER = [0, 1, 2, 3, 4, 5, 12, 13, 8, 9, 10, 11, 6, 7, 14, 15]
PAIRS = [[0, 1], [2, 3], [4, 5], [6, 7]]
EVEN_PROC = [6, 7, 0, 1, 2, 3, 4, 5]
TWO_PI = 2.0 * math.pi
CW1 = 6.28125
CW2 = TWO_PI - CW1


class Buf:
    __slots__ = ("name", "w", "r")

    def __init__(self, name=""):
        self.name = name
        self.w = None
        self.r = {}


class DSem:
    def __init__(self, key):
        self.key = key
        self.count = 0


class Eng:
    def __init__(self, name):
        self.name = name
        self.ops = []
        self.count = 0
        self.waited = {}
        self.key = "E_" + name


class Emitter:
    def __init__(self):
        self.engs = {n: Eng(n) for n in ("pe", "act", "dve", "pool", "sp")}
        self.dsems = []

    def new_dsem(self):
        d = DSem("D_%d" % len(self.dsems))
        self.dsems.append(d)
        return d

    def _deps(self, eng, reads, writes, is_dma):
        deps = {}

        def add(semkey, val, ename, kind):
            if (not is_dma) and ename == eng.name and kind != "raw":
                return
            if eng.waited.get(semkey, 0) >= val:
                return
            if deps.get(semkey, 0) < val:
                deps[semkey] = val

        for b in reads:
            if b.w is not None:
                add(b.w[0], b.w[1], b.w[2], "raw")
        for b in writes:
            if b.w is not None:
                add(b.w[0], b.w[1], b.w[2], "waw")
            for k, (v, en) in b.r.items():
                add(k, v, en, "war")
        return deps

    def _wait(self, eng, deps):
        for k, v in deps.items():
            eng.ops.append(("wait", k, v))
            eng.waited[k] = v

    def _commit(self, tok, reads, writes):
        k, v, en = tok
        for b in reads:
            b.r[k] = (v, en)
        for b in writes:
            b.w = tok
            b.r = {}

    def op(self, engname, fn, reads=(), writes=()):
        eng = self.engs[engname]
        self._wait(eng, self._deps(eng, reads, writes, False))
        eng.count += 1
        eng.ops.append(("op", fn))
        self._commit((eng.key, eng.count, eng.name), reads, writes)

    def dma(self, parts, dsem, reads=(), writes=()):
        for (q, o, i) in parts:
            eng = self.engs[q]
            self._wait(eng, self._deps(eng, reads, writes, True))
            eng.ops.append(("dma", o, i, dsem.key))
            dsem.count += 16
        self._commit((dsem.key, dsem.count, "dma"), reads, writes)

    def cc(self, ins, outs, dsem, reads=(), writes=()):
        eng = self.engs["pool"]
        self._wait(eng, self._deps(eng, reads, writes, True))
        eng.ops.append(("cc", ins, outs, dsem.key))
        dsem.count += 1
        self._commit((dsem.key, dsem.count, "dma"), reads, writes)

    def fence(self, exclude=()):
        ex = set(d.key for d in exclude)
        for e in self.engs.values():
            deps = {}
            for e2 in self.engs.values():
                if e2 is not e and e2.count > e.waited.get(e2.key, 0):
                    deps[e2.key] = e2.count
            for d in self.dsems:
                if d.key not in ex and d.count > e.waited.get(d.key, 0):
                    deps[d.key] = d.count
            self._wait(e, deps)

    def replay(self, nc):
        with ExitStack() as es:
            sems = {}
            for e in self.engs.values():
                sems[e.key] = es.enter_context(nc.semaphore("s_" + e.name))
            for d in self.dsems:
                sems[d.key] = es.enter_context(nc.semaphore("s_" + d.key))
            block = es.enter_context(nc.Block())

            def run(eng, h):
                for o in eng.ops:
                    if o[0] == "wait":
                        h.wait_ge(sems[o[1]], o[2])
                    elif o[0] == "op":
                        o[1](h).then_inc(sems[eng.key], 1)
                    elif o[0] == "cc":
                        h.collective_compute("AllGather", ALU.bypass, replica_groups=PAIRS, ins=[o[1]],
                                             outs=[o[2]]).then_inc(sems[o[3]], 1)
                    else:
                        h.dma_start(out=o[1], in_=o[2]).then_inc(sems[o[3]], 16)

            @block.tensor
            def _(h):
                run(self.engs["pe"], h)

            @block.scalar
            def _(h):
                run(self.engs["act"], h)

            @block.vector
            def _(h):
                run(self.engs["dve"], h)

            @block.gpsimd
            def _(h):
                run(self.engs["pool"], h)

            @block.sync
            def _(h):
                run(self.engs["sp"], h)


class Ring:
    def __init__(self, items):
        self.items = items
        self.i = 0

    def get(self):
        it = self.items[self.i % len(self.items)]
        self.i += 1
        return it


def host_consts():
    kk = np.arange(128)[:, None]
    c = np.arange(2432)[None, :]
    d = c - kk - 384
    mA = (((d >= 0) & (d <= 128)).astype(np.float32) + ((d % 4 == 0) & (d >= 0) & (d <= 512)).astype(np.float32)
          + ((d % 16 == 0) & (d >= 0) & (d <= 2048)).astype(np.float32))
    c = np.arange(512)[None, :]
    d = c - 128 - kk
    mB = ((d >= 0) & (d <= 127)).astype(np.float32)
    c = np.arange(896)[None, :]
    d = c - 384 - kk
    mC = (d >= 0).astype(np.float32)
    cmask = np.concatenate([mA, mB, mC], axis=1).astype(np.float32)
    ident = np.eye(128, dtype=np.float32)
    esel = np.zeros((8, 8, 128), np.float32)
    for b in range(8):
        esel[b, b, :] = 1.0
    esel = esel.reshape(8, 1024)
    t = np.arange(16)[:, None]
    blk = np.arange(8)[None, :]
    past = blk < (t // 2)
    own = blk == (t // 2)
    pn = np.stack([np.where(past, 0.0, np.where(own, 1e30, -1e30)),
                   np.where(past | own, -30000.0, -60000.0)]).astype(np.float32).reshape(1, 256)
    p = np.arange(128)
    inv128 = np.exp(np.arange(64, dtype=np.float32) * np.float32(-2.0 * math.log(10000.0) / 128)).astype(np.float32)
    inv64 = np.exp(np.arange(32, dtype=np.float32) * np.float32(-2.0 * math.log(10000.0) / 64)).astype(np.float32)
    invf = np.stack([inv128[p % 64], inv64[p % 32]], axis=1).astype(np.float32)
    return dict(cmask=cmask, ident=ident, esel=esel, pastc=pn, invf=invf)


def build_program(n_layers=4):
    nc = bass.Bass("TRN2", target_bir_lowering=False)
    em = Emitter()
    es = ExitStack()

    def dram(name, shape, dt, kind="ExternalInput"):
        return nc.dram_tensor(name, shape, dt, kind=kind).ap()

    x_in = dram("x", [S, DM], F32)
    mem_in = dram("mem", [256, DM], F32)
    pos_in = dram("pos", [1, S], I32)
    w_in_d = [dram("even_w_in", [2, DM, 3584], F32), dram("odd_w_in", [2, DM, 2304], F32)]
    w_mem_d = [dram("even_w_mem_kv", [2, DM, 512], F32), dram("odd_w_mem_kv", [2, DM, 512], F32)]
    w_out_d = [dram("even_w_out", [2, DM, DM], F32), dram("odd_w_out", [2, DM, DM], F32)]
    ncols_d = dram("ncols", [5, 128, 16], F32)
    fin_d = dram("final_norm", [1, DM], F32)
    sinks_d = dram("sinks2", [2, 2, 6], F32)
    cmask_d = dram("cmask", [128, 3840], F32)
    ident_d = dram("ident", [128, 128], F32)
    esel_d = dram("esel", [8, 1024], F32)
    pastc_d = dram("pastc", [1, 256], F32)
    invf_d = dram("invf", [128, 2], F32)
    out_d = dram("out", [S, DM], F32, kind="ExternalOutput")
    xs_d = dram("xs_scratch", [S, DM], F32, kind="Internal")
    memn_d = dram("memn_scratch", [128, 4096], BF16, kind="Internal")
    ybounce_d = [[nc.dram_tensor("ybounce%d_%d" % (i, j), [256, S], BF16).ap() for j in range(4)]
                 for i in range(n_layers)]
    ygath_d = [[nc.dram_tensor("ygath%d_%d" % (i, j), [512, S], BF16).ap() for j in range(4)]
               for i in range(n_layers)]
    cc_ds = [[em.new_dsem() for j in range(4)] for _ in range(n_layers)]
    gin_ds = [[em.new_dsem() for r in range(2)] for j in range(4)]
    gout_ds = [em.new_dsem() for j in range(4)]

    def sb(name, shape, dt):
        return es.enter_context(nc.sbuf_tensor(name, shape, dt))

    R1 = sb("R1", [128, 32768], BF16)
    R2 = sb("R2", [128, 32768], BF16)
    R3 = sb("R3", [128, 16896], BF16)
    cosT = sb("cosT", [128, S], F32)
    sinS = sb("sinS", [128, S], F32)
    mkT = sb("mkT", [128, 2, 256], BF16)
    mvT = sb("mvT", [128, 2, 2, 128], BF16)
    wslots = [sb("wslot%d" % i, [128, 16, 128], BF16) for i in range(3)]
    cmask = sb("cmaskb", [128, 3840], BF16)
    identf = sb("identf", [128, 128], F32)
    identb = sb("identb", [128, 128], BF16)
    ones = sb("ones", [128, 128], BF16)
    ones_lo = sb("ones_lo", [128, 128], BF16)
    ones_hi = sb("ones_hi", [128, 128], BF16)
    esel = sb("eselb", [8, 1024], BF16)
    pastc = sb("pastcb", [128, 256], F32)
    invf = sb("invfb", [128, 2], F32)
    ncols = sb("ncolsb", [128, 5, 16], F32)
    esink = sb("esink", [128, 6], F32)
    small = sb("small", [128, 640], F32)
    smallb = sb("smallb", [128, 160], BF16)

    banks = [es.enter_context(nc.psum_tensor("bank%d" % i, [128, 512], F32)) for i in range(8)]
    PA = Ring([(banks[i], Buf("pa%d" % i)) for i in (0, 1)])
    PS = Ring([(banks[i], Buf("ps%d" % i)) for i in (2, 3, 4)])
    POD = Ring([(banks[i], Buf("pod%d" % i)) for i in (5, 6, 7)])
    PN = Ring(PA.items + PS.items + POD.items)

    hT = R1[:, :].rearrange("p (c n) -> p c n", c=16)
    wout = hT
    yT = R2[:, :].rearrange("p (c n) -> p c n", c=16)
    R2f = R2[:, :].bitcast(F32)
    NG = 2
    xstage = [R2f[:, g * 4096:(g + 1) * 4096].rearrange("p (j d) -> p j d", j=NG) for g in range(4)]
    R3f = R3[:, :].bitcast(F32)

    def r3b(kib_lo, kib_hi):
        return R3[:, kib_lo * 512:kib_hi * 512]

    def r3f(kib_lo, kib_hi):
        return R3f[:, kib_lo * 256:kib_hi * 256]

    qT = r3b(0, 4)
    kT0 = r3b(4, 8)
    kT1 = r3b(8, 12)
    biasT = r3b(8, 12)
    V0 = r3b(12, 16).rearrange("p (t d) -> p t d", t=16)
    V1 = r3b(16, 20).rearrange("p (t d) -> p t d", t=16)
    gT = r3b(20, 24)
    t1 = r3f(24, 26)
    t2 = r3f(26, 28)
    rD = t2
    ytmp = t1
    Pt = [r3b(28 + i, 29 + i) for i in range(3)] + [r3b(32, 33)]
    vTtmp = r3b(31, 32)
    junk = r3b(28, 32)
    posi = R2f[:, 0:2048].bitcast(I32)
    angf = R2f[:, 2048:4096]
    kff = R2f[:, 4096:6144]
    rr = R2f[:, 6144:8192]
    memnT = r3b(16, 24).rearrange("p (c n) -> p c n", c=16)
    xpieces = [r3f(2 * i, 2 * i + 2) for i in range(4)]
    xrow = [r3f(8, 16), r3f(16, 24)]
    gfin = r3f(24, 32)
    junkO = r3b(0, 4)

    B = {}

    def buf(name):
        if name not in B:
            B[name] = Buf(name)
        return B[name]

    PP = Ring([(Pt[i], buf("P%d" % i)) for i in range(4)])
    XP = Ring([(xpieces[i], buf("xp%d" % i), em.new_dsem(), em.new_dsem()) for i in range(4)])
    XR = Ring([(xrow[i], buf("xr%d" % i), em.new_dsem(), em.new_dsem()) for i in range(2)])
    XST = Ring([(xstage[i], buf("xst%d" % i), em.new_dsem()) for i in range(4)])
    xs_bufs = [buf("xs%d" % t) for t in range(16)]
    misc_ds = em.new_dsem()
    rope_ds = em.new_dsem()
    memn_ds = em.new_dsem()
    out_buf = buf("out")
    out_ds = em.new_dsem()
    wout_ds = [em.new_dsem() for _ in range(4)]

    def MM(out, lhsT, rhs, start, stop, reads, writes):
        em.op("pe", lambda h: h.matmul(out, lhsT=lhsT, rhs=rhs, start=start, stop=stop), reads, writes)

    def TR(out, in_, ident, reads, writes):
        em.op("pe", lambda h: h.transpose(out, in_, ident), reads, writes)

    def ACT(out, in_, func, reads, writes, bias=0.0, scale=1.0, accum_out=None):
        if accum_out is None:
            em.op("act", lambda h: h.activation(out=out, in_=in_, func=func, bias=bias, scale=scale), reads, writes)
        else:
            em.op("act", lambda h: h.activation(out=out, in_=in_, func=func, bias=bias, scale=scale,
                                                accum_out=accum_out), reads, writes)

    def ACTMUL(out, in_, mul, reads, writes):
        em.op("act", lambda h: h.mul(out=out, in_=in_, mul=mul), reads, writes)

    def ACTCOPY(out, in_, reads, writes):
        em.op("act", lambda h: h.copy(out=out, in_=in_), reads, writes)

    def TT(eng, out, in0, in1, op, reads, writes):
        em.op(eng, lambda h: h.tensor_tensor(out=out, in0=in0, in1=in1, op=op), reads, writes)

    def TS(eng, out, in0, s1, s2, op0, op1, reads, writes):
        if op1 is None:
            em.op(eng, lambda h: h.tensor_scalar(out=out, in0=in0, scalar1=s1, scalar2=None, op0=op0), reads, writes)
        else:
            em.op(eng, lambda h: h.tensor_scalar(out=out, in0=in0, scalar1=s1, scalar2=s2, op0=op0, op1=op1),
                  reads, writes)

    def STT(eng, out, in0, scalar, in1, op0, op1, reads, writes):
        em.op(eng, lambda h: h.scalar_tensor_tensor(out=out, in0=in0, scalar=scalar, in1=in1, op0=op0, op1=op1),
              reads, writes)

    def CP(eng, out, in_, reads, writes):
        em.op(eng, lambda h: h.tensor_copy(out=out, in_=in_), reads, writes)

    def MEMSET(eng, ap, val, writes):
        em.op(eng, lambda h: h.memset(ap, val), (), writes)

    def RECIP(out, in_, reads, writes):
        em.op("dve", lambda h: h.reciprocal(out=out, in_=in_), reads, writes)

    def DMA(q, out, in_, ds, reads, writes):
        em.dma([(q, out, in_)], ds, reads, writes)

    cb = buf("consts")
    DMA("pool", cmask[:], cmask_d, misc_ds, [], [cb])
    DMA("sp", identf[:], ident_d, misc_ds, [], [cb])
    DMA("pool", identb[:], ident_d, misc_ds, [], [cb])
    DMA("pool", esel[:], esel_d, misc_ds, [], [cb])
    DMA("sp", pastc[:], pastc_d.to_broadcast([128, 256]), misc_ds, [], [cb])
    DMA("sp", invf[:], invf_d, misc_ds, [], [cb])
    DMA("sp", ncols[:], ncols_d.rearrange("l p c -> p l c"), misc_ds, [], [cb])
    MEMSET("pool", ones[:], 1.0, [cb])
    MEMSET("pool", ones_lo[:], 0.0, [cb])
    MEMSET("pool", ones_hi[:], 0.0, [cb])
    em.fence()
    MEMSET("pool", ones_lo[:, 0:64], 1.0, [cb])
    MEMSET("pool", ones_hi[:, 64:128], 1.0, [cb])
    mA = cmask[:, 0:2432]
    mB = cmask[:, 2432:2944]
    mC = cmask[:, 2944:3840]
    em.fence()

    wring = [(wslots[i], buf("w%d" % i), em.new_dsem()) for i in range(3)]
    wq = []
    wstate = dict(loaded=0, used=0)
    loaded_items = []

    def w_issue():
        i = wstate["loaded"]
        spec = wq[i]
        ap, bf, ds = wring[i % 3]
        if spec["zero"]:
            MEMSET("pool", ap[:, :, :], 0.0, [bf])
        parts = []
        for (lo, hi, src) in spec["segs"]:
            parts.append(("pool", ap[:, :, lo:hi], src.rearrange("(c p) n -> p c n", p=128)))
        em.dma(parts, ds, [], [bf])
        wstate["loaded"] += 1

    def w_get(tag):
        i = wstate["used"]
        assert wq[i]["tag"] == tag, (wq[i]["tag"], tag)
        while wstate["loaded"] < min(len(wq), i + 3):
            w_issue()
        wstate["used"] += 1
        ap, bf, ds = wring[i % 3]
        return ap, bf

    def spec(tag, segs, zero=False):
        return dict(tag=tag, segs=segs, zero=zero)

    def layer_specs(l):
        par = l % 2
        li = l // 2
        win = w_in_d[par][li]
        wm = w_mem_d[par][li]
        sp_ = []
        for h in range(2):
            sp_.append(spec(("mk", l, h), [(0, 128, wm[:, h * 128:(h + 1) * 128])]))
            sp_.append(spec(("mv", l, h), [(0, 128, wm[:, 256 + h * 128:256 + (h + 1) * 128])]))
        if par == 0:
            E = EVEN_OFF
            for c in EVEN_PROC:
                sp_.append(spec(("g", l, c), [(0, 128, win[:, E["gate"] + c * 128:E["gate"] + (c + 1) * 128])]))
                if c < 3:
                    names = ("qa", "ka", "va")
                    hh = c
                elif c < 6:
                    names = ("qb", "kb", "vb")
                    hh = c - 3
                else:
                    names = ("qm",)
                    hh = c - 6
                for nm in names:
                    sp_.append(spec((nm, l, c), [(0, 128, win[:, E[nm] + hh * 128:E[nm] + (hh + 1) * 128])]))
        else:
            O = ODD_OFF
            for h in range(2):
                c = 6 + h
                sp_.append(spec(("g", l, c), [(0, 128, win[:, O["gate"] + c * 128:O["gate"] + (c + 1) * 128])]))
                sp_.append(spec(("qm", l, c), [(0, 128, win[:, O["qm"] + h * 128:O["qm"] + (h + 1) * 128])]))
            for g, chunks in ((0, (0, 1, 2, 3)), (1, (4, 5))):
                k0 = O["kc"] + g * 64
                v0 = O["vc"] + g * 64
                sp_.append(spec(("k0", l, g), [(0, 32, win[:, k0:k0 + 32]), (64, 96, win[:, k0 + 32:k0 + 64])], True))
                sp_.append(spec(("k1", l, g), [(32, 64, win[:, k0:k0 + 32]), (96, 128, win[:, k0 + 32:k0 + 64])], True))
                sp_.append(spec(("v0", l, g), [(0, 64, win[:, v0:v0 + 64])], True))
                sp_.append(spec(("v1", l, g), [(64, 128, win[:, v0:v0 + 64])], True))
                for c in chunks:
                    sp_.append(spec(("g", l, c), [(0, 128, win[:, O["gate"] + c * 128:O["gate"] + (c + 1) * 128])]))
                    q0 = O["qc"] + c * 128
                    sp_.append(spec(("qc", l, c), [(0, 32, win[:, q0:q0 + 32]), (64, 96, win[:, q0 + 32:q0 + 64]),
                                                   (32, 64, win[:, q0 + 64:q0 + 96]),
                                                   (96, 128, win[:, q0 + 96:q0 + 128])]))
        return sp_

    for l in range(n_layers):
        wq.extend(layer_specs(l))

    hTb = buf("R1")
    yTb = [buf("yT%d" % c) for c in range(16)]
    sm = buf("small")

    def norm_phase(src_rows, ntiles, gidx, dst, dst_buf, src_bufs):
        ngroups = (ntiles + NG - 1) // NG
        for g in range(ngroups):
            nt = min(NG, ntiles - g * NG)
            xg, xb, xds = XST.get()
            em.dma([("sp", xg[:, 0:nt, :], src_rows(g * NG, nt).rearrange("(j p) d -> p j d", p=128))], xds,
                   [src_bufs[g * NG + j] for j in range(nt)], [xb])
            for j in range(nt):
                t = g * NG + j
                ACT(junk, xg[:, j, :], AF.Square, [xb], [buf("junk"), sm], accum_out=small[:, t:t + 1])
            ACT(small[:, 16 + g * NG:16 + g * NG + nt], small[:, g * NG:g * NG + nt], AF.Sqrt, [sm], [sm], bias=EPS,
                scale=1.0 / DM)
            RECIP(small[:, 16 + g * NG:16 + g * NG + nt], small[:, 16 + g * NG:16 + g * NG + nt], [sm], [sm])
            for j in range(nt):
                t = g * NG + j
                TS("dve", xg[:, j, :], xg[:, j, :], small[:, 16 + t:17 + t], None, ALU.mult, None, [sm, xb], [xb])
            for c in range(16):
                ps, pb = PN.get()
                for j in range(nt):
                    TR(ps[:, j * 128:(j + 1) * 128], xg[:, j, c * 128:(c + 1) * 128], identf[:], [xb, cb], [pb])
                if c % 2 == 0:
                    ACTMUL(dst[:, c, g * NG * 128:g * NG * 128 + nt * 128], ps[:, 0:nt * 128],
                           ncols[:, gidx, c:c + 1], [pb, cb], [dst_buf])
                else:
                    TS("dve", dst[:, c, g * NG * 128:g * NG * 128 + nt * 128], ps[:, 0:nt * 128],
                       ncols[:, gidx, c:c + 1], None, ALU.mult, None, [pb, cb], [dst_buf])

    tb = buf("tables")

    def rope_tables(col):
        pb_ = buf("ropetmp")
        T0, T1, T2, T3 = r3f(8, 10), r3f(10, 12), r3f(12, 14), r3f(14, 16)
        for ch in range(4):
            cs = slice(ch * 512, (ch + 1) * 512)
            DMA("sp", T0.bitcast(I32), pos_in[:, cs].to_broadcast([128, 512]), rope_ds, [], [pb_])
            CP("dve", T1, T0.bitcast(I32), [pb_], [pb_])
            TS("dve", T1, T1, invf[:, col:col + 1], None, ALU.mult, None, [pb_, cb], [pb_])
            TS("dve", T2, T1, 1.0 / TWO_PI, None, ALU.mult, None, [pb_], [pb_])
            CP("dve", T3.bitcast(I32), T2, [pb_], [pb_])
            CP("dve", T2, T3.bitcast(I32), [pb_], [pb_])
            STT("dve", T1, T2, -CW1, T1, ALU.mult, ALU.add, [pb_], [pb_])
            STT("dve", T1, T2, -CW2, T1, ALU.mult, ALU.add, [pb_], [pb_])
            TS("dve", T2, T1, math.pi, -TWO_PI, ALU.is_gt, ALU.mult, [pb_], [pb_])
            TT("dve", T3, T1, T2, ALU.add, [pb_], [pb_])
            TS("dve", T2, T3, -math.pi, TWO_PI, ALU.is_lt, ALU.mult, [pb_], [pb_])
            TT("dve", T3, T3, T2, ALU.add, [pb_], [pb_])
            ACT(sinS[0:64, cs], T3[0:64, :], AF.Sin, [pb_], [tb])
            ACT(sinS[64:128, cs], T3[64:128, :], AF.Sin, [pb_], [tb], scale=-1.0)
            TS("dve", T0, T1, math.pi / 2, None, ALU.add, None, [pb_], [pb_])
            TS("dve", T2, T0, math.pi, -TWO_PI, ALU.is_gt, ALU.mult, [pb_], [pb_])
            TT("dve", T0, T0, T2, ALU.add, [pb_], [pb_])
            ACT(cosT[:, cs], T0, AF.Sin, [pb_], [tb])

    def proj_fm(tag, evac):
        wp, wb = w_get(tag)
        for tt in range(4):
            ps, pb = PA.get()
            for kc in range(16):
                MM(ps[:, :], wp[:, kc, :], hT[:, kc, tt * 512:(tt + 1) * 512], kc == 0, kc == 15, [wb, hTb], [pb])
            evac(ps, pb, tt)

    def rope_evac(dst, dst_buf):
        def f(ps, pb, tt):
            sl = slice(tt * 512, (tt + 1) * 512)
            TT("dve", t1, ps[:, :], cosT[:, sl], ALU.mult, [pb, tb], [buf("t1")])
            TT("dve", t2[0:64, :], ps[64:128, :], sinS[64:128, sl], ALU.mult, [pb, tb], [buf("t2")])
            TT("dve", t2[64:128, :], ps[0:64, :], sinS[0:64, sl], ALU.mult, [pb, tb], [buf("t2")])
            TT("pool", dst[:, sl], t1, t2, ALU.add, [buf("t1"), buf("t2")], [dst_buf])
        return f

    def copy_evac(dst, dst_buf):
        def f(ps, pb, tt):
            ACTCOPY(dst[:, tt * 512:(tt + 1) * 512], ps[:, :], [pb], [dst_buf])
        return f

    def silu_evac(ps, pb, tt):
        ACT(gT[:, tt * 512:(tt + 1) * 512], ps[:, :], AF.Silu, [pb], [buf("gT")])

    def v_evac(Vt, vbuf):
        def f(ps, pb, tt):
            ACTCOPY(vTtmp, ps[:, :], [pb], [buf("vTtmp")])
            p2, p2b = PA.get()
            p2v = p2[:, :].bitcast(BF16)
            for j in range(4):
                TR(p2v[:, j * 128:(j + 1) * 128], vTtmp[:, j * 128:(j + 1) * 128], identb[:], [buf("vTtmp"), cb], [p2b])
            CP("dve", Vt[:, tt * 4:(tt + 1) * 4, :], p2v[:, 0:512].rearrange("p (t d) -> p t d", t=4), [p2b], [vbuf])
        return f

    def attn_head(ranges):
        flat = [(ri, bj) for ri, r in enumerate(ranges) for bj in range(len(r[0]))]
        Ps = {}
        acc = {}

        def emit_qk(g):
            ri, bj = flat[g]
            blocks, Wq, Ws, scale, finish = ranges[ri]
            b = blocks[bj]
            Sx, Sb = PS.get()
            for (lhsT, rhs, lo, hi) in b["qk"]:
                MM(Sx[:, lo:hi], lhsT, rhs, True, b.get("bias") is None, b["reads"], [Sb])
                if b.get("bias") is not None:
                    bl, br = b["bias"]
                    MM(Sx[:, lo:hi], bl, br, False, True, [cb, buf("biasT")], [Sb])
            P, Pb = PP.get()
            ACT(P[:, 0:Ws], Sx[:, 0:Ws], AF.Exp, [Sb], [Pb], scale=scale)
            for (lo, hi, m) in b.get("masks", ()):
                TT("dve", P[:, lo:hi], P[:, lo:hi], m, ALU.mult, [Pb, cb], [Pb])
            Ps[g] = (P, Pb)

        def emit_pv(g):
            ri, bj = flat[g]
            blocks, Wq, Ws, scale, finish = ranges[ri]
            nb = len(blocks)
            b = blocks[bj]
            if bj == 0:
                Dn, Db = POD.get()
                O, Ob = POD.get()
                acc[ri] = (O, Ob, Dn, Db)
            O, Ob, Dn, Db = acc[ri]
            P, Pb = Ps.pop(g)
            n = len(b["pv"])
            for idx, (vl, ol, lo, hi) in enumerate(b["pv"]):
                first = (bj == 0 and idx == 0)
                last = (bj == nb - 1 and idx == n - 1)
                MM(O[:, 0:Wq], vl, P[:, lo:hi], first, last, [Pb] + b["vreads"], [Ob])
                MM(Dn[:, 0:Wq], ol, P[:, lo:hi], first, last, [Pb, cb], [Db])
            if bj == nb - 1:
                finish(O, Ob, Dn, Db)

        ng = len(flat)
        emit_qk(0)
        if ng > 1:
            emit_qk(1)
        for g in range(ng):
            if g + 2 < ng:
                emit_qk(g + 2)
            emit_pv(g)

    def finish_std(c, q0, Wq, sink_col=None):
        def f(O, Ob, Dn, Db):
            rb, yb = buf("t2"), buf("t1")
            if sink_col is None:
                ACT(rD[:, 0:Wq], Dn[:, 0:Wq], AF.Ln, [Db], [rb])
            else:
                ACT(rD[:, 0:Wq], Dn[:, 0:Wq], AF.Ln, [Db, cb], [rb], bias=sink_col)
            ACT(rD[:, 0:Wq], rD[:, 0:Wq], AF.Exp, [rb], [rb], scale=-1.0)
            TT("dve", ytmp[:, 0:Wq], O[:, 0:Wq], rD[:, 0:Wq], ALU.mult, [Ob, rb], [yb])
            TT("pool", yT[:, c, q0:q0 + Wq], ytmp[:, 0:Wq], gT[:, q0:q0 + Wq], ALU.mult, [yb, buf("gT")], [yTb[c]])
        return f

    SC128 = 128 ** -0.5
    SC64 = 64 ** -0.5
    qb_, kb0_, kb1_, vb0_, vb1_ = buf("qT"), buf("kT0"), buf("kT1"), buf("V0"), buf("V1")

    def mem_head(l, c, h, qtag):
        proj_fm(("g", l, c), silu_evac)
        proj_fm(qtag, copy_evac(qT, qb_))
        rngs = []
        for r in range(4):
            q0 = r * 512
            blocks = []
            for kb in range(2):
                blocks.append(dict(qk=[(mkT[:, h, kb * 128:(kb + 1) * 128], qT[:, q0:q0 + 512], 0, 512)],
                                   reads=[buf("mk"), qb_], pv=[(mvT[:, kb, h, :], ones[:], 0, 512)],
                                   vreads=[buf("mk")]))
            rngs.append((blocks, 512, 512, SC128, finish_std(c, q0, 512)))
        attn_head(rngs)

    def phase_M(l):
        DMA("sp", r3b(16, 24), memn_d, memn_ds, [buf("memn_d")], [buf("memnT")])
        mb = buf("mk")
        for h in range(2):
            wp, wb = w_get(("mk", l, h))
            ps, pb = PA.get()
            for kc in range(16):
                MM(ps[:, 0:256], wp[:, kc, :], memnT[:, kc, :], kc == 0, kc == 15, [wb, buf("memnT")], [pb])
            ACTCOPY(mkT[:, h, :], ps[:, 0:256], [pb], [mb])
            wp, wb = w_get(("mv", l, h))
            ps, pb = PA.get()
            for tl in range(2):
                for kc in range(16):
                    MM(ps[:, tl * 128:(tl + 1) * 128], memnT[:, kc, tl * 128:(tl + 1) * 128], wp[:, kc, :], kc == 0,
                       kc == 15, [wb, buf("memnT")], [pb])
            CP("dve", mvT[:, :, h, :], ps[:, 0:256].rearrange("p (t d) -> p t d", t=2), [pb], [mb])

    def even_layer(l):
        done = set()
        for c in EVEN_PROC:
            if c < 3:
                proj_fm(("g", l, c), silu_evac)
                proj_fm(("qa", l, c), rope_evac(qT, qb_))
                proj_fm(("ka", l, c), rope_evac(kT0, kb0_))
                proj_fm(("va", l, c), v_evac(V0, vb0_))
                rngs = []
                for r in range(4):
                    q0 = r * 512
                    blocks = []
                    for kb in range(4 * r + 4):
                        delta = q0 - 128 * kb
                        blocks.append(dict(qk=[(kT0[:, kb * 128:(kb + 1) * 128], qT[:, q0:q0 + 512], 0, 512)],
                                           reads=[kb0_, qb_], masks=[(0, 512, mA[:, delta + 384:delta + 384 + 512])],
                                           pv=[(V0[:, kb, :], ones[:], 0, 512)], vreads=[vb0_]))
                    rngs.append((blocks, 512, 512, SC128, finish_std(c, q0, 512)))
                attn_head(rngs)
            elif c < 6:
                proj_fm(("g", l, c), silu_evac)
                proj_fm(("qb", l, c), rope_evac(qT, qb_))
                proj_fm(("kb", l, c), rope_evac(kT0, kb0_))
                proj_fm(("vb", l, c), v_evac(V0, vb0_))
                if c == EVEN_PROC[-1]:
                    prefetch_wout(l)
                km = small[:, 32:40]
                kmt = small[:, 40:48]
                kmh = smallb[:, 0:8]
                kml = smallb[:, 8:16]
                gm = small[:, 64:192]
                mx = small[:, 192:320]
                bs = small[:, 320:448]
                bsb = smallb[:, 16:144]
                em.op("dve", lambda h: h.reduce_sum(out=km, in_=kT0.rearrange("p (b n) -> p b n", b=8), axis=AX.X),
                      [kb0_], [sm])
                TS("dve", km, km, 1.0 / 256.0, None, ALU.mult, None, [sm], [sm])
                CP("dve", kmh, km, [sm], [sm])
                TT("dve", kmt, km, kmh, ALU.subtract, [sm], [sm])
                CP("dve", kml, kmt, [sm], [sm])
                gp, gpb = PA.get()
                for t in range(16):
                    MM(gp[:, t * 8:(t + 1) * 8], qT[:, t * 128:(t + 1) * 128], kmh, True, False, [qb_, sm], [gpb])
                    MM(gp[:, t * 8:(t + 1) * 8], qT[:, t * 128:(t + 1) * 128], kml, False, True, [qb_, sm], [gpb])
                TT("dve", gm, gp[:, 0:128], pastc[:, 0:128], ALU.add, [gpb, cb], [sm])
                for t in range(16):
                    em.op("dve", (lambda t_: (lambda h: h.max(out=mx[:, t_ * 8:(t_ + 1) * 8],
                                                             in_=gm[:, t_ * 8:(t_ + 1) * 8])))(t), [sm], [sm])
                for t in range(16):
                    TS("dve", bs[:, t * 8:(t + 1) * 8], gm[:, t * 8:(t + 1) * 8], mx[:, t * 8 + 3:t * 8 + 4], None,
                       ALU.is_ge, None, [sm], [sm])
                STT("dve", bsb, bs, 30000.0, pastc[:, 128:256], ALU.mult, ALU.add, [sm, cb], [sm])
                for g4 in range(4):
                    p2, p2b = PA.get()
                    p2v = p2[:, :].bitcast(BF16)
                    for j in range(4):
                        t = g4 * 4 + j
                        TR(p2v[0:8, j * 128:(j + 1) * 128], bsb[:, t * 8:(t + 1) * 8], identb[:], [sm, cb], [p2b])
                    CP("dve", biasT[0:8, g4 * 512:(g4 + 1) * 512], p2v[0:8, 0:512], [p2b], [buf("biasT")])
                rngs = []
                for r in range(4):
                    q0 = r * 512
                    blocks = []
                    for kb in range(4 * r + 4):
                        b_ = dict(qk=[(kT0[:, kb * 128:(kb + 1) * 128], qT[:, q0:q0 + 512], 0, 512)],
                                  reads=[kb0_, qb_],
                                  bias=(esel[0:8, (kb // 2) * 128:(kb // 2 + 1) * 128], biasT[0:8, q0:q0 + 512]),
                                  pv=[(V0[:, kb, :], ones[:], 0, 512)], vreads=[vb0_])
                        if kb >= 4 * r:
                            delta = q0 - 128 * kb
                            b_["masks"] = [(0, 512, mC[:, delta + 384:delta + 384 + 512])]
                        blocks.append(b_)
                    rngs.append((blocks, 512, 512, SC128, finish_std(c, q0, 512)))
                attn_head(rngs)
            else:
                mem_head(l, c, c - 6, ("qm", l, c))
            done.add(c)
            if (c ^ 1) in done:
                exchange_part(l, c // 2)

    def odd_layer(l):
        li = l // 2
        DMA("sp", esink[0:64, :], sinks_d[li, 0:1, :].to_broadcast([64, 6]), misc_ds, [], [buf("esink")])
        DMA("sp", esink[64:128, :], sinks_d[li, 1:2, :].to_broadcast([64, 6]), misc_ds, [], [buf("esink")])
        ACT(esink[:, :], esink[:, :], AF.Exp, [buf("esink")], [cb])
        for h in range(2):
            mem_head(l, 6 + h, h, ("qm", l, 6 + h))
        exchange_part(l, 3)
        for g, chunks in ((0, (0, 1, 2, 3)), (1, (4, 5))):
            proj_fm(("k0", l, g), rope_evac(kT0, kb0_))
            proj_fm(("k1", l, g), rope_evac(kT1, kb1_))
            proj_fm(("v0", l, g), v_evac(V0, vb0_))
            proj_fm(("v1", l, g), v_evac(V1, vb1_))
            for c in chunks:
                proj_fm(("g", l, c), silu_evac)
                proj_fm(("qc", l, c), rope_evac(qT, qb_))
                if c == 5:
                    prefetch_wout(l)
                rngs = []
                for i in range(8):
                    q0 = i * 256
                    blocks = []
                    for kb in (2 * i - 1, 2 * i, 2 * i + 1):
                        if kb < 0:
                            continue
                        delta = q0 - 128 * kb
                        m = mB[:, delta + 128:delta + 128 + 256]
                        ks = slice(kb * 128, (kb + 1) * 128)
                        blocks.append(dict(qk=[(kT0[:, ks], qT[:, q0:q0 + 256], 0, 256),
                                               (kT1[:, ks], qT[:, q0:q0 + 256], 256, 512)],
                                           reads=[kb0_, kb1_, qb_], masks=[(0, 256, m), (256, 512, m)],
                                           pv=[(V0[:, kb, :], ones_lo[:], 0, 256), (V1[:, kb, :], ones_hi[:], 256, 512)],
                                           vreads=[vb0_, vb1_]))
                    rngs.append((blocks, 256, 512, SC64, finish_std(c, q0, 256, sink_col=esink[:, c:c + 1])))
                attn_head(rngs)
                if c % 2 == 1:
                    exchange_part(l, c // 2)

    def exchange_part(l, j):
        bb, gb = buf("ybounce%d_%d" % (l, j)), buf("ygath%d_%d" % (l, j))
        DMA("sp", ybounce_d[l][j].rearrange("(c p) n -> p c n", p=128), yT[:, 2 * j:2 * j + 2, :], gout_ds[j],
            yTb[2 * j:2 * j + 2], [bb])
        em.cc(ybounce_d[l][j], ygath_d[l][j], cc_ds[l][j], [bb], [gb])
        for r in range(2):
            DMA("sp", yT[:, r * 8 + 2 * j:r * 8 + 2 * j + 2, :],
                ygath_d[l][j][r * 256:(r + 1) * 256, :].rearrange("(c p) n -> p c n", p=128), gin_ds[j][r], [gb],
                yTb[r * 8 + 2 * j:r * 8 + 2 * j + 2])

    def prefetch_wout(l):
        par, li = l % 2, l // 2
        wo = w_out_d[par][li]
        for n in range(4):
            em.dma([("pool", wout[:, :, n * 512:(n + 1) * 512],
                     wo[:, n * 512:(n + 1) * 512].rearrange("(c p) n -> p c n", p=128))], wout_ds[n], [],
                   [hTb, buf("wout%d" % n)])

    def phase_O(l):
        par, li = l % 2, l // 2
        src = x_in if l == 0 else xs_d
        for n in range(4):
            wb = buf("wout%d" % n)
            for t in range(16):
                xp, xb, ds_in, ds_out = XP.get()
                rows = slice(t * 128, (t + 1) * 128)
                cols = slice(n * 512, (n + 1) * 512)
                xsb = buf("xs%d_%d" % (t, n))
                DMA("sp", xp, src[rows, cols], ds_in, [xsb], [xb])
                ps, pb = PA.get()
                for kc in range(16):
                    MM(ps[:, :], yT[:, kc, rows], wout[:, kc, cols], kc == 0, kc == 15, [yTb[kc], wb], [pb])
                TT("dve", xp, ps[:, :], xp, ALU.add, [pb, xb], [xb])
                DMA("sp", xs_d[rows, cols], xp, ds_out, [xb], [xsb, xs_bufs[t]])

    def phase_F():
        DMA("sp", gfin, fin_d.to_broadcast([128, DM]), misc_ds, [], [buf("gfin")])
        for t in range(16):
            xr, xb, ds_in, ds_out = XR.get()
            rows = slice(t * 128, (t + 1) * 128)
            DMA("sp", xr, xs_d[rows, :], ds_in, [xs_bufs[t]], [xb])
            ACT(junkO, xr, AF.Square, [xb], [buf("junkO"), sm], accum_out=small[:, t:t + 1])
            ACT(small[:, 16 + t:17 + t], small[:, t:t + 1], AF.Sqrt, [sm], [sm], bias=EPS, scale=1.0 / DM)
            RECIP(small[:, 16 + t:17 + t], small[:, 16 + t:17 + t], [sm], [sm])
            STT("dve", xr, xr, small[:, 16 + t:17 + t], gfin, ALU.mult, ALU.mult, [sm, xb, buf("gfin")], [xb])
            DMA("sp", out_d[rows, :], xr, ds_out, [xb], [out_buf])

    memb = buf("memn_sb")
    norm_phase(lambda t0, n: mem_in[t0 * 128:(t0 + n) * 128, :], 2, 4, memnT, memb, [buf("memsrc")] * 2)
    DMA("sp", memn_d, r3b(16, 24), misc_ds, [memb], [buf("memn_d")])
    em.fence()

    xin_bufs = [buf("xin")] * 16
    rope_tables(0)
    phase_M(0)
    em.fence()
    for l in range(n_layers):
        par = l % 2
        src = x_in if l == 0 else xs_d
        norm_phase(lambda t0, n, s=src: s[t0 * 128:(t0 + n) * 128, :], 16, l, hT, hTb,
                   xin_bufs if l == 0 else xs_bufs)
        em.fence()
        if par == 0:
            even_layer(l)
        else:
            odd_layer(l)
        em.fence(exclude=wout_ds + [d for j in range(4) for d in gin_ds[j]] + [d for dl in cc_ds for d in dl]
                 + gout_ds)
        if l + 1 < n_layers:
            rope_tables((l + 1) % 2)
            phase_M(l + 1)
        phase_O(l)
        em.fence()
    phase_F()
    em.fence()
    em.replay(nc)
    es.close()
    return nc


_CACHE = {}


def _slice_even(w, r):
    ids = [3 * r + j for j in range(3)]
    cols = []
    for base in (0, 768, 1536, 2304, 3072, 3840):
        for h in ids:
            cols.append(np.arange(base + h * 128, base + (h + 1) * 128))
    for h in (2 * r, 2 * r + 1):
        cols.append(np.arange(4608 + h * 128, 4608 + (h + 1) * 128))
    for gc in EVEN_ORDER[8 * r:8 * r + 8]:
        cols.append(np.arange(5120 + gc * 128, 5120 + (gc + 1) * 128))
    return np.ascontiguousarray(w[:, :, np.concatenate(cols)])


def _slice_odd(w, r):
    chunks = ODD_ORDER[8 * r:8 * r + 8]
    groups = [0, 1] if r == 0 else [2, 1]
    cols = []
    for gc in chunks[:6]:
        cols.append(np.arange(gc * 128, (gc + 1) * 128))
    for g in groups:
        cols.append(np.arange(1536 + g * 64, 1536 + (g + 1) * 64))
    for g in groups:
        cols.append(np.arange(1728 + g * 64, 1728 + (g + 1) * 64))
    for h in (2 * r, 2 * r + 1):
        cols.append(np.arange(1920 + h * 128, 1920 + (h + 1) * 128))
    for gc in chunks:
        cols.append(np.arange(2432 + gc * 128, 2432 + (gc + 1) * 128))
    return np.ascontiguousarray(w[:, :, np.concatenate(cols)])


def _slice_mem(w, r):
    cols = []
    for base in (0, 512):
        for h in (2 * r, 2 * r + 1):
            cols.append(np.arange(base + h * 128, base + (h + 1) * 128))
    return np.ascontiguousarray(w[:, :, np.concatenate(cols)])


def _perm_rows(w, order):
    rows = np.concatenate([np.arange(gc * 128, (gc + 1) * 128) for gc in order])
    return np.ascontiguousarray(w[:, rows, :])


def kernel(x, mem, positions, even_norm, even_w_in, even_w_mem_kv, even_w_out,
           odd_norm, odd_w_in, odd_w_mem_kv, odd_w_out, odd_sinks, mem_norm, final_norm, _n_layers=4):
    f32 = np.float32
    x = np.asarray(x, f32)
    mem = np.asarray(mem, f32)
    positions = np.asarray(positions, np.int32)
    even_w_in = np.asarray(even_w_in, f32)
    odd_w_in = np.asarray(odd_w_in, f32)
    even_w_mem_kv = np.asarray(even_w_mem_kv, f32)
    odd_w_mem_kv = np.asarray(odd_w_mem_kv, f32)
    consts = host_consts()
    norms = [np.asarray(even_norm, f32)[0], np.asarray(odd_norm, f32)[0], np.asarray(even_norm, f32)[1],
             np.asarray(odd_norm, f32)[1], np.asarray(mem_norm, f32)]
    ncols = np.stack([n.reshape(16, 128).T for n in norms]).astype(f32)
    sk = np.asarray(odd_sinks, f32)
    common = dict(even_w_out=_perm_rows(np.asarray(even_w_out, f32), EVEN_ORDER),
                  odd_w_out=_perm_rows(np.asarray(odd_w_out, f32), ODD_ORDER),
                  ncols=ncols, final_norm=np.asarray(final_norm, f32).reshape(1, DM), **consts)
    role = []
    for r in range(2):
        chunks = ODD_ORDER[8 * r:8 * r + 6]
        sinks2 = np.stack([sk[:, [2 * gc for gc in chunks]], sk[:, [2 * gc + 1 for gc in chunks]]], axis=1)
        role.append(dict(even_w_in=_slice_even(even_w_in, r), odd_w_in=_slice_odd(odd_w_in, r),
                         even_w_mem_kv=_slice_mem(even_w_mem_kv, r), odd_w_mem_kv=_slice_mem(odd_w_mem_kv, r),
                         sinks2=np.ascontiguousarray(sinks2.astype(f32))))
    if _n_layers not in _CACHE:
        _CACHE[_n_layers] = build_program(_n_layers)
    nc = _CACHE[_n_layers]
    in_maps = []
    for core in range(8):
        b, r = core // 2, core % 2
        m = dict(common)
        m.update(role[r])
        m["x"] = np.ascontiguousarray(x[b])
        m["mem"] = np.ascontiguousarray(mem[b])
        m["pos"] = np.ascontiguousarray(positions[b].reshape(1, S))
        in_maps.append(m)
    res = run_bass_kernel_spmd(nc, in_maps, core_ids=list(range(8)))
    out = np.stack([res.results[2 * b]["out"] for b in range(4)]).astype(f32)
    return out
```

```python
import math
from contextlib import ExitStack
import numpy as np
import concourse.bass as bass
import concourse.mybir as mybir
from concourse.bass_utils import run_bass_kernel_spmd

F32 = mybir.dt.float32
BF16 = mybir.dt.bfloat16
I32 = mybir.dt.int32
AF = mybir.ActivationFunctionType
ALU = mybir.AluOpType
AX = mybir.AxisListType

S = 2048
DM = 2048
EPS = 1e-6
EVEN_OFF = dict(qa=0, ka=384, va=768, qb=1152, kb=1536, vb=1920, qm=2304, gate=2560)
ODD_OFF = dict(qc=0, kc=768, vc=896, qm=1024, gate=1280)
EVEN_ORDER = [0, 1, 2, 6, 7, 8, 12, 13, 3, 4, 5, 9, 10, 11, 14, 15]
ODD_ORDER = [0, 1, 2, 3, 4, 5, 12, 13, 8, 9, 10, 11, 6, 7, 14, 15]
PAIRS = [[0, 1], [2, 3], [4, 5], [6, 7]]
EVEN_PROC = [6, 7, 0, 1, 2, 3, 4, 5]
TWO_PI = 2.0 * math.pi
CW1 = 6.28125
CW2 = TWO_PI - CW1


class Buf:
    __slots__ = ("name", "w", "r")

    def __init__(self, name=""):
        self.name = name
        self.w = None
        self.r = {}


class DSem:
    def __init__(self, key):
        self.key = key
        self.count = 0


class Eng:
    def __init__(self, name):
        self.name = name
        self.ops = []
        self.count = 0
        self.waited = {}
        self.key = "E_" + name


class Emitter:
    def __init__(self):
        self.engs = {n: Eng(n) for n in ("pe", "act", "dve", "pool", "sp")}
        self.dsems = []

    def new_dsem(self):
        d = DSem("D_%d" % len(self.dsems))
        self.dsems.append(d)
        return d

    def _deps(self, eng, reads, writes, is_dma):
        deps = {}

        def add(semkey, val, ename, kind):
            if (not is_dma) and ename == eng.name and kind != "raw":
                return
            if eng.waited.get(semkey, 0) >= val:
                return
            if deps.get(semkey, 0) < val:
                deps[semkey] = val

        for b in reads:
            if b.w is not None:
                add(b.w[0], b.w[1], b.w[2], "raw")
        for b in writes:
            if b.w is not None:
                add(b.w[0], b.w[1], b.w[2], "waw")
            for k, (v, en) in b.r.items():
                add(k, v, en, "war")
        return deps

    def _wait(self, eng, deps):
        for k, v in deps.items():
            eng.ops.append(("wait", k, v))
            eng.waited[k] = v

    def _commit(self, tok, reads, writes):
        k, v, en = tok
        for b in reads:
            b.r[k] = (v, en)
        for b in writes:
            b.w = tok
            b.r = {}

    def op(self, engname, fn, reads=(), writes=()):
        eng = self.engs[engname]
        self._wait(eng, self._deps(eng, reads, writes, False))
        eng.count += 1
        eng.ops.append(("op", fn))
        self._commit((eng.key, eng.count, eng.name), reads, writes)

    def dma(self, parts, dsem, reads=(), writes=()):
        for (q, o, i) in parts:
            eng = self.engs[q]
            self._wait(eng, self._deps(eng, reads, writes, True))
            eng.ops.append(("dma", o, i, dsem.key))
            dsem.count += 16
        self._commit((dsem.key, dsem.count, "dma"), reads, writes)

    def cc(self, ins, outs, dsem, reads=(), writes=()):
        eng = self.engs["pool"]
        self._wait(eng, self._deps(eng, reads, writes, True))
        eng.ops.append(("cc", ins, outs, dsem.key))
        dsem.count += 1
        self._commit((dsem.key, dsem.count, "dma"), reads, writes)

    def fence(self, exclude=()):
        ex = set(d.key for d in exclude)
        for e in self.engs.values():
            deps = {}
            for e2 in self.engs.values():
                if e2 is not e and e2.count > e.waited.get(e2.key, 0):
                    deps[e2.key] = e2.count
            for d in self.dsems:
                if d.key not in ex and d.count > e.waited.get(d.key, 0):
                    deps[d.key] = d.count
            self._wait(e, deps)

    def replay(self, nc):
        with ExitStack() as es:
            sems = {}
            for e in self.engs.values():
                sems[e.key] = es.enter_context(nc.semaphore("s_" + e.name))
            for d in self.dsems:
                sems[d.key] = es.enter_context(nc.semaphore("s_" + d.key))
            block = es.enter_context(nc.Block())

            def run(eng, h):
                for o in eng.ops:
                    if o[0] == "wait":
                        h.wait_ge(sems[o[1]], o[2])
                    elif o[0] == "op":
                        o[1](h).then_inc(sems[eng.key], 1)
                    elif o[0] == "cc":
                        h.collective_compute("AllGather", ALU.bypass, replica_groups=PAIRS, ins=[o[1]],
                                             outs=[o[2]]).then_inc(sems[o[3]], 1)
                    else:
                        h.dma_start(out=o[1], in_=o[2]).then_inc(sems[o[3]], 16)

            @block.tensor
            def _(h):
                run(self.engs["pe"], h)

            @block.scalar
            def _(h):
                run(self.engs["act"], h)

            @block.vector
            def _(h):
                run(self.engs["dve"], h)

            @block.gpsimd
            def _(h):
                run(self.engs["pool"], h)

            @block.sync
            def _(h):
                run(self.engs["sp"], h)


class Ring:
    def __init__(self, items):
        self.items = items
        self.i = 0

    def get(self):
        it = self.items[self.i % len(self.items)]
        self.i += 1
        return it


def host_consts():
    kk = np.arange(128)[:, None]
    c = np.arange(2432)[None, :]
    d = c - kk - 384
    mA = (((d >= 0) & (d <= 128)).astype(np.float32) + ((d % 4 == 0) & (d >= 0) & (d <= 512)).astype(np.float32)
          + ((d % 16 == 0) & (d >= 0) & (d <= 2048)).astype(np.float32))
    c = np.arange(512)[None, :]
    d = c - 128 - kk
    mB = ((d >= 0) & (d <= 127)).astype(np.float32)
    c = np.arange(896)[None, :]
    d = c - 384 - kk
    mC = (d >= 0).astype(np.float32)
    cmask = np.concatenate([mA, mB, mC], axis=1).astype(np.float32)
    ident = np.eye(128, dtype=np.float32)
    esel = np.zeros((8, 8, 128), np.float32)
    for b in range(8):
        esel[b, b, :] = 1.0
    esel = esel.reshape(8, 1024)
    t = np.arange(16)[:, None]
    blk = np.arange(8)[None, :]
    past = blk < (t // 2)
    own = blk == (t // 2)
    pn = np.stack([np.where(past, 0.0, np.where(own, 1e30, -1e30)),
                   np.where(past | own, -30000.0, -60000.0)]).astype(np.float32).reshape(1, 256)
    p = np.arange(128)
    inv128 = np.exp(np.arange(64, dtype=np.float32) * np.float32(-2.0 * math.log(10000.0) / 128)).astype(np.float32)
    inv64 = np.exp(np.arange(32, dtype=np.float32) * np.float32(-2.0 * math.log(10000.0) / 64)).astype(np.float32)
    invf = np.stack([inv128[p % 64], inv64[p % 32]], axis=1).astype(np.float32)
    return dict(cmask=cmask, ident=ident, esel=esel, pastc=pn, invf=invf)


def build_program(n_layers=4):
    nc = bass.Bass("TRN2", target_bir_lowering=False)
    em = Emitter()
    es = ExitStack()

    def dram(name, shape, dt, kind="ExternalInput"):
        return nc.dram_tensor(name, shape, dt, kind=kind).ap()

    x_in = dram("x", [S, DM], F32)
    mem_in = dram("mem", [256, DM], F32)
    pos_in = dram("pos", [1, S], I32)
    w_in_d = [dram("even_w_in", [2, DM, 3584], F32), dram("odd_w_in", [2, DM, 2304], F32)]
    w_mem_d = [dram("even_w_mem_kv", [2, DM, 512], F32), dram("odd_w_mem_kv", [2, DM, 512], F32)]
    w_out_d = [dram("even_w_out", [2, DM, DM], F32), dram("odd_w_out", [2, DM, DM], F32)]
    ncols_d = dram("ncols", [5, 128, 16], F32)
    fin_d = dram("final_norm", [1, DM], F32)
    sinks_d = dram("sinks2", [2, 2, 6], F32)
    cmask_d = dram("cmask", [128, 3840], F32)
    ident_d = dram("ident", [128, 128], F32)
    esel_d = dram("esel", [8, 1024], F32)
    pastc_d = dram("pastc", [1, 256], F32)
    invf_d = dram("invf", [128, 2], F32)
    out_d = dram("out", [S, DM], F32, kind="ExternalOutput")
    xs_d = dram("xs_scratch", [S, DM], F32, kind="Internal")
    memn_d = dram("memn_scratch", [128, 4096], BF16, kind="Internal")
    ybounce_d = [[nc.dram_tensor("ybounce%d_%d" % (i, j), [256, S], BF16).ap() for j in range(4)]
                 for i in range(n_layers)]
    ygath_d = [[nc.dram_tensor("ygath%d_%d" % (i, j), [512, S], BF16).ap() for j in range(4)]
               for i in range(n_layers)]
    cc_ds = [[em.new_dsem() for j in range(4)] for _ in range(n_layers)]
    gin_ds = [[em.new_dsem() for r in range(2)] for j in range(4)]
    gout_ds = [em.new_dsem() for j in range(4)]

    def sb(name, shape, dt):
        return es.enter_context(nc.sbuf_tensor(name, shape, dt))

    R1 = sb("R1", [128, 32768], BF16)
    R2 = sb("R2", [128, 32768], BF16)
    R3 = sb("R3", [128, 16896], BF16)
    cosT = sb("cosT", [128, S], F32)
    sinS = sb("sinS", [128, S], F32)
    mkT = sb("mkT", [128, 2, 256], BF16)
    mvT = sb("mvT", [128, 2, 2, 128], BF16)
    wslots = [sb("wslot%d" % i, [128, 16, 128], BF16) for i in range(3)]
    cmask = sb("cmaskb", [128, 3840], BF16)
    identf = sb("identf", [128, 128], F32)
    identb = sb("identb", [128, 128], BF16)
    ones = sb("ones", [128, 128], BF16)
    ones_lo = sb("ones_lo", [128, 128], BF16)
    ones_hi = sb("ones_hi", [128, 128], BF16)
    esel = sb("eselb", [8, 1024], BF16)
    pastc = sb("pastcb", [128, 256], F32)
    invf = sb("invfb", [128, 2], F32)
    ncols = sb("ncolsb", [128, 5, 16], F32)
    esink = sb("esink", [128, 6], F32)
    small = sb("small", [128, 640], F32)
    smallb = sb("smallb", [128, 160], BF16)

    banks = [es.enter_context(nc.psum_tensor("bank%d" % i, [128, 512], F32)) for i in range(8)]
    PA = Ring([(banks[i], Buf("pa%d" % i)) for i in (0, 1)])
    PS = Ring([(banks[i], Buf("ps%d" % i)) for i in (2, 3, 4)])
    POD = Ring([(banks[i], Buf("pod%d" % i)) for i in (5, 6, 7)])
    PN = Ring(PA.items + PS.items + POD.items)

    hT = R1[:, :].rearrange("p (c n) -> p c n", c=16)
    wout = hT
    yT = R2[:, :].rearrange("p (c n) -> p c n", c=16)
    R2f = R2[:, :].bitcast(F32)
    NG = 2
    xstage = [R2f[:, g * 4096:(g + 1) * 4096].rearrange("p (j d) -> p j d", j=NG) for g in range(4)]
    R3f = R3[:, :].bitcast(F32)

    def r3b(kib_lo, kib_hi):
        return R3[:, kib_lo * 512:kib_hi * 512]

    def r3f(kib_lo, kib_hi):
        return R3f[:, kib_lo * 256:kib_hi * 256]

    qT = r3b(0, 4)
    kT0 = r3b(4, 8)
    kT1 = r3b(8, 12)
    biasT = r3b(8, 12)
    V0 = r3b(12, 16).rearrange("p (t d) -> p t d", t=16)
    V1 = r3b(16, 20).rearrange("p (t d) -> p t d", t=16)
    gT = r3b(20, 24)
    t1 = r3f(24, 26)
    t2 = r3f(26, 28)
    rD = t2
    ytmp = t1
    Pt = [r3b(28 + i, 29 + i) for i in range(3)] + [r3b(32, 33)]
    vTtmp = r3b(31, 32)
    junk = r3b(28, 32)
    posi = R2f[:, 0:2048].bitcast(I32)
    angf = R2f[:, 2048:4096]
    kff = R2f[:, 4096:6144]
    rr = R2f[:, 6144:8192]
    memnT = r3b(16, 24).rearrange("p (c n) -> p c n", c=16)
    xpieces = [r3f(2 * i, 2 * i + 2) for i in range(4)]
    xrow = [r3f(8, 16), r3f(16, 24)]
    gfin = r3f(24, 32)
    junkO = r3b(0, 4)

    B = {}

    def buf(name):
        if name not in B:
            B[name] = Buf(name)
        return B[name]

    PP = Ring([(Pt[i], buf("P%d" % i)) for i in range(4)])
    XP = Ring([(xpieces[i], buf("xp%d" % i), em.new_dsem(), em.new_dsem()) for i in range(4)])
    XR = Ring([(xrow[i], buf("xr%d" % i), em.new_dsem(), em.new_dsem()) for i in range(2)])
    XST = Ring([(xstage[i], buf("xst%d" % i), em.new_dsem()) for i in range(4)])
    xs_bufs = [buf("xs%d" % t) for t in range(16)]
    misc_ds = em.new_dsem()
    rope_ds = em.new_dsem()
    memn_ds = em.new_dsem()
    out_buf = buf("out")
    out_ds = em.new_dsem()
    wout_ds = [em.new_dsem() for _ in range(4)]

    def MM(out, lhsT, rhs, start, stop, reads, writes):
        em.op("pe", lambda h: h.matmul(out, lhsT=lhsT, rhs=rhs, start=start, stop=stop), reads, writes)

    def TR(out, in_, ident, reads, writes):
        em.op("pe", lambda h: h.transpose(out, in_, ident), reads, writes)

    def ACT(out, in_, func, reads, writes, bias=0.0, scale=1.0, accum_out=None):
        if accum_out is None:
            em.op("act", lambda h: h.activation(out=out, in_=in_, func=func, bias=bias, scale=scale), reads, writes)
        else:
            em.op("act", lambda h: h.activation(out=out, in_=in_, func=func, bias=bias, scale=scale,
                                                accum_out=accum_out), reads, writes)

    def ACTMUL(out, in_, mul, reads, writes):
        em.op("act", lambda h: h.mul(out=out, in_=in_, mul=mul), reads, writes)

    def ACTCOPY(out, in_, reads, writes):
        em.op("act", lambda h: h.copy(out=out, in_=in_), reads, writes)

    def TT(eng, out, in0, in1, op, reads, writes):
        em.op(eng, lambda h: h.tensor_tensor(out=out, in0=in0, in1=in1, op=op), reads, writes)

    def TS(eng, out, in0, s1, s2, op0, op1, reads, writes):
        if op1 is None:
            em.op(eng, lambda h: h.tensor_scalar(out=out, in0=in0, scalar1=s1, scalar2=None, op0=op0), reads, writes)
        else:
            em.op(eng, lambda h: h.tensor_scalar(out=out, in0=in0, scalar1=s1, scalar2=s2, op0=op0, op1=op1),
                  reads, writes)

    def STT(eng, out, in0, scalar, in1, op0, op1, reads, writes):
        em.op(eng, lambda h: h.scalar_tensor_tensor(out=out, in0=in0, scalar=scalar, in1=in1, op0=op0, op1=op1),
              reads, writes)

    def CP(eng, out, in_, reads, writes):
        em.op(eng, lambda h: h.tensor_copy(out=out, in_=in_), reads, writes)

    def MEMSET(eng, ap, val, writes):
        em.op(eng, lambda h: h.memset(ap, val), (), writes)

    def RECIP(out, in_, reads, writes):
        em.op("dve", lambda h: h.reciprocal(out=out, in_=in_), reads, writes)

    def DMA(q, out, in_, ds, reads, writes):
        em.dma([(q, out, in_)], ds, reads, writes)

    cb = buf("consts")
    DMA("pool", cmask[:], cmask_d, misc_ds, [], [cb])
    DMA("sp", identf[:], ident_d, misc_ds, [], [cb])
    DMA("pool", identb[:], ident_d, misc_ds, [], [cb])
    DMA("pool", esel[:], esel_d, misc_ds, [], [cb])
    DMA("sp", pastc[:], pastc_d.to_broadcast([128, 256]), misc_ds, [], [cb])
    DMA("sp", invf[:], invf_d, misc_ds, [], [cb])
    DMA("sp", ncols[:], ncols_d.rearrange("l p c -> p l c"), misc_ds, [], [cb])
    MEMSET("pool", ones[:], 1.0, [cb])
    MEMSET("pool", ones_lo[:], 0.0, [cb])
    MEMSET("pool", ones_hi[:], 0.0, [cb])
    em.fence()
    MEMSET("pool", ones_lo[:, 0:64], 1.0, [cb])
    MEMSET("pool", ones_hi[:, 64:128], 1.0, [cb])
    mA = cmask[:, 0:2432]
    mB = cmask[:, 2432:2944]
    mC = cmask[:, 2944:3840]
    em.fence()

    wring = [(wslots[i], buf("w%d" % i), em.new_dsem()) for i in range(3)]
    wq = []
    wstate = dict(loaded=0, used=0)
    loaded_items = []

    def w_issue():
        i = wstate["loaded"]
        spec = wq[i]
        ap, bf, ds = wring[i % 3]
        if spec["zero"]:
            MEMSET("pool", ap[:, :, :], 0.0, [bf])
        parts = []
        for (lo, hi, src) in spec["segs"]:
            parts.append(("pool", ap[:, :, lo:hi], src.rearrange("(c p) n -> p c n", p=128)))
        em.dma(parts, ds, [], [bf])
        wstate["loaded"] += 1

    def w_get(tag):
        i = wstate["used"]
        assert wq[i]["tag"] == tag, (wq[i]["tag"], tag)
        while wstate["loaded"] < min(len(wq), i + 3):
            w_issue()
        wstate["used"] += 1
        ap, bf, ds = wring[i % 3]
        return ap, bf

    def spec(tag, segs, zero=False):
        return dict(tag=tag, segs=segs, zero=zero)

    def layer_specs(l):
        par = l % 2
        li = l // 2
        win = w_in_d[par][li]
        wm = w_mem_d[par][li]
        sp_ = []
        for h in range(2):
            sp_.append(spec(("mk", l, h), [(0, 128, wm[:, h * 128:(h + 1) * 128])]))
            sp_.append(spec(("mv", l, h), [(0, 128, wm[:, 256 + h * 128:256 + (h + 1) * 128])]))
        if par == 0:
            E = EVEN_OFF
            for c in EVEN_PROC:
                sp_.append(spec(("g", l, c), [(0, 128, win[:, E["gate"] + c * 128:E["gate"] + (c + 1) * 128])]))
                if c < 3:
                    names = ("qa", "ka", "va")
                    hh = c
                elif c < 6:
                    names = ("qb", "kb", "vb")
                    hh = c - 3
                else:
                    names = ("qm",)
                    hh = c - 6
                for nm in names:
                    sp_.append(spec((nm, l, c), [(0, 128, win[:, E[nm] + hh * 128:E[nm] + (hh + 1) * 128])]))
        else:
            O = ODD_OFF
            for h in range(2):
                c = 6 + h
                sp_.append(spec(("g", l, c), [(0, 128, win[:, O["gate"] + c * 128:O["gate"] + (c + 1) * 128])]))
                sp_.append(spec(("qm", l, c), [(0, 128, win[:, O["qm"] + h * 128:O["qm"] + (h + 1) * 128])]))
            for g, chunks in ((0, (0, 1, 2, 3)), (1, (4, 5))):
                k0 = O["kc"] + g * 64
                v0 = O["vc"] + g * 64
                sp_.append(spec(("k0", l, g), [(0, 32, win[:, k0:k0 + 32]), (64, 96, win[:, k0 + 32:k0 + 64])], True))
                sp_.append(spec(("k1", l, g), [(32, 64, win[:, k0:k0 + 32]), (96, 128, win[:, k0 + 32:k0 + 64])], True))
                sp_.append(spec(("v0", l, g), [(0, 64, win[:, v0:v0 + 64])], True))
                sp_.append(spec(("v1", l, g), [(64, 128, win[:, v0:v0 + 64])], True))
                for c in chunks:
                    sp_.append(spec(("g", l, c), [(0, 128, win[:, O["gate"] + c * 128:O["gate"] + (c + 1) * 128])]))
                    q0 = O["qc"] + c * 128
                    sp_.append(spec(("qc", l, c), [(0, 32, win[:, q0:q0 + 32]), (64, 96, win[:, q0 + 32:q0 + 64]),
                                                   (32, 64, win[:, q0 + 64:q0 + 96]),
                                                   (96, 128, win[:, q0 + 96:q0 + 128])]))
        return sp_

    for l in range(n_layers):
        wq.extend(layer_specs(l))

    hTb = buf("R1")
    yTb = [buf("yT%d" % c) for c in range(16)]
    sm = buf("small")

    def norm_phase(src_rows, ntiles, gidx, dst, dst_buf, src_bufs):
        ngroups = (ntiles + NG - 1) // NG
        for g in range(ngroups):
            nt = min(NG, ntiles - g * NG)
            xg, xb, xds = XST.get()
            em.dma([("sp", xg[:, 0:nt, :], src_rows(g * NG, nt).rearrange("(j p) d -> p j d", p=128))], xds,
                   [src_bufs[g * NG + j] for j in range(nt)], [xb])
            sg = buf("nstat%d" % g)
            for j in range(nt):
                t = g * NG + j
                ACT(junk, xg[:, j, :], AF.Square, [xb], [buf("junk"), sg], accum_out=small[:, t:t + 1])
            ACT(small[:, 16 + g * NG:16 + g * NG + nt], small[:, g * NG:g * NG + nt], AF.Sqrt, [sg], [sg], bias=EPS,
                scale=1.0 / DM)
            RECIP(small[:, 16 + g * NG:16 + g * NG + nt], small[:, 16 + g * NG:16 + g * NG + nt], [sg], [sg])
            for j in range(nt):
                t = g * NG + j
                TS("dve", xg[:, j, :], xg[:, j, :], small[:, 16 + t:17 + t], None, ALU.mult, None, [sg, xb], [xb])
            for c in range(16):
                ps, pb = PN.get()
                for j in range(nt):
                    TR(ps[:, j * 128:(j + 1) * 128], xg[:, j, c * 128:(c + 1) * 128], identf[:], [xb, cb], [pb])
                if c % 2 == 0:
                    ACTMUL(dst[:, c, g * NG * 128:g * NG * 128 + nt * 128], ps[:, 0:nt * 128],
                           ncols[:, gidx, c:c + 1], [pb, cb], [dst_buf])
                else:
                    TS("dve", dst[:, c, g * NG * 128:g * NG * 128 + nt * 128], ps[:, 0:nt * 128],
                       ncols[:, gidx, c:c + 1], None, ALU.mult, None, [pb, cb], [dst_buf])

    tb = buf("tables")

    def rope_tables(col):
        pb_ = buf("ropetmp")
        T0, T1, T2, T3 = r3f(8, 10), r3f(10, 12), r3f(12, 14), r3f(14, 16)
        for ch in range(4):
            cs = slice(ch * 512, (ch + 1) * 512)
            DMA("sp", T0.bitcast(I32), pos_in[:, cs].to_broadcast([128, 512]), rope_ds, [], [pb_])
            CP("dve", T1, T0.bitcast(I32), [pb_], [pb_])
            TS("dve", T1, T1, invf[:, col:col + 1], None, ALU.mult, None, [pb_, cb], [pb_])
            TS("dve", T2, T1, 1.0 / TWO_PI, None, ALU.mult, None, [pb_], [pb_])
            CP("dve", T3.bitcast(I32), T2, [pb_], [pb_])
            CP("dve", T2, T3.bitcast(I32), [pb_], [pb_])
            STT("dve", T1, T2, -CW1, T1, ALU.mult, ALU.add, [pb_], [pb_])
            STT("dve", T1, T2, -CW2, T1, ALU.mult, ALU.add, [pb_], [pb_])
            TS("dve", T2, T1, math.pi, -TWO_PI, ALU.is_gt, ALU.mult, [pb_], [pb_])
            TT("dve", T3, T1, T2, ALU.add, [pb_], [pb_])
            TS("dve", T2, T3, -math.pi, TWO_PI, ALU.is_lt, ALU.mult, [pb_], [pb_])
            TT("dve", T3, T3, T2, ALU.add, [pb_], [pb_])
            ACT(sinS[0:64, cs], T3[0:64, :], AF.Sin, [pb_], [tb])
            ACT(sinS[64:128, cs], T3[64:128, :], AF.Sin, [pb_], [tb], scale=-1.0)
            TS("dve", T0, T1, math.pi / 2, None, ALU.add, None, [pb_], [pb_])
            TS("dve", T2, T0, math.pi, -TWO_PI, ALU.is_gt, ALU.mult, [pb_], [pb_])
            TT("dve", T0, T0, T2, ALU.add, [pb_], [pb_])
            ACT(cosT[:, cs], T0, AF.Sin, [pb_], [tb])

    def proj_fm(tag, evac):
        wp, wb = w_get(tag)
        for tt in range(4):
            ps, pb = PA.get()
            for kc in range(16):
                MM(ps[:, :], wp[:, kc, :], hT[:, kc, tt * 512:(tt + 1) * 512], kc == 0, kc == 15, [wb, hTb], [pb])
            evac(ps, pb, tt)

    def rope_evac(dst, dst_buf):
        def f(ps, pb, tt):
            sl = slice(tt * 512, (tt + 1) * 512)
            TT("dve", t1, ps[:, :], cosT[:, sl], ALU.mult, [pb, tb], [buf("t1")])
            TT("dve", t2[0:64, :], ps[64:128, :], sinS[64:128, sl], ALU.mult, [pb, tb], [buf("t2")])
            TT("dve", t2[64:128, :], ps[0:64, :], sinS[0:64, sl], ALU.mult, [pb, tb], [buf("t2")])
            TT("pool", dst[:, sl], t1, t2, ALU.add, [buf("t1"), buf("t2")], [dst_buf])
        return f

    def copy_evac(dst, dst_buf):
        def f(ps, pb, tt):
            ACTCOPY(dst[:, tt * 512:(tt + 1) * 512], ps[:, :], [pb], [dst_buf])
        return f

    def silu_evac(ps, pb, tt):
        ACT(gT[:, tt * 512:(tt + 1) * 512], ps[:, :], AF.Silu, [pb], [buf("gT")])

    def v_evac(Vt, vbuf):
        def f(ps, pb, tt):
            ACTCOPY(vTtmp, ps[:, :], [pb], [buf("vTtmp")])
            p2, p2b = PA.get()
            p2v = p2[:, :].bitcast(BF16)
            for j in range(4):
                TR(p2v[:, j * 128:(j + 1) * 128], vTtmp[:, j * 128:(j + 1) * 128], identb[:], [buf("vTtmp"), cb], [p2b])
            CP("dve", Vt[:, tt * 4:(tt + 1) * 4, :], p2v[:, 0:512].rearrange("p (t d) -> p t d", t=4), [p2b], [vbuf])
        return f

    def attn_head(ranges):
        flat = [(ri, bj) for ri, r in enumerate(ranges) for bj in range(len(r[0]))]
        Ps = {}
        acc = {}

        def emit_qk(g):
            ri, bj = flat[g]
            blocks, Wq, Ws, scale, finish = ranges[ri]
            b = blocks[bj]
            Sx, Sb = PS.get()
            for (lhsT, rhs, lo, hi) in b["qk"]:
                MM(Sx[:, lo:hi], lhsT, rhs, True, b.get("bias") is None, b["reads"], [Sb])
                if b.get("bias") is not None:
                    bl, br = b["bias"]
                    MM(Sx[:, lo:hi], bl, br, False, True, [cb, buf("biasT")], [Sb])
            P, Pb = PP.get()
            ACT(P[:, 0:Ws], Sx[:, 0:Ws], AF.Exp, [Sb], [Pb], scale=scale)
            for (lo, hi, m) in b.get("masks", ()):
                TT("dve", P[:, lo:hi], P[:, lo:hi], m, ALU.mult, [Pb, cb], [Pb])
            Ps[g] = (P, Pb)

        def emit_pv(g):
            ri, bj = flat[g]
            blocks, Wq, Ws, scale, finish = ranges[ri]
            nb = len(blocks)
            b = blocks[bj]
            if bj == 0:
                Dn, Db = POD.get()
                O, Ob = POD.get()
                acc[ri] = (O, Ob, Dn, Db)
            O, Ob, Dn, Db = acc[ri]
            P, Pb = Ps.pop(g)
            n = len(b["pv"])
            for idx, (vl, ol, lo, hi) in enumerate(b["pv"]):
                first = (bj == 0 and idx == 0)
                last = (bj == nb - 1 and idx == n - 1)
                MM(O[:, 0:Wq], vl, P[:, lo:hi], first, last, [Pb] + b["vreads"], [Ob])
                MM(Dn[:, 0:Wq], ol, P[:, lo:hi], first, last, [Pb, cb], [Db])
            if bj == nb - 1:
                finish(O, Ob, Dn, Db)

        ng = len(flat)
        emit_qk(0)
        if ng > 1:
            emit_qk(1)
        for g in range(ng):
            if g + 2 < ng:
                emit_qk(g + 2)
            emit_pv(g)

    def finish_std(c, q0, Wq, sink_col=None):
        def f(O, Ob, Dn, Db):
            rb, yb = buf("t2"), buf("t1")
            if sink_col is None:
                ACT(rD[:, 0:Wq], Dn[:, 0:Wq], AF.Ln, [Db], [rb])
            else:
                ACT(rD[:, 0:Wq], Dn[:, 0:Wq], AF.Ln, [Db, cb], [rb], bias=sink_col)
            ACT(rD[:, 0:Wq], rD[:, 0:Wq], AF.Exp, [rb], [rb], scale=-1.0)
            TT("dve", ytmp[:, 0:Wq], O[:, 0:Wq], rD[:, 0:Wq], ALU.mult, [Ob, rb], [yb])
            TT("pool", yT[:, c, q0:q0 + Wq], ytmp[:, 0:Wq], gT[:, q0:q0 + Wq], ALU.mult, [yb, buf("gT")], [yTb[c]])
        return f

    SC128 = 128 ** -0.5
    SC64 = 64 ** -0.5
    qb_, kb0_, kb1_, vb0_, vb1_ = buf("qT"), buf("kT0"), buf("kT1"), buf("V0"), buf("V1")

    def mem_head(l, c, h, qtag):
        proj_fm(("g", l, c), silu_evac)
        proj_fm(qtag, copy_evac(qT, qb_))
        rngs = []
        for r in range(4):
            q0 = r * 512
            blocks = []
            for kb in range(2):
                blocks.append(dict(qk=[(mkT[:, h, kb * 128:(kb + 1) * 128], qT[:, q0:q0 + 512], 0, 512)],
                                   reads=[buf("mk"), qb_], pv=[(mvT[:, kb, h, :], ones[:], 0, 512)],
                                   vreads=[buf("mk")]))
            rngs.append((blocks, 512, 512, SC128, finish_std(c, q0, 512)))
        attn_head(rngs)

    def phase_M(l):
        DMA("sp", r3b(16, 24), memn_d, memn_ds, [buf("memn_d")], [buf("memnT")])
        mb = buf("mk")
        for h in range(2):
            wp, wb = w_get(("mk", l, h))
            ps, pb = PA.get()
            for kc in range(16):
                MM(ps[:, 0:256], wp[:, kc, :], memnT[:, kc, :], kc == 0, kc == 15, [wb, buf("memnT")], [pb])
            ACTCOPY(mkT[:, h, :], ps[:, 0:256], [pb], [mb])
            wp, wb = w_get(("mv", l, h))
            ps, pb = PA.get()
            for tl in range(2):
                for kc in range(16):
                    MM(ps[:, tl * 128:(tl + 1) * 128], memnT[:, kc, tl * 128:(tl + 1) * 128], wp[:, kc, :], kc == 0,
                       kc == 15, [wb, buf("memnT")], [pb])
            CP("dve", mvT[:, :, h, :], ps[:, 0:256].rearrange("p (t d) -> p t d", t=2), [pb], [mb])

    def even_layer(l):
        done = set()
        for c in EVEN_PROC:
            if c < 3:
                proj_fm(("g", l, c), silu_evac)
                proj_fm(("qa", l, c), rope_evac(qT, qb_))
                proj_fm(("ka", l, c), rope_evac(kT0, kb0_))
                proj_fm(("va", l, c), v_evac(V0, vb0_))
                rngs = []
                for r in range(4):
                    q0 = r * 512
                    blocks = []
                    for kb in range(4 * r + 4):
                        delta = q0 - 128 * kb
                        blocks.append(dict(qk=[(kT0[:, kb * 128:(kb + 1) * 128], qT[:, q0:q0 + 512], 0, 512)],
                                           reads=[kb0_, qb_], masks=[(0, 512, mA[:, delta + 384:delta + 384 + 512])],
                                           pv=[(V0[:, kb, :], ones[:], 0, 512)], vreads=[vb0_]))
                    rngs.append((blocks, 512, 512, SC128, finish_std(c, q0, 512)))
                attn_head(rngs)
            elif c < 6:
                proj_fm(("g", l, c), silu_evac)
                proj_fm(("qb", l, c), rope_evac(qT, qb_))
                proj_fm(("kb", l, c), rope_evac(kT0, kb0_))
                proj_fm(("vb", l, c), v_evac(V0, vb0_))
                if c == EVEN_PROC[-1]:
                    prefetch_wout(l)
                km = small[:, 32:40]
                kmt = small[:, 40:48]
                kmh = smallb[:, 0:8]
                kml = smallb[:, 8:16]
                gm = small[:, 64:192]
                mx = small[:, 192:320]
                bs = small[:, 320:448]
                bsb = smallb[:, 16:144]
                em.op("dve", lambda h: h.reduce_sum(out=km, in_=kT0.rearrange("p (b n) -> p b n", b=8), axis=AX.X),
                      [kb0_], [sm])
                TS("dve", km, km, 1.0 / 256.0, None, ALU.mult, None, [sm], [sm])
                CP("dve", kmh, km, [sm], [sm])
                TT("dve", kmt, km, kmh, ALU.subtract, [sm], [sm])
                CP("dve", kml, kmt, [sm], [sm])
                gp, gpb = PA.get()
                for t in range(16):
                    MM(gp[:, t * 8:(t + 1) * 8], qT[:, t * 128:(t + 1) * 128], kmh, True, False, [qb_, sm], [gpb])
                    MM(gp[:, t * 8:(t + 1) * 8], qT[:, t * 128:(t + 1) * 128], kml, False, True, [qb_, sm], [gpb])
                TT("dve", gm, gp[:, 0:128], pastc[:, 0:128], ALU.add, [gpb, cb], [sm])
                for t in range(16):
                    em.op("dve", (lambda t_: (lambda h: h.max(out=mx[:, t_ * 8:(t_ + 1) * 8],
                                                             in_=gm[:, t_ * 8:(t_ + 1) * 8])))(t), [sm], [sm])
                for t in range(16):
                    TS("dve", bs[:, t * 8:(t + 1) * 8], gm[:, t * 8:(t + 1) * 8], mx[:, t * 8 + 3:t * 8 + 4], None,
                       ALU.is_ge, None, [sm], [sm])
                STT("dve", bsb, bs, 30000.0, pastc[:, 128:256], ALU.mult, ALU.add, [sm, cb], [sm])
                for g4 in range(4):
                    p2, p2b = PA.get()
                    p2v = p2[:, :].bitcast(BF16)
                    for j in range(4):
                        t = g4 * 4 + j
                        TR(p2v[0:8, j * 128:(j + 1) * 128], bsb[:, t * 8:(t + 1) * 8], identb[:], [sm, cb], [p2b])
                    CP("dve", biasT[0:8, g4 * 512:(g4 + 1) * 512], p2v[0:8, 0:512], [p2b], [buf("biasT")])
                rngs = []
                for r in range(4):
                    q0 = r * 512
                    blocks = []
                    for kb in range(4 * r + 4):
                        b_ = dict(qk=[(kT0[:, kb * 128:(kb + 1) * 128], qT[:, q0:q0 + 512], 0, 512)],
                                  reads=[kb0_, qb_],
                                  bias=(esel[0:8, (kb // 2) * 128:(kb // 2 + 1) * 128], biasT[0:8, q0:q0 + 512]),
                                  pv=[(V0[:, kb, :], ones[:], 0, 512)], vreads=[vb0_])
                        if kb >= 4 * r:
                            delta = q0 - 128 * kb
                            b_["masks"] = [(0, 512, mC[:, delta + 384:delta + 384 + 512])]
                        blocks.append(b_)
                    rngs.append((blocks, 512, 512, SC128, finish_std(c, q0, 512)))
                attn_head(rngs)
            else:
                mem_head(l, c, c - 6, ("qm", l, c))
            done.add(c)
            if (c ^ 1) in done:
                exchange_part(l, c // 2)

    def odd_layer(l):
        li = l // 2
        DMA("sp", esink[0:64, :], sinks_d[li, 0:1, :].to_broadcast([64, 6]), misc_ds, [], [buf("esink")])
        DMA("sp", esink[64:128, :], sinks_d[li, 1:2, :].to_broadcast([64, 6]), misc_ds, [], [buf("esink")])
        ACT(esink[:, :], esink[:, :], AF.Exp, [buf("esink")], [cb])
        for h in range(2):
            mem_head(l, 6 + h, h, ("qm", l, 6 + h))
        exchange_part(l, 3)
        for g, chunks in ((0, (0, 1, 2, 3)), (1, (4, 5))):
            proj_fm(("k0", l, g), rope_evac(kT0, kb0_))
            proj_fm(("k1", l, g), rope_evac(kT1, kb1_))
            proj_fm(("v0", l, g), v_evac(V0, vb0_))
            proj_fm(("v1", l, g), v_evac(V1, vb1_))
            for c in chunks:
                proj_fm(("g", l, c), silu_evac)
                proj_fm(("qc", l, c), rope_evac(qT, qb_))
                if c == 5:
                    prefetch_wout(l)
                rngs = []
                for i in range(8):
                    q0 = i * 256
                    blocks = []
                    for kb in (2 * i - 1, 2 * i, 2 * i + 1):
                        if kb < 0:
                            continue
                        delta = q0 - 128 * kb
                        m = mB[:, delta + 128:delta + 128 + 256]
                        ks = slice(kb * 128, (kb + 1) * 128)
                        blocks.append(dict(qk=[(kT0[:, ks], qT[:, q0:q0 + 256], 0, 256),
                                               (kT1[:, ks], qT[:, q0:q0 + 256], 256, 512)],
                                           reads=[kb0_, kb1_, qb_], masks=[(0, 256, m), (256, 512, m)],
                                           pv=[(V0[:, kb, :], ones_lo[:], 0, 256), (V1[:, kb, :], ones_hi[:], 256, 512)],
                                           vreads=[vb0_, vb1_]))
                    rngs.append((blocks, 256, 512, SC64, finish_std(c, q0, 256, sink_col=esink[:, c:c + 1])))
                attn_head(rngs)
                if c % 2 == 1:
                    exchange_part(l, c // 2)

    def exchange_part(l, j):
        bb, gb = buf("ybounce%d_%d" % (l, j)), buf("ygath%d_%d" % (l, j))
        DMA("sp", ybounce_d[l][j].rearrange("(c p) n -> p c n", p=128), yT[:, 2 * j:2 * j + 2, :], gout_ds[j],
            yTb[2 * j:2 * j + 2], [bb])
        em.cc(ybounce_d[l][j], ygath_d[l][j], cc_ds[l][j], [bb], [gb])
        for r in range(2):
            DMA("sp", yT[:, r * 8 + 2 * j:r * 8 + 2 * j + 2, :],
                ygath_d[l][j][r * 256:(r + 1) * 256, :].rearrange("(c p) n -> p c n", p=128), gin_ds[j][r], [gb],
                yTb[r * 8 + 2 * j:r * 8 + 2 * j + 2])

    def prefetch_wout(l):
        par, li = l % 2, l // 2
        wo = w_out_d[par][li]
        for n in range(4):
            em.dma([("pool", wout[:, :, n * 512:(n + 1) * 512],
                     wo[:, n * 512:(n + 1) * 512].rearrange("(c p) n -> p c n", p=128))], wout_ds[n], [],
                   [hTb, buf("wout%d" % n)] if n == 0 else [buf("wout%d" % n)])

    def phase_O(l):
        par, li = l % 2, l // 2
        src = x_in if l == 0 else xs_d
        for n in range(4):
            wb = buf("wout%d" % n)
            for t in range(16):
                xp, xb, ds_in, ds_out = XP.get()
                rows = slice(t * 128, (t + 1) * 128)
                cols = slice(n * 512, (n + 1) * 512)
                xsb = buf("xs%d_%d" % (t, n))
                DMA("sp", xp, src[rows, cols], ds_in, [xsb], [xb])
                ps, pb = PA.get()
                for kc in range(16):
                    MM(ps[:, :], yT[:, kc, rows], wout[:, kc, cols], kc == 0, kc == 15, [yTb[kc], wb], [pb])
                TT("dve", xp, ps[:, :], xp, ALU.add, [pb, xb], [xb])
                DMA("sp", xs_d[rows, cols], xp, ds_out, [xb], [xsb, xs_bufs[t]])

    def phase_F():
        DMA("sp", gfin, fin_d.to_broadcast([128, DM]), misc_ds, [], [buf("gfin")])
        for t in range(16):
            xr, xb, ds_in, ds_out = XR.get()
            rows = slice(t * 128, (t + 1) * 128)
            DMA("sp", xr, xs_d[rows, :], ds_in, [xs_bufs[t]], [xb])
            sg = buf("fstat%d" % t)
            ACT(junkO, xr, AF.Square, [xb], [buf("junkO"), sg], accum_out=small[:, t:t + 1])
            ACT(small[:, 16 + t:17 + t], small[:, t:t + 1], AF.Sqrt, [sg], [sg], bias=EPS, scale=1.0 / DM)
            RECIP(small[:, 16 + t:17 + t], small[:, 16 + t:17 + t], [sg], [sg])
            STT("dve", xr, xr, small[:, 16 + t:17 + t], gfin, ALU.mult, ALU.mult, [sg, xb, buf("gfin")], [xb])
            DMA("sp", out_d[rows, :], xr, ds_out, [xb], [out_buf])

    memb = buf("memn_sb")
    norm_phase(lambda t0, n: mem_in[t0 * 128:(t0 + n) * 128, :], 2, 4, memnT, memb, [buf("memsrc")] * 2)
    DMA("sp", memn_d, r3b(16, 24), misc_ds, [memb], [buf("memn_d")])
    em.fence()

    xin_bufs = [buf("xin")] * 16
    rope_tables(0)
    phase_M(0)
    em.fence()
    for l in range(n_layers):
        par = l % 2
        src = x_in if l == 0 else xs_d
        norm_phase(lambda t0, n, s=src: s[t0 * 128:(t0 + n) * 128, :], 16, l, hT, hTb,
                   xin_bufs if l == 0 else xs_bufs)
        em.fence()
        if par == 0:
            even_layer(l)
        else:
            odd_layer(l)
        em.fence(exclude=wout_ds + [d for j in range(4) for d in gin_ds[j]] + [d for dl in cc_ds for d in dl]
                 + gout_ds)
        if l + 1 < n_layers:
            rope_tables((l + 1) % 2)
            phase_M(l + 1)
        phase_O(l)
        em.fence()
    phase_F()
    em.fence()
    em.replay(nc)
    es.close()
    return nc


_CACHE = {}


def _slice_even(w, r):
    ids = [3 * r + j for j in range(3)]
    cols = []
    for base in (0, 768, 1536, 2304, 3072, 3840):
        for h in ids:
            cols.append(np.arange(base + h * 128, base + (h + 1) * 128))
    for h in (2 * r, 2 * r + 1):
        cols.append(np.arange(4608 + h * 128, 4608 + (h + 1) * 128))
    for gc in EVEN_ORDER[8 * r:8 * r + 8]:
        cols.append(np.arange(5120 + gc * 128, 5120 + (gc + 1) * 128))
    return np.ascontiguousarray(w[:, :, np.concatenate(cols)])


def _slice_odd(w, r):
    chunks = ODD_ORDER[8 * r:8 * r + 8]
    groups = [0, 1] if r == 0 else [2, 1]
    cols = []
    for gc in chunks[:6]:
        cols.append(np.arange(gc * 128, (gc + 1) * 128))
    for g in groups:
        cols.append(np.arange(1536 + g * 64, 1536 + (g + 1) * 64))
    for g in groups:
        cols.append(np.arange(1728 + g * 64, 1728 + (g + 1) * 64))
    for h in (2 * r, 2 * r + 1):
        cols.append(np.arange(1920 + h * 128, 1920 + (h + 1) * 128))
    for gc in chunks:
        cols.append(np.arange(2432 + gc * 128, 2432 + (gc + 1) * 128))
    return np.ascontiguousarray(w[:, :, np.concatenate(cols)])


def _slice_mem(w, r):
    cols = []
    for base in (0, 512):
        for h in (2 * r, 2 * r + 1):
            cols.append(np.arange(base + h * 128, base + (h + 1) * 128))
    return np.ascontiguousarray(w[:, :, np.concatenate(cols)])


def _perm_rows(w, order):
    rows = np.concatenate([np.arange(gc * 128, (gc + 1) * 128) for gc in order])
    return np.ascontiguousarray(w[:, rows, :])


def kernel(x, mem, positions, even_norm, even_w_in, even_w_mem_kv, even_w_out,
           odd_norm, odd_w_in, odd_w_mem_kv, odd_w_out, odd_sinks, mem_norm, final_norm, _n_layers=4):
    f32 = np.float32
    x = np.asarray(x, f32)
    mem = np.asarray(mem, f32)
    positions = np.asarray(positions, np.int32)
    even_w_in = np.asarray(even_w_in, f32)
    odd_w_in = np.asarray(odd_w_in, f32)
    even_w_mem_kv = np.asarray(even_w_mem_kv, f32)
    odd_w_mem_kv = np.asarray(odd_w_mem_kv, f32)
    consts = host_consts()
    norms = [np.asarray(even_norm, f32)[0], np.asarray(odd_norm, f32)[0], np.asarray(even_norm, f32)[1],
             np.asarray(odd_norm, f32)[1], np.asarray(mem_norm, f32)]
    ncols = np.stack([n.reshape(16, 128).T for n in norms]).astype(f32)
    sk = np.asarray(odd_sinks, f32)
    common = dict(even_w_out=_perm_rows(np.asarray(even_w_out, f32), EVEN_ORDER),
                  odd_w_out=_perm_rows(np.asarray(odd_w_out, f32), ODD_ORDER),
                  ncols=ncols, final_norm=np.asarray(final_norm, f32).reshape(1, DM), **consts)
    role = []
    for r in range(2):
        chunks = ODD_ORDER[8 * r:8 * r + 6]
        sinks2 = np.stack([sk[:, [2 * gc for gc in chunks]], sk[:, [2 * gc + 1 for gc in chunks]]], axis=1)
        role.append(dict(even_w_in=_slice_even(even_w_in, r), odd_w_in=_slice_odd(odd_w_in, r),
                         even_w_mem_kv=_slice_mem(even_w_mem_kv, r), odd_w_mem_kv=_slice_mem(odd_w_mem_kv, r),
                         sinks2=np.ascontiguousarray(sinks2.astype(f32))))
    if _n_layers not in _CACHE:
        _CACHE[_n_layers] = build_program(_n_layers)
    nc = _CACHE[_n_layers]
    in_maps = []
    for core in range(8):
        b, r = core // 2, core % 2
        m = dict(common)
        m.update(role[r])
        m["x"] = np.ascontiguousarray(x[b])
        m["mem"] = np.ascontiguousarray(mem[b])
        m["pos"] = np.ascontiguousarray(positions[b].reshape(1, S))
        in_maps.append(m)
    res = run_bass_kernel_spmd(nc, in_maps, core_ids=list(range(8)))
    out = np.stack([res.results[2 * b]["out"] for b in range(4)]).astype(f32)
    return out
```

```python
import math
from contextlib import ExitStack
import numpy as np
import concourse.bass as bass
import concourse.mybir as mybir
from concourse.bass_utils import run_bass_kernel_spmd

F32 = mybir.dt.float32
BF16 = mybir.dt.bfloat16
I32 = mybir.dt.int32
AF = mybir.ActivationFunctionType
ALU = mybir.AluOpType
AX = mybir.AxisListType

S = 2048
DM = 2048
EPS = 1e-6
EVEN_OFF = dict(qa=0, ka=384, va=768, qb=1152, kb=1536, vb=1920, qm=2304, gate=2560)
ODD_OFF = dict(qc=0, kc=768, vc=896, qm=1024, gate=1280)
EVEN_ORDER = [0, 1, 2, 6, 7, 8, 12, 13, 3, 4, 5, 9, 10, 11, 14, 15]
ODD_ORDER = [0, 1, 2, 3, 4, 5, 12, 13, 8, 9, 10, 11, 6, 7, 14, 15]
PAIRS = [[0, 1], [2, 3], [4, 5], [6, 7]]
EVEN_PROC = [6, 7, 0, 1, 2, 3, 4, 5]
TWO_PI = 2.0 * math.pi
CW1 = 6.28125
CW2 = TWO_PI - CW1


class Buf:
    __slots__ = ("name", "w", "r")

    def __init__(self, name=""):
        self.name = name
        self.w = None
        self.r = {}


class DSem:
    def __init__(self, key):
        self.key = key
        self.count = 0


class Eng:
    def __init__(self, name):
        self.name = name
        self.ops = []
        self.count = 0
        self.waited = {}
        self.key = "E_" + name


class Emitter:
    def __init__(self):
        self.engs = {n: Eng(n) for n in ("pe", "act", "dve", "pool", "sp")}
        self.dsems = []

    def new_dsem(self):
        d = DSem("D_%d" % len(self.dsems))
        self.dsems.append(d)
        return d

    def _deps(self, eng, reads, writes, is_dma):
        deps = {}

        def add(semkey, val, ename, kind):
            if (not is_dma) and ename == eng.name and kind != "raw":
                return
            if eng.waited.get(semkey, 0) >= val:
                return
            if deps.get(semkey, 0) < val:
                deps[semkey] = val

        for b in reads:
            if b.w is not None:
                add(b.w[0], b.w[1], b.w[2], "raw")
        for b in writes:
            if b.w is not None:
                add(b.w[0], b.w[1], b.w[2], "waw")
            for k, (v, en) in b.r.items():
                add(k, v, en, "war")
        return deps

    def _wait(self, eng, deps):
        for k, v in deps.items():
            eng.ops.append(("wait", k, v))
            eng.waited[k] = v

    def _commit(self, tok, reads, writes):
        k, v, en = tok
        for b in reads:
            b.r[k] = (v, en)
        for b in writes:
            b.w = tok
            b.r = {}

    def op(self, engname, fn, reads=(), writes=()):
        eng = self.engs[engname]
        self._wait(eng, self._deps(eng, reads, writes, False))
        eng.count += 1
        eng.ops.append(("op", fn))
        self._commit((eng.key, eng.count, eng.name), reads, writes)

    def dma(self, parts, dsem, reads=(), writes=()):
        for (q, o, i) in parts:
            eng = self.engs[q]
            self._wait(eng, self._deps(eng, reads, writes, True))
            eng.ops.append(("dma", o, i, dsem.key))
            dsem.count += 16
        self._commit((dsem.key, dsem.count, "dma"), reads, writes)

    def cc(self, ins, outs, dsem, reads=(), writes=()):
        eng = self.engs["pool"]
        self._wait(eng, self._deps(eng, reads, writes, True))
        eng.ops.append(("cc", ins, outs, dsem.key))
        dsem.count += 1
        self._commit((dsem.key, dsem.count, "dma"), reads, writes)

    def fence(self, exclude=()):
        ex = set(d.key for d in exclude)
        for e in self.engs.values():
            deps = {}
            for e2 in self.engs.values():
                if e2 is not e and e2.count > e.waited.get(e2.key, 0):
                    deps[e2.key] = e2.count
            for d in self.dsems:
                if d.key not in ex and d.count > e.waited.get(d.key, 0):
                    deps[d.key] = d.count
            self._wait(e, deps)

    def replay(self, nc):
        with ExitStack() as es:
            sems = {}
            for e in self.engs.values():
                sems[e.key] = es.enter_context(nc.semaphore("s_" + e.name))
            for d in self.dsems:
                sems[d.key] = es.enter_context(nc.semaphore("s_" + d.key))
            block = es.enter_context(nc.Block())

            def run(eng, h):
                for o in eng.ops:
                    if o[0] == "wait":
                        h.wait_ge(sems[o[1]], o[2])
                    elif o[0] == "op":
                        o[1](h).then_inc(sems[eng.key], 1)
                    elif o[0] == "cc":
                        h.collective_compute("AllGather", ALU.bypass, replica_groups=PAIRS, ins=[o[1]],
                                             outs=[o[2]]).then_inc(sems[o[3]], 1)
                    else:
                        h.dma_start(out=o[1], in_=o[2]).then_inc(sems[o[3]], 16)

            @block.tensor
            def _(h):
                run(self.engs["pe"], h)

            @block.scalar
            def _(h):
                run(self.engs["act"], h)

            @block.vector
            def _(h):
                run(self.engs["dve"], h)

            @block.gpsimd
            def _(h):
                run(self.engs["pool"], h)

            @block.sync
            def _(h):
                run(self.engs["sp"], h)


class Ring:
    def __init__(self, items):
        self.items = items
        self.i = 0

    def get(self):
        it = self.items[self.i % len(self.items)]
        self.i += 1
        return it


def host_consts():
    kk = np.arange(128)[:, None]
    c = np.arange(2432)[None, :]
    d = c - kk - 384
    mA = (((d >= 0) & (d <= 128)).astype(np.float32) + ((d % 4 == 0) & (d >= 0) & (d <= 512)).astype(np.float32)
          + ((d % 16 == 0) & (d >= 0) & (d <= 2048)).astype(np.float32))
    c = np.arange(512)[None, :]
    d = c - 128 - kk
    mB = ((d >= 0) & (d <= 127)).astype(np.float32)
    c = np.arange(896)[None, :]
    d = c - 384 - kk
    mC = (d >= 0).astype(np.float32)
    cmask = np.concatenate([mA, mB, mC], axis=1).astype(np.float32)
    ident = np.eye(128, dtype=np.float32)
    esel = np.zeros((8, 8, 128), np.float32)
    for b in range(8):
        esel[b, b, :] = 1.0
    esel = esel.reshape(8, 1024)
    t = np.arange(16)[:, None]
    blk = np.arange(8)[None, :]
    past = blk < (t // 2)
    own = blk == (t // 2)
    pn = np.stack([np.where(past, 0.0, np.where(own, 1e30, -1e30)),
                   np.where(past | own, -30000.0, -60000.0)]).astype(np.float32).reshape(1, 256)
    p = np.arange(128)
    inv128 = np.exp(np.arange(64, dtype=np.float32) * np.float32(-2.0 * math.log(10000.0) / 128)).astype(np.float32)
    inv64 = np.exp(np.arange(32, dtype=np.float32) * np.float32(-2.0 * math.log(10000.0) / 64)).astype(np.float32)
    invf = np.stack([inv128[p % 64], inv64[p % 32]], axis=1).astype(np.float32)
    return dict(cmask=cmask, ident=ident, esel=esel, pastc=pn, invf=invf)


def build_program(n_layers=4):
    nc = bass.Bass("TRN2", target_bir_lowering=False)
    em = Emitter()
    es = ExitStack()

    def dram(name, shape, dt, kind="ExternalInput"):
        return nc.dram_tensor(name, shape, dt, kind=kind).ap()

    x_in = dram("x", [S, DM], F32)
    mem_in = dram("mem", [256, DM], F32)
    pos_in = dram("pos", [1, S], I32)
    w_in_d = [dram("even_w_in", [2, DM, 3584], F32), dram("odd_w_in", [2, DM, 2304], F32)]
    w_mem_d = [dram("even_w_mem_kv", [2, DM, 512], F32), dram("odd_w_mem_kv", [2, DM, 512], F32)]
    w_out_d = [dram("even_w_out", [2, DM, DM], F32), dram("odd_w_out", [2, DM, DM], F32)]
    ncols_d = dram("ncols", [5, 128, 16], F32)
    fin_d = dram("final_norm", [1, DM], F32)
    sinks_d = dram("sinks2", [2, 2, 6], F32)
    cmask_d = dram("cmask", [128, 3840], F32)
    ident_d = dram("ident", [128, 128], F32)
    esel_d = dram("esel", [8, 1024], F32)
    pastc_d = dram("pastc", [1, 256], F32)
    invf_d = dram("invf", [128, 2], F32)
    out_d = dram("out", [S, DM], F32, kind="ExternalOutput")
    xs_d = dram("xs_scratch", [S, DM], F32, kind="Internal")
    memn_d = dram("memn_scratch", [128, 4096], BF16, kind="Internal")
    ybounce_d = [[nc.dram_tensor("ybounce%d_%d" % (i, j), [256, S], BF16).ap() for j in range(4)]
                 for i in range(n_layers)]
    ygath_d = [[nc.dram_tensor("ygath%d_%d" % (i, j), [512, S], BF16).ap() for j in range(4)]
               for i in range(n_layers)]
    cc_ds = [[em.new_dsem() for j in range(4)] for _ in range(n_layers)]
    gin_ds = [[em.new_dsem() for r in range(2)] for j in range(4)]
    gout_ds = [em.new_dsem() for j in range(4)]

    def sb(name, shape, dt):
        return es.enter_context(nc.sbuf_tensor(name, shape, dt))

    R1 = sb("R1", [128, 32768], BF16)
    R2 = sb("R2", [128, 32768], BF16)
    R3 = sb("R3", [128, 16896], BF16)
    cosT = sb("cosT", [128, S], F32)
    sinS = sb("sinS", [128, S], F32)
    mkT = sb("mkT", [128, 2, 256], BF16)
    mvT = sb("mvT", [128, 2, 2, 128], BF16)
    wslots = [sb("wslot%d" % i, [128, 16, 128], BF16) for i in range(3)]
    cmask = sb("cmaskb", [128, 3840], BF16)
    identf = sb("identf", [128, 128], F32)
    identb = sb("identb", [128, 128], BF16)
    ones = sb("ones", [128, 128], BF16)
    ones_lo = sb("ones_lo", [128, 128], BF16)
    ones_hi = sb("ones_hi", [128, 128], BF16)
    esel = sb("eselb", [8, 1024], BF16)
    pastc = sb("pastcb", [128, 256], F32)
    invf = sb("invfb", [128, 2], F32)
    ncols = sb("ncolsb", [128, 5, 16], F32)
    esink = sb("esink", [128, 6], F32)
    small = sb("small", [128, 640], F32)
    smallb = sb("smallb", [128, 160], BF16)

    banks = [es.enter_context(nc.psum_tensor("bank%d" % i, [128, 512], F32)) for i in range(8)]
    PA = Ring([(banks[i], Buf("pa%d" % i)) for i in (0, 1)])
    PS = Ring([(banks[i], Buf("ps%d" % i)) for i in (2, 3, 4)])
    POD = Ring([(banks[i], Buf("pod%d" % i)) for i in (5, 6, 7)])
    PN = Ring(PA.items + PS.items + POD.items)

    hT = R1[:, :].rearrange("p (c n) -> p c n", c=16)
    wout = hT
    yT = R2[:, :].rearrange("p (c n) -> p c n", c=16)
    R2f = R2[:, :].bitcast(F32)
    NG = 2
    xstage = [R2f[:, g * 4096:(g + 1) * 4096].rearrange("p (j d) -> p j d", j=NG) for g in range(4)]
    R3f = R3[:, :].bitcast(F32)

    def r3b(kib_lo, kib_hi):
        return R3[:, kib_lo * 512:kib_hi * 512]

    def r3f(kib_lo, kib_hi):
        return R3f[:, kib_lo * 256:kib_hi * 256]

    qT = r3b(0, 4)
    kT0 = r3b(4, 8)
    kT1 = r3b(8, 12)
    biasT = r3b(8, 12)
    V0 = r3b(12, 16).rearrange("p (t d) -> p t d", t=16)
    V1 = r3b(16, 20).rearrange("p (t d) -> p t d", t=16)
    gT = r3b(20, 24)
    t1 = r3f(24, 26)
    t2 = r3f(26, 28)
    rD = t2
    ytmp = t1
    Pt = [r3b(28 + i, 29 + i) for i in range(3)] + [r3b(32, 33)]
    vTtmp = r3b(31, 32)
    junk = r3b(28, 32)
    posi = R2f[:, 0:2048].bitcast(I32)
    angf = R2f[:, 2048:4096]
    kff = R2f[:, 4096:6144]
    rr = R2f[:, 6144:8192]
    memnT = r3b(16, 24).rearrange("p (c n) -> p c n", c=16)
    xpieces = [r3f(2 * i, 2 * i + 2) for i in range(4)]
    xrow = [r3f(8, 16), r3f(16, 24)]
    gfin = r3f(24, 32)
    junkO = r3b(0, 4)

    B = {}

    def buf(name):
        if name not in B:
            B[name] = Buf(name)
        return B[name]

    PP = Ring([(Pt[i], buf("P%d" % i)) for i in range(4)])
    XP = Ring([(xpieces[i], buf("xp%d" % i), em.new_dsem(), em.new_dsem()) for i in range(4)])
    XR = Ring([(xrow[i], buf("xr%d" % i), em.new_dsem(), em.new_dsem()) for i in range(2)])
    XST = Ring([(xstage[i], buf("xst%d" % i), em.new_dsem()) for i in range(4)])
    xs_bufs = [buf("xs%d" % t) for t in range(16)]
    misc_ds = em.new_dsem()
    rope_ds = em.new_dsem()
    memn_ds = em.new_dsem()
    out_buf = buf("out")
    out_ds = em.new_dsem()
    wout_ds = [em.new_dsem() for _ in range(4)]

    def MM(out, lhsT, rhs, start, stop, reads, writes):
        em.op("pe", lambda h: h.matmul(out, lhsT=lhsT, rhs=rhs, start=start, stop=stop), reads, writes)

    def TR(out, in_, ident, reads, writes):
        em.op("pe", lambda h: h.transpose(out, in_, ident), reads, writes)

    def ACT(out, in_, func, reads, writes, bias=0.0, scale=1.0, accum_out=None):
        if accum_out is None:
            em.op("act", lambda h: h.activation(out=out, in_=in_, func=func, bias=bias, scale=scale), reads, writes)
        else:
            em.op("act", lambda h: h.activation(out=out, in_=in_, func=func, bias=bias, scale=scale,
                                                accum_out=accum_out), reads, writes)

    def ACTMUL(out, in_, mul, reads, writes):
        em.op("act", lambda h: h.mul(out=out, in_=in_, mul=mul), reads, writes)

    def ACTCOPY(out, in_, reads, writes):
        em.op("act", lambda h: h.copy(out=out, in_=in_), reads, writes)

    def TT(eng, out, in0, in1, op, reads, writes):
        em.op(eng, lambda h: h.tensor_tensor(out=out, in0=in0, in1=in1, op=op), reads, writes)

    def TS(eng, out, in0, s1, s2, op0, op1, reads, writes):
        if op1 is None:
            em.op(eng, lambda h: h.tensor_scalar(out=out, in0=in0, scalar1=s1, scalar2=None, op0=op0), reads, writes)
        else:
            em.op(eng, lambda h: h.tensor_scalar(out=out, in0=in0, scalar1=s1, scalar2=s2, op0=op0, op1=op1),
                  reads, writes)

    def STT(eng, out, in0, scalar, in1, op0, op1, reads, writes):
        em.op(eng, lambda h: h.scalar_tensor_tensor(out=out, in0=in0, scalar=scalar, in1=in1, op0=op0, op1=op1),
              reads, writes)

    def CP(eng, out, in_, reads, writes):
        em.op(eng, lambda h: h.tensor_copy(out=out, in_=in_), reads, writes)

    def MEMSET(eng, ap, val, writes):
        em.op(eng, lambda h: h.memset(ap, val), (), writes)

    def RECIP(out, in_, reads, writes):
        em.op("dve", lambda h: h.reciprocal(out=out, in_=in_), reads, writes)

    def DMA(q, out, in_, ds, reads, writes):
        em.dma([(q, out, in_)], ds, reads, writes)

    cb = buf("consts")
    DMA("pool", cmask[:], cmask_d, misc_ds, [], [cb])
    DMA("sp", identf[:], ident_d, misc_ds, [], [cb])
    DMA("pool", identb[:], ident_d, misc_ds, [], [cb])
    DMA("pool", esel[:], esel_d, misc_ds, [], [cb])
    DMA("sp", pastc[:], pastc_d.to_broadcast([128, 256]), misc_ds, [], [cb])
    DMA("sp", invf[:], invf_d, misc_ds, [], [cb])
    DMA("sp", ncols[:], ncols_d.rearrange("l p c -> p l c"), misc_ds, [], [cb])
    MEMSET("pool", ones[:], 1.0, [cb])
    MEMSET("pool", ones_lo[:], 0.0, [cb])
    MEMSET("pool", ones_hi[:], 0.0, [cb])
    em.fence()
    MEMSET("pool", ones_lo[:, 0:64], 1.0, [cb])
    MEMSET("pool", ones_hi[:, 64:128], 1.0, [cb])
    mA = cmask[:, 0:2432]
    mB = cmask[:, 2432:2944]
    mC = cmask[:, 2944:3840]
    em.fence()

    wring = [(wslots[i], buf("w%d" % i), em.new_dsem()) for i in range(3)]
    wq = []
    wstate = dict(loaded=0, used=0)
    loaded_items = []

    def w_issue():
        i = wstate["loaded"]
        spec = wq[i]
        ap, bf, ds = wring[i % 3]
        if spec["zero"]:
            MEMSET("pool", ap[:, :, :], 0.0, [bf])
        parts = []
        for (lo, hi, src) in spec["segs"]:
            parts.append(("pool", ap[:, :, lo:hi], src.rearrange("(c p) n -> p c n", p=128)))
        em.dma(parts, ds, [], [bf])
        wstate["loaded"] += 1

    def w_get(tag):
        i = wstate["used"]
        assert wq[i]["tag"] == tag, (wq[i]["tag"], tag)
        while wstate["loaded"] < min(len(wq), i + 3):
            w_issue()
        wstate["used"] += 1
        ap, bf, ds = wring[i % 3]
        return ap, bf

    def spec(tag, segs, zero=False):
        return dict(tag=tag, segs=segs, zero=zero)

    def layer_specs(l):
        par = l % 2
        li = l // 2
        win = w_in_d[par][li]
        wm = w_mem_d[par][li]
        sp_ = []
        for h in range(2):
            sp_.append(spec(("mk", l, h), [(0, 128, wm[:, h * 128:(h + 1) * 128])]))
            sp_.append(spec(("mv", l, h), [(0, 128, wm[:, 256 + h * 128:256 + (h + 1) * 128])]))
        if par == 0:
            E = EVEN_OFF
            for c in EVEN_PROC:
                sp_.append(spec(("g", l, c), [(0, 128, win[:, E["gate"] + c * 128:E["gate"] + (c + 1) * 128])]))
                if c < 3:
                    names = ("qa", "ka", "va")
                    hh = c
                elif c < 6:
                    names = ("qb", "kb", "vb")
                    hh = c - 3
                else:
                    names = ("qm",)
                    hh = c - 6
                for nm in names:
                    sp_.append(spec((nm, l, c), [(0, 128, win[:, E[nm] + hh * 128:E[nm] + (hh + 1) * 128])]))
        else:
            O = ODD_OFF
            for h in range(2):
                c = 6 + h
                sp_.append(spec(("g", l, c), [(0, 128, win[:, O["gate"] + c * 128:O["gate"] + (c + 1) * 128])]))
                sp_.append(spec(("qm", l, c), [(0, 128, win[:, O["qm"] + h * 128:O["qm"] + (h + 1) * 128])]))
            for g, chunks in ((0, (0, 1, 2, 3)), (1, (4, 5))):
                k0 = O["kc"] + g * 64
                v0 = O["vc"] + g * 64
                sp_.append(spec(("k0", l, g), [(0, 32, win[:, k0:k0 + 32]), (64, 96, win[:, k0 + 32:k0 + 64])], True))
                sp_.append(spec(("k1", l, g), [(32, 64, win[:, k0:k0 + 32]), (96, 128, win[:, k0 + 32:k0 + 64])], True))
                sp_.append(spec(("v0", l, g), [(0, 64, win[:, v0:v0 + 64])], True))
                sp_.append(spec(("v1", l, g), [(64, 128, win[:, v0:v0 + 64])], True))
                for c in chunks:
                    sp_.append(spec(("g", l, c), [(0, 128, win[:, O["gate"] + c * 128:O["gate"] + (c + 1) * 128])]))
                    q0 = O["qc"] + c * 128
                    sp_.append(spec(("qc", l, c), [(0, 32, win[:, q0:q0 + 32]), (64, 96, win[:, q0 + 32:q0 + 64]),
                                                   (32, 64, win[:, q0 + 64:q0 + 96]),
                                                   (96, 128, win[:, q0 + 96:q0 + 128])]))
        return sp_

    for l in range(n_layers):
        wq.extend(layer_specs(l))

    hTb = buf("R1")
    yTb = [buf("yT%d" % c) for c in range(16)]
    sm = buf("small")

    def norm_phase(src_rows, ntiles, gidx, dst, dst_buf, src_bufs):
        ngroups = (ntiles + NG - 1) // NG
        for g in range(ngroups):
            nt = min(NG, ntiles - g * NG)
            xg, xb, xds = XST.get()
            em.dma([("sp", xg[:, 0:nt, :], src_rows(g * NG, nt).rearrange("(j p) d -> p j d", p=128))], xds,
                   [src_bufs[g * NG + j] for j in range(nt)], [xb])
            sg = buf("nstat%d" % g)
            for j in range(nt):
                t = g * NG + j
                ACT(junk, xg[:, j, :], AF.Square, [xb], [buf("junk"), sg], accum_out=small[:, t:t + 1])
            ACT(small[:, 16 + g * NG:16 + g * NG + nt], small[:, g * NG:g * NG + nt], AF.Sqrt, [sg], [sg], bias=EPS,
                scale=1.0 / DM)
            RECIP(small[:, 16 + g * NG:16 + g * NG + nt], small[:, 16 + g * NG:16 + g * NG + nt], [sg], [sg])
            for j in range(nt):
                t = g * NG + j
                TS("dve", xg[:, j, :], xg[:, j, :], small[:, 16 + t:17 + t], None, ALU.mult, None, [sg, xb], [xb])
            for c in range(16):
                ps, pb = PN.get()
                for j in range(nt):
                    TR(ps[:, j * 128:(j + 1) * 128], xg[:, j, c * 128:(c + 1) * 128], identf[:], [xb, cb], [pb])
                dcb = buf("%s_c%d" % (dst_buf.name, c))
                if c % 2 == 0:
                    ACTMUL(dst[:, c, g * NG * 128:g * NG * 128 + nt * 128], ps[:, 0:nt * 128],
                           ncols[:, gidx, c:c + 1], [pb, cb], [dcb])
                else:
                    TS("dve", dst[:, c, g * NG * 128:g * NG * 128 + nt * 128], ps[:, 0:nt * 128],
                       ncols[:, gidx, c:c + 1], None, ALU.mult, None, [pb, cb], [dcb])

    tb = buf("tables")

    def rope_tables(col):
        pb_ = buf("ropetmp")
        T0, T1, T2, T3 = r3f(8, 10), r3f(10, 12), r3f(12, 14), r3f(14, 16)
        for ch in range(4):
            cs = slice(ch * 512, (ch + 1) * 512)
            DMA("sp", T0.bitcast(I32), pos_in[:, cs].to_broadcast([128, 512]), rope_ds, [], [pb_])
            CP("dve", T1, T0.bitcast(I32), [pb_], [pb_])
            TS("dve", T1, T1, invf[:, col:col + 1], None, ALU.mult, None, [pb_, cb], [pb_])
            TS("dve", T2, T1, 1.0 / TWO_PI, None, ALU.mult, None, [pb_], [pb_])
            CP("dve", T3.bitcast(I32), T2, [pb_], [pb_])
            CP("dve", T2, T3.bitcast(I32), [pb_], [pb_])
            STT("dve", T1, T2, -CW1, T1, ALU.mult, ALU.add, [pb_], [pb_])
            STT("dve", T1, T2, -CW2, T1, ALU.mult, ALU.add, [pb_], [pb_])
            TS("dve", T2, T1, math.pi, -TWO_PI, ALU.is_gt, ALU.mult, [pb_], [pb_])
            TT("dve", T3, T1, T2, ALU.add, [pb_], [pb_])
            TS("dve", T2, T3, -math.pi, TWO_PI, ALU.is_lt, ALU.mult, [pb_], [pb_])
            TT("dve", T3, T3, T2, ALU.add, [pb_], [pb_])
            ACT(sinS[0:64, cs], T3[0:64, :], AF.Sin, [pb_], [tb])
            ACT(sinS[64:128, cs], T3[64:128, :], AF.Sin, [pb_], [tb], scale=-1.0)
            TS("dve", T0, T1, math.pi / 2, None, ALU.add, None, [pb_], [pb_])
            TS("dve", T2, T0, math.pi, -TWO_PI, ALU.is_gt, ALU.mult, [pb_], [pb_])
            TT("dve", T0, T0, T2, ALU.add, [pb_], [pb_])
            ACT(cosT[:, cs], T0, AF.Sin, [pb_], [tb])

    def proj_fm(tag, evac):
        wp, wb = w_get(tag)
        for tt in range(4):
            ps, pb = PA.get()
            for kc in range(16):
                MM(ps[:, :], wp[:, kc, :], hT[:, kc, tt * 512:(tt + 1) * 512], kc == 0, kc == 15, [wb, hTb], [pb])
            evac(ps, pb, tt)

    def rope_evac(dst, dst_buf):
        def f(ps, pb, tt):
            sl = slice(tt * 512, (tt + 1) * 512)
            TT("dve", t1, ps[:, :], cosT[:, sl], ALU.mult, [pb, tb], [buf("t1")])
            TT("dve", t2[0:64, :], ps[64:128, :], sinS[64:128, sl], ALU.mult, [pb, tb], [buf("t2")])
            TT("dve", t2[64:128, :], ps[0:64, :], sinS[0:64, sl], ALU.mult, [pb, tb], [buf("t2")])
            TT("pool", dst[:, sl], t1, t2, ALU.add, [buf("t1"), buf("t2")], [dst_buf])
        return f

    def copy_evac(dst, dst_buf):
        def f(ps, pb, tt):
            ACTCOPY(dst[:, tt * 512:(tt + 1) * 512], ps[:, :], [pb], [dst_buf])
        return f

    def silu_evac(ps, pb, tt):
        ACT(gT[:, tt * 512:(tt + 1) * 512], ps[:, :], AF.Silu, [pb], [buf("gT")])

    def v_evac(Vt, vbuf):
        def f(ps, pb, tt):
            ACTCOPY(vTtmp, ps[:, :], [pb], [buf("vTtmp")])
            p2, p2b = PA.get()
            p2v = p2[:, :].bitcast(BF16)
            for j in range(4):
                TR(p2v[:, j * 128:(j + 1) * 128], vTtmp[:, j * 128:(j + 1) * 128], identb[:], [buf("vTtmp"), cb], [p2b])
            CP("dve", Vt[:, tt * 4:(tt + 1) * 4, :], p2v[:, 0:512].rearrange("p (t d) -> p t d", t=4), [p2b], [vbuf])
        return f

    def attn_head(ranges):
        flat = [(ri, bj) for ri, r in enumerate(ranges) for bj in range(len(r[0]))]
        Ps = {}
        acc = {}

        def emit_qk(g):
            ri, bj = flat[g]
            blocks, Wq, Ws, scale, finish = ranges[ri]
            b = blocks[bj]
            Sx, Sb = PS.get()
            for (lhsT, rhs, lo, hi) in b["qk"]:
                MM(Sx[:, lo:hi], lhsT, rhs, True, b.get("bias") is None, b["reads"], [Sb])
                if b.get("bias") is not None:
                    bl, br = b["bias"]
                    MM(Sx[:, lo:hi], bl, br, False, True, [cb, buf("biasT")], [Sb])
            P, Pb = PP.get()
            ACT(P[:, 0:Ws], Sx[:, 0:Ws], AF.Exp, [Sb], [Pb], scale=scale)
            for (lo, hi, m) in b.get("masks", ()):
                TT("dve", P[:, lo:hi], P[:, lo:hi], m, ALU.mult, [Pb, cb], [Pb])
            Ps[g] = (P, Pb)

        def emit_pv(g):
            ri, bj = flat[g]
            blocks, Wq, Ws, scale, finish = ranges[ri]
            nb = len(blocks)
            b = blocks[bj]
            if bj == 0:
                Dn, Db = POD.get()
                O, Ob = POD.get()
                acc[ri] = (O, Ob, Dn, Db)
            O, Ob, Dn, Db = acc[ri]
            P, Pb = Ps.pop(g)
            n = len(b["pv"])
            for idx, (vl, ol, lo, hi) in enumerate(b["pv"]):
                first = (bj == 0 and idx == 0)
                last = (bj == nb - 1 and idx == n - 1)
                MM(O[:, 0:Wq], vl, P[:, lo:hi], first, last, [Pb] + b["vreads"], [Ob])
                MM(Dn[:, 0:Wq], ol, P[:, lo:hi], first, last, [Pb, cb], [Db])
            if bj == nb - 1:
                finish(O, Ob, Dn, Db)

        ng = len(flat)
        emit_qk(0)
        if ng > 1:
            emit_qk(1)
        for g in range(ng):
            if g + 2 < ng:
                emit_qk(g + 2)
            emit_pv(g)

    def finish_std(c, q0, Wq, sink_col=None):
        def f(O, Ob, Dn, Db):
            rb, yb = buf("t2"), buf("t1")
            if sink_col is None:
                ACT(rD[:, 0:Wq], Dn[:, 0:Wq], AF.Ln, [Db], [rb])
            else:
                ACT(rD[:, 0:Wq], Dn[:, 0:Wq], AF.Ln, [Db, cb], [rb], bias=sink_col)
            ACT(rD[:, 0:Wq], rD[:, 0:Wq], AF.Exp, [rb], [rb], scale=-1.0)
            TT("dve", ytmp[:, 0:Wq], O[:, 0:Wq], rD[:, 0:Wq], ALU.mult, [Ob, rb], [yb])
            TT("pool", yT[:, c, q0:q0 + Wq], ytmp[:, 0:Wq], gT[:, q0:q0 + Wq], ALU.mult, [yb, buf("gT")], [yTb[c]])
        return f

    SC128 = 128 ** -0.5
    SC64 = 64 ** -0.5
    qb_, kb0_, kb1_, vb0_, vb1_ = buf("qT"), buf("kT0"), buf("kT1"), buf("V0"), buf("V1")

    def mem_head(l, c, h, qtag):
        proj_fm(("g", l, c), silu_evac)
        proj_fm(qtag, copy_evac(qT, qb_))
        rngs = []
        for r in range(4):
            q0 = r * 512
            blocks = []
            for kb in range(2):
                blocks.append(dict(qk=[(mkT[:, h, kb * 128:(kb + 1) * 128], qT[:, q0:q0 + 512], 0, 512)],
                                   reads=[buf("mk"), qb_], pv=[(mvT[:, kb, h, :], ones[:], 0, 512)],
                                   vreads=[buf("mk")]))
            rngs.append((blocks, 512, 512, SC128, finish_std(c, q0, 512)))
        attn_head(rngs)

    def phase_M(l):
        DMA("sp", r3b(16, 24), memn_d, memn_ds, [buf("memn_d")], [buf("memnT")])
        mb = buf("mk")
        for h in range(2):
            wp, wb = w_get(("mk", l, h))
            ps, pb = PA.get()
            for kc in range(16):
                MM(ps[:, 0:256], wp[:, kc, :], memnT[:, kc, :], kc == 0, kc == 15, [wb, buf("memnT")], [pb])
            ACTCOPY(mkT[:, h, :], ps[:, 0:256], [pb], [mb])
            wp, wb = w_get(("mv", l, h))
            ps, pb = PA.get()
            for tl in range(2):
                for kc in range(16):
                    MM(ps[:, tl * 128:(tl + 1) * 128], memnT[:, kc, tl * 128:(tl + 1) * 128], wp[:, kc, :], kc == 0,
                       kc == 15, [wb, buf("memnT")], [pb])
            CP("dve", mvT[:, :, h, :], ps[:, 0:256].rearrange("p (t d) -> p t d", t=2), [pb], [mb])

    def even_layer(l):
        done = set()
        for c in EVEN_PROC:
            if c < 3:
                proj_fm(("g", l, c), silu_evac)
                proj_fm(("qa", l, c), rope_evac(qT, qb_))
                proj_fm(("ka", l, c), rope_evac(kT0, kb0_))
                proj_fm(("va", l, c), v_evac(V0, vb0_))
                rngs = []
                for r in range(4):
                    q0 = r * 512
                    blocks = []
                    for kb in range(4 * r + 4):
                        delta = q0 - 128 * kb
                        blocks.append(dict(qk=[(kT0[:, kb * 128:(kb + 1) * 128], qT[:, q0:q0 + 512], 0, 512)],
                                           reads=[kb0_, qb_], masks=[(0, 512, mA[:, delta + 384:delta + 384 + 512])],
                                           pv=[(V0[:, kb, :], ones[:], 0, 512)], vreads=[vb0_]))
                    rngs.append((blocks, 512, 512, SC128, finish_std(c, q0, 512)))
                attn_head(rngs)
            elif c < 6:
                proj_fm(("g", l, c), silu_evac)
                proj_fm(("qb", l, c), rope_evac(qT, qb_))
                proj_fm(("kb", l, c), rope_evac(kT0, kb0_))
                proj_fm(("vb", l, c), v_evac(V0, vb0_))
                if c == EVEN_PROC[-1]:
                    prefetch_wout(l)
                km = small[:, 32:40]
                kmt = small[:, 40:48]
                kmh = smallb[:, 0:8]
                kml = smallb[:, 8:16]
                gm = small[:, 64:192]
                mx = small[:, 192:320]
                bs = small[:, 320:448]
                bsb = smallb[:, 16:144]
                em.op("dve", lambda h: h.reduce_sum(out=km, in_=kT0.rearrange("p (b n) -> p b n", b=8), axis=AX.X),
                      [kb0_], [sm])
                TS("dve", km, km, 1.0 / 256.0, None, ALU.mult, None, [sm], [sm])
                CP("dve", kmh, km, [sm], [sm])
                TT("dve", kmt, km, kmh, ALU.subtract, [sm], [sm])
                CP("dve", kml, kmt, [sm], [sm])
                gp, gpb = PA.get()
                for t in range(16):
                    MM(gp[:, t * 8:(t + 1) * 8], qT[:, t * 128:(t + 1) * 128], kmh, True, False, [qb_, sm], [gpb])
                    MM(gp[:, t * 8:(t + 1) * 8], qT[:, t * 128:(t + 1) * 128], kml, False, True, [qb_, sm], [gpb])
                TT("dve", gm, gp[:, 0:128], pastc[:, 0:128], ALU.add, [gpb, cb], [sm])
                for t in range(16):
                    em.op("dve", (lambda t_: (lambda h: h.max(out=mx[:, t_ * 8:(t_ + 1) * 8],
                                                             in_=gm[:, t_ * 8:(t_ + 1) * 8])))(t), [sm], [sm])
                for t in range(16):
                    TS("dve", bs[:, t * 8:(t + 1) * 8], gm[:, t * 8:(t + 1) * 8], mx[:, t * 8 + 3:t * 8 + 4], None,
                       ALU.is_ge, None, [sm], [sm])
                STT("dve", bsb, bs, 30000.0, pastc[:, 128:256], ALU.mult, ALU.add, [sm, cb], [sm])
                for g4 in range(4):
                    p2, p2b = PA.get()
                    p2v = p2[:, :].bitcast(BF16)
                    for j in range(4):
                        t = g4 * 4 + j
                        TR(p2v[0:8, j * 128:(j + 1) * 128], bsb[:, t * 8:(t + 1) * 8], identb[:], [sm, cb], [p2b])
                    CP("dve", biasT[0:8, g4 * 512:(g4 + 1) * 512], p2v[0:8, 0:512], [p2b], [buf("biasT")])
                rngs = []
                for r in range(4):
                    q0 = r * 512
                    blocks = []
                    for kb in range(4 * r + 4):
                        b_ = dict(qk=[(kT0[:, kb * 128:(kb + 1) * 128], qT[:, q0:q0 + 512], 0, 512)],
                                  reads=[kb0_, qb_],
                                  bias=(esel[0:8, (kb // 2) * 128:(kb // 2 + 1) * 128], biasT[0:8, q0:q0 + 512]),
                                  pv=[(V0[:, kb, :], ones[:], 0, 512)], vreads=[vb0_])
                        if kb >= 4 * r:
                            delta = q0 - 128 * kb
                            b_["masks"] = [(0, 512, mC[:, delta + 384:delta + 384 + 512])]
                        blocks.append(b_)
                    rngs.append((blocks, 512, 512, SC128, finish_std(c, q0, 512)))
                attn_head(rngs)
            else:
                mem_head(l, c, c - 6, ("qm", l, c))
            done.add(c)
            if (c ^ 1) in done:
                exchange_part(l, c // 2)

    def odd_layer(l):
        li = l // 2
        DMA("sp", esink[0:64, :], sinks_d[li, 0:1, :].to_broadcast([64, 6]), misc_ds, [], [buf("esink")])
        DMA("sp", esink[64:128, :], sinks_d[li, 1:2, :].to_broadcast([64, 6]), misc_ds, [], [buf("esink")])
        ACT(esink[:, :], esink[:, :], AF.Exp, [buf("esink")], [cb])
        for h in range(2):
            mem_head(l, 6 + h, h, ("qm", l, 6 + h))
        exchange_part(l, 3)
        for g, chunks in ((0, (0, 1, 2, 3)), (1, (4, 5))):
            proj_fm(("k0", l, g), rope_evac(kT0, kb0_))
            proj_fm(("k1", l, g), rope_evac(kT1, kb1_))
            proj_fm(("v0", l, g), v_evac(V0, vb0_))
            proj_fm(("v1", l, g), v_evac(V1, vb1_))
            for c in chunks:
                proj_fm(("g", l, c), silu_evac)
                proj_fm(("qc", l, c), rope_evac(qT, qb_))
                if c == 5:
                    prefetch_wout(l)
                rngs = []
                for i in range(8):
                    q0 = i * 256
                    blocks = []
                    for kb in (2 * i - 1, 2 * i, 2 * i + 1):
                        if kb < 0:
                            continue
                        delta = q0 - 128 * kb
                        m = mB[:, delta + 128:delta + 128 + 256]
                        ks = slice(kb * 128, (kb + 1) * 128)
                        blocks.append(dict(qk=[(kT0[:, ks], qT[:, q0:q0 + 256], 0, 256),
                                               (kT1[:, ks], qT[:, q0:q0 + 256], 256, 512)],
                                           reads=[kb0_, kb1_, qb_], masks=[(0, 256, m), (256, 512, m)],
                                           pv=[(V0[:, kb, :], ones_lo[:], 0, 256), (V1[:, kb, :], ones_hi[:], 256, 512)],
                                           vreads=[vb0_, vb1_]))
                    rngs.append((blocks, 256, 512, SC64, finish_std(c, q0, 256, sink_col=esink[:, c:c + 1])))
                attn_head(rngs)
                if c % 2 == 1:
                    exchange_part(l, c // 2)

    def exchange_part(l, j):
        bb, gb = buf("ybounce%d_%d" % (l, j)), buf("ygath%d_%d" % (l, j))
        DMA("sp", ybounce_d[l][j].rearrange("(c p) n -> p c n", p=128), yT[:, 2 * j:2 * j + 2, :], gout_ds[j],
            yTb[2 * j:2 * j + 2], [bb])
        em.cc(ybounce_d[l][j], ygath_d[l][j], cc_ds[l][j], [bb], [gb])
        for r in range(2):
            DMA("sp", yT[:, r * 8 + 2 * j:r * 8 + 2 * j + 2, :],
                ygath_d[l][j][r * 256:(r + 1) * 256, :].rearrange("(c p) n -> p c n", p=128), gin_ds[j][r], [gb],
                yTb[r * 8 + 2 * j:r * 8 + 2 * j + 2])

    def prefetch_wout(l):
        par, li = l % 2, l // 2
        wo = w_out_d[par][li]
        for n in range(4):
            em.dma([("pool", wout[:, :, n * 512:(n + 1) * 512],
                     wo[:, n * 512:(n + 1) * 512].rearrange("(c p) n -> p c n", p=128))], wout_ds[n], [],
                   [hTb, buf("wout%d" % n)] if n == 0 else [buf("wout%d" % n)])

    def phase_O(l, ns=(0, 1, 2, 3)):
        par, li = l % 2, l // 2
        src = x_in if l == 0 else xs_d
        for n in ns:
            wb = buf("wout%d" % n)
            for t in range(16):
                xp, xb, ds_in, ds_out = XP.get()
                rows = slice(t * 128, (t + 1) * 128)
                cols = slice(n * 512, (n + 1) * 512)
                xsb = buf("xs%d_%d" % (t, n))
                DMA("sp", xp, src[rows, cols], ds_in, [xsb], [xb])
                ps, pb = PA.get()
                for kc in range(16):
                    MM(ps[:, :], yT[:, kc, rows], wout[:, kc, cols], kc == 0, kc == 15, [yTb[kc], wb], [pb])
                TT("dve", xp, ps[:, :], xp, ALU.add, [pb, xb], [xb])
                DMA("sp", xs_d[rows, cols], xp, ds_out, [xb], [xsb, xs_bufs[t]])

    def phase_F():
        DMA("sp", gfin, fin_d.to_broadcast([128, DM]), misc_ds, [], [buf("gfin")])
        for t in range(16):
            xr, xb, ds_in, ds_out = XR.get()
            rows = slice(t * 128, (t + 1) * 128)
            DMA("sp", xr, xs_d[rows, :], ds_in, [xs_bufs[t]], [xb])
            sg = buf("fstat%d" % t)
            ACT(junkO, xr, AF.Square, [xb], [buf("junkO"), sg], accum_out=small[:, t:t + 1])
            ACT(small[:, 16 + t:17 + t], small[:, t:t + 1], AF.Sqrt, [sg], [sg], bias=EPS, scale=1.0 / DM)
            RECIP(small[:, 16 + t:17 + t], small[:, 16 + t:17 + t], [sg], [sg])
            STT("dve", xr, xr, small[:, 16 + t:17 + t], gfin, ALU.mult, ALU.mult, [sg, xb, buf("gfin")], [xb])
            DMA("sp", out_d[rows, :], xr, ds_out, [xb], [out_buf])

    memb = buf("memn_sb")
    norm_phase(lambda t0, n: mem_in[t0 * 128:(t0 + n) * 128, :], 2, 4, memnT, memb, [buf("memsrc")] * 2)
    em.fence()
    DMA("sp", memn_d, r3b(16, 24), misc_ds, [memb], [buf("memn_d")])
    em.fence()

    xin_bufs = [buf("xin")] * 16
    rope_tables(0)
    phase_M(0)
    em.fence()
    for l in range(n_layers):
        par = l % 2
        src = x_in if l == 0 else xs_d
        norm_phase(lambda t0, n, s=src: s[t0 * 128:(t0 + n) * 128, :], 16, l, hT, hTb,
                   xin_bufs if l == 0 else xs_bufs)
        em.fence()
        if par == 0:
            even_layer(l)
        else:
            odd_layer(l)
        em.fence(exclude=wout_ds + [d for j in range(4) for d in gin_ds[j]] + [d for dl in cc_ds for d in dl]
                 + gout_ds)
        phase_O(l, (0,))
        if l + 1 < n_layers:
            rope_tables((l + 1) % 2)
            phase_M(l + 1)
        phase_O(l, (1, 2, 3))
        em.fence()
    phase_F()
    em.fence()
    em.replay(nc)
    es.close()
    return nc


_CACHE = {}


def _slice_even(w, r):
    ids = [3 * r + j for j in range(3)]
    cols = []
    for base in (0, 768, 1536, 2304, 3072, 3840):
        for h in ids:
            cols.append(np.arange(base + h * 128, base + (h + 1) * 128))
    for h in (2 * r, 2 * r + 1):
        cols.append(np.arange(4608 + h * 128, 4608 + (h + 1) * 128))
    for gc in EVEN_ORDER[8 * r:8 * r + 8]:
        cols.append(np.arange(5120 + gc * 128, 5120 + (gc + 1) * 128))
    return np.ascontiguousarray(w[:, :, np.concatenate(cols)])


def _slice_odd(w, r):
    chunks = ODD_ORDER[8 * r:8 * r + 8]
    groups = [0, 1] if r == 0 else [2, 1]
    cols = []
    for gc in chunks[:6]:
        cols.append(np.arange(gc * 128, (gc + 1) * 128))
    for g in groups:
        cols.append(np.arange(1536 + g * 64, 1536 + (g + 1) * 64))
    for g in groups:
        cols.append(np.arange(1728 + g * 64, 1728 + (g + 1) * 64))
    for h in (2 * r, 2 * r + 1):
        cols.append(np.arange(1920 + h * 128, 1920 + (h + 1) * 128))
    for gc in chunks:
        cols.append(np.arange(2432 + gc * 128, 2432 + (gc + 1) * 128))
    return np.ascontiguousarray(w[:, :, np.concatenate(cols)])


def _slice_mem(w, r):
    cols = []
    for base in (0, 512):
        for h in (2 * r, 2 * r + 1):
            cols.append(np.arange(base + h * 128, base + (h + 1) * 128))
    return np.ascontiguousarray(w[:, :, np.concatenate(cols)])


def _perm_rows(w, order):
    rows = np.concatenate([np.arange(gc * 128, (gc + 1) * 128) for gc in order])
    return np.ascontiguousarray(w[:, rows, :])


def kernel(x, mem, positions, even_norm, even_w_in, even_w_mem_kv, even_w_out,
           odd_norm, odd_w_in, odd_w_mem_kv, odd_w_out, odd_sinks, mem_norm, final_norm, _n_layers=4):
    f32 = np.float32
    x = np.asarray(x, f32)
    mem = np.asarray(mem, f32)
    positions = np.asarray(positions, np.int32)
    even_w_in = np.asarray(even_w_in, f32)
    odd_w_in = np.asarray(odd_w_in, f32)
    even_w_mem_kv = np.asarray(even_w_mem_kv, f32)
    odd_w_mem_kv = np.asarray(odd_w_mem_kv, f32)
    consts = host_consts()
    norms = [np.asarray(even_norm, f32)[0], np.asarray(odd_norm, f32)[0], np.asarray(even_norm, f32)[1],
             np.asarray(odd_norm, f32)[1], np.asarray(mem_norm, f32)]
    ncols = np.stack([n.reshape(16, 128).T for n in norms]).astype(f32)
    sk = np.asarray(odd_sinks, f32)
    common = dict(even_w_out=_perm_rows(np.asarray(even_w_out, f32), EVEN_ORDER),
                  odd_w_out=_perm_rows(np.asarray(odd_w_out, f32), ODD_ORDER),
                  ncols=ncols, final_norm=np.asarray(final_norm, f32).reshape(1, DM), **consts)
    role = []
    for r in range(2):
        chunks = ODD_ORDER[8 * r:8 * r + 6]
        sinks2 = np.stack([sk[:, [2 * gc for gc in chunks]], sk[:, [2 * gc + 1 for gc in chunks]]], axis=1)
        role.append(dict(even_w_in=_slice_even(even_w_in, r), odd_w_in=_slice_odd(odd_w_in, r),
                         even_w_mem_kv=_slice_mem(even_w_mem_kv, r), odd_w_mem_kv=_slice_mem(odd_w_mem_kv, r),
                         sinks2=np.ascontiguousarray(sinks2.astype(f32))))
    if _n_layers not in _CACHE:
        _CACHE[_n_layers] = build_program(_n_layers)
    nc = _CACHE[_n_layers]
    in_maps = []
    for core in range(8):
        b, r = core // 2, core % 2
        m = dict(common)
        m.update(role[r])
        m["x"] = np.ascontiguousarray(x[b])
        m["mem"] = np.ascontiguousarray(mem[b])
        m["pos"] = np.ascontiguousarray(positions[b].reshape(1, S))
        in_maps.append(m)
    res = run_bass_kernel_spmd(nc, in_maps, core_ids=list(range(8)))
    out = np.stack([res.results[2 * b]["out"] for b in range(4)]).astype(f32)
    return out
```

```python
import math
from contextlib import ExitStack
import numpy as np
import concourse.bass as bass
import concourse.mybir as mybir
from concourse.bass_utils import run_bass_kernel_spmd

F32 = mybir.dt.float32
BF16 = mybir.dt.bfloat16
I32 = mybir.dt.int32
AF = mybir.ActivationFunctionType
ALU = mybir.AluOpType
AX = mybir.AxisListType

S = 2048
DM = 2048
EPS = 1e-6
EVEN_OFF = dict(qa=0, ka=384, va=768, qb=1152, kb=1536, vb=1920, qm=2304, gate=2560)
ODD_OFF = dict(qc=0, kc=768, vc=896, qm=1024, gate=1280)
EVEN_ORDER = [0, 1, 2, 6, 7, 8, 12, 13, 3, 4, 5, 9, 10, 11, 14, 15]
ODD_ORDER = [0, 1, 2, 3, 4, 5, 12, 13, 8, 9, 10, 11, 6, 7, 14, 15]
PAIRS = [[0, 1], [2, 3], [4, 5], [6, 7]]
EVEN_PROC = [6, 7, 0, 1, 2, 3, 4, 5]
TWO_PI = 2.0 * math.pi
CW1 = 6.28125
CW2 = TWO_PI - CW1


class Buf:
    __slots__ = ("name", "w", "r")

    def __init__(self, name=""):
        self.name = name
        self.w = None
        self.r = {}


class DSem:
    def __init__(self, key):
        self.key = key
        self.count = 0


class Eng:
    def __init__(self, name):
        self.name = name
        self.ops = []
        self.count = 0
        self.waited = {}
        self.key = "E_" + name


class Emitter:
    def __init__(self):
        self.engs = {n: Eng(n) for n in ("pe", "act", "dve", "pool", "sp")}
        self.dsems = []

    def new_dsem(self):
        d = DSem("D_%d" % len(self.dsems))
        self.dsems.append(d)
        return d

    def _deps(self, eng, reads, writes, is_dma):
        deps = {}

        def add(semkey, val, ename, kind):
            if (not is_dma) and ename == eng.name and kind != "raw":
                return
            if eng.waited.get(semkey, 0) >= val:
                return
            if deps.get(semkey, 0) < val:
                deps[semkey] = val

        for b in reads:
            if b.w is not None:
                add(b.w[0], b.w[1], b.w[2], "raw")
        for b in writes:
            if b.w is not None:
                add(b.w[0], b.w[1], b.w[2], "waw")
            for k, (v, en) in b.r.items():
                add(k, v, en, "war")
        return deps

    def _wait(self, eng, deps):
        for k, v in deps.items():
            eng.ops.append(("wait", k, v))
            eng.waited[k] = v

    def _commit(self, tok, reads, writes):
        k, v, en = tok
        for b in reads:
            b.r[k] = (v, en)
        for b in writes:
            b.w = tok
            b.r = {}

    def op(self, engname, fn, reads=(), writes=()):
        eng = self.engs[engname]
        self._wait(eng, self._deps(eng, reads, writes, False))
        eng.count += 1
        eng.ops.append(("op", fn))
        self._commit((eng.key, eng.count, eng.name), reads, writes)

    def dma(self, parts, dsem, reads=(), writes=()):
        for (q, o, i) in parts:
            eng = self.engs[q]
            self._wait(eng, self._deps(eng, reads, writes, True))
            eng.ops.append(("dma", o, i, dsem.key))
            dsem.count += 16
        self._commit((dsem.key, dsem.count, "dma"), reads, writes)

    def cc(self, ins, outs, dsem, reads=(), writes=()):
        eng = self.engs["pool"]
        self._wait(eng, self._deps(eng, reads, writes, True))
        eng.ops.append(("cc", ins, outs, dsem.key))
        dsem.count += 1
        self._commit((dsem.key, dsem.count, "dma"), reads, writes)

    def fence(self, exclude=()):
        ex = set(d.key for d in exclude)
        for e in self.engs.values():
            deps = {}
            for e2 in self.engs.values():
                if e2 is not e and e2.count > e.waited.get(e2.key, 0):
                    deps[e2.key] = e2.count
            for d in self.dsems:
                if d.key not in ex and d.count > e.waited.get(d.key, 0):
                    deps[d.key] = d.count
            self._wait(e, deps)

    def replay(self, nc):
        with ExitStack() as es:
            sems = {}
            for e in self.engs.values():
                sems[e.key] = es.enter_context(nc.semaphore("s_" + e.name))
            for d in self.dsems:
                sems[d.key] = es.enter_context(nc.semaphore("s_" + d.key))
            block = es.enter_context(nc.Block())

            def run(eng, h):
                for o in eng.ops:
                    if o[0] == "wait":
                        h.wait_ge(sems[o[1]], o[2])
                    elif o[0] == "op":
                        o[1](h).then_inc(sems[eng.key], 1)
                    elif o[0] == "cc":
                        h.collective_compute("AllGather", ALU.bypass, replica_groups=PAIRS, ins=[o[1]],
                                             outs=[o[2]]).then_inc(sems[o[3]], 1)
                    else:
                        h.dma_start(out=o[1], in_=o[2]).then_inc(sems[o[3]], 16)

            @block.tensor
            def _(h):
                run(self.engs["pe"], h)

            @block.scalar
            def _(h):
                run(self.engs["act"], h)

            @block.vector
            def _(h):
                run(self.engs["dve"], h)

            @block.gpsimd
            def _(h):
                run(self.engs["pool"], h)

            @block.sync
            def _(h):
                run(self.engs["sp"], h)


class Ring:
    def __init__(self, items):
        self.items = items
        self.i = 0

    def get(self):
        it = self.items[self.i % len(self.items)]
        self.i += 1
        return it


def host_consts():
    kk = np.arange(128)[:, None]
    c = np.arange(2432)[None, :]
    d = c - kk - 384
    mA = (((d >= 0) & (d <= 128)).astype(np.float32) + ((d % 4 == 0) & (d >= 0) & (d <= 512)).astype(np.float32)
          + ((d % 16 == 0) & (d >= 0) & (d <= 2048)).astype(np.float32))
    c = np.arange(512)[None, :]
    d = c - 128 - kk
    mB = ((d >= 0) & (d <= 127)).astype(np.float32)
    c = np.arange(896)[None, :]
    d = c - 384 - kk
    mC = (d >= 0).astype(np.float32)
    cmask = np.concatenate([mA, mB, mC], axis=1).astype(np.float32)
    ident = np.eye(128, dtype=np.float32)
    esel = np.zeros((8, 8, 128), np.float32)
    for b in range(8):
        esel[b, b, :] = 1.0
    esel = esel.reshape(8, 1024)
    t = np.arange(16)[:, None]
    blk = np.arange(8)[None, :]
    past = blk < (t // 2)
    own = blk == (t // 2)
    pn = np.stack([np.where(past, 0.0, np.where(own, 1e30, -1e30)),
                   np.where(past | own, -30000.0, -60000.0)]).astype(np.float32).reshape(1, 256)
    p = np.arange(128)
    inv128 = np.exp(np.arange(64, dtype=np.float32) * np.float32(-2.0 * math.log(10000.0) / 128)).astype(np.float32)
    inv64 = np.exp(np.arange(32, dtype=np.float32) * np.float32(-2.0 * math.log(10000.0) / 64)).astype(np.float32)
    invf = np.stack([inv128[p % 64], inv64[p % 32]], axis=1).astype(np.float32)
    return dict(cmask=cmask, ident=ident, esel=esel, pastc=pn, invf=invf)


def build_program(n_layers=4):
    nc = bass.Bass("TRN2", target_bir_lowering=False)
    em = Emitter()
    es = ExitStack()

    def dram(name, shape, dt, kind="ExternalInput"):
        return nc.dram_tensor(name, shape, dt, kind=kind).ap()

    x_in = dram("x", [S, DM], F32)
    mem_in = dram("mem", [256, DM], F32)
    pos_in = dram("pos", [1, S], I32)
    w_in_d = [dram("even_w_in", [2, DM, 3584], F32), dram("odd_w_in", [2, DM, 2304], F32)]
    w_mem_d = [dram("even_w_mem_kv", [2, DM, 512], F32), dram("odd_w_mem_kv", [2, DM, 512], F32)]
    w_out_d = [dram("even_w_out", [2, DM, DM], F32), dram("odd_w_out", [2, DM, DM], F32)]
    ncols_d = dram("ncols", [5, 128, 16], F32)
    fin_d = dram("final_norm", [1, DM], F32)
    sinks_d = dram("sinks2", [2, 2, 6], F32)
    cmask_d = dram("cmask", [128, 3840], F32)
    ident_d = dram("ident", [128, 128], F32)
    esel_d = dram("esel", [8, 1024], F32)
    pastc_d = dram("pastc", [1, 256], F32)
    invf_d = dram("invf", [128, 2], F32)
    out_d = dram("out", [S, DM], F32, kind="ExternalOutput")
    xs_d = dram("xs_scratch", [S, DM], F32, kind="Internal")
    memn_d = dram("memn_scratch", [128, 4096], BF16, kind="Internal")
    ybounce_d = [[nc.dram_tensor("ybounce%d_%d" % (i, j), [256, S], BF16).ap() for j in range(4)]
                 for i in range(n_layers)]
    ygath_d = [[nc.dram_tensor("ygath%d_%d" % (i, j), [512, S], BF16).ap() for j in range(4)]
               for i in range(n_layers)]
    cc_ds = [[em.new_dsem() for j in range(4)] for _ in range(n_layers)]
    gin_ds = [[em.new_dsem() for r in range(2)] for j in range(4)]
    gout_ds = [em.new_dsem() for j in range(4)]

    def sb(name, shape, dt):
        return es.enter_context(nc.sbuf_tensor(name, shape, dt))

    R1 = sb("R1", [128, 32768], BF16)
    R2 = sb("R2", [128, 32768], BF16)
    R3 = sb("R3", [128, 16896], BF16)
    cosT = sb("cosT", [128, S], F32)
    sinS = sb("sinS", [128, S], F32)
    mkT = sb("mkT", [128, 2, 256], BF16)
    mvT = sb("mvT", [128, 2, 2, 128], BF16)
    wslots = [sb("wslot%d" % i, [128, 16, 128], BF16) for i in range(3)]
    cmask = sb("cmaskb", [128, 3840], BF16)
    identf = sb("identf", [128, 128], F32)
    identb = sb("identb", [128, 128], BF16)
    ones = sb("ones", [128, 128], BF16)
    ones_lo = sb("ones_lo", [128, 128], BF16)
    ones_hi = sb("ones_hi", [128, 128], BF16)
    esel = sb("eselb", [8, 1024], BF16)
    pastc = sb("pastcb", [128, 256], F32)
    invf = sb("invfb", [128, 2], F32)
    ncols = sb("ncolsb", [128, 5, 16], F32)
    esink = sb("esink", [128, 6], F32)
    small = sb("small", [128, 640], F32)
    smallb = sb("smallb", [128, 160], BF16)

    banks = [es.enter_context(nc.psum_tensor("bank%d" % i, [128, 512], F32)) for i in range(8)]
    PA = Ring([(banks[i], Buf("pa%d" % i)) for i in (0, 1)])
    PS = Ring([(banks[i], Buf("ps%d" % i)) for i in (2, 3, 4)])
    POD = Ring([(banks[i], Buf("pod%d" % i)) for i in (5, 6, 7)])
    PN = Ring(PA.items + PS.items + POD.items)

    hT = R1[:, :].rearrange("p (c n) -> p c n", c=16)
    wout = hT
    yT = R2[:, :].rearrange("p (c n) -> p c n", c=16)
    R2f = R2[:, :].bitcast(F32)
    NG = 2
    xstage = [R2f[:, g * 4096:(g + 1) * 4096].rearrange("p (j d) -> p j d", j=NG) for g in range(4)]
    R3f = R3[:, :].bitcast(F32)

    def r3b(kib_lo, kib_hi):
        return R3[:, kib_lo * 512:kib_hi * 512]

    def r3f(kib_lo, kib_hi):
        return R3f[:, kib_lo * 256:kib_hi * 256]

    qT = r3b(0, 4)
    kT0 = r3b(4, 8)
    kT1 = r3b(8, 12)
    biasT = r3b(8, 12)
    V0 = r3b(12, 16).rearrange("p (t d) -> p t d", t=16)
    V1 = r3b(16, 20).rearrange("p (t d) -> p t d", t=16)
    gT = r3b(20, 24)
    t1 = r3f(24, 26)
    t2 = r3f(26, 28)
    rD = t2
    ytmp = t1
    Pt = [r3b(28 + i, 29 + i) for i in range(3)] + [r3b(32, 33)]
    vTtmp = r3b(31, 32)
    junk = r3b(28, 32)
    posi = R2f[:, 0:2048].bitcast(I32)
    angf = R2f[:, 2048:4096]
    kff = R2f[:, 4096:6144]
    rr = R2f[:, 6144:8192]
    memnT = r3b(16, 24).rearrange("p (c n) -> p c n", c=16)
    xpieces = [r3f(2 * i, 2 * i + 2) for i in range(4)]
    xrow = [r3f(8, 16), r3f(16, 24)]
    gfin = r3f(24, 32)
    junkO = r3b(0, 4)

    B = {}

    def buf(name):
        if name not in B:
            B[name] = Buf(name)
        return B[name]

    PP = Ring([(Pt[i], buf("P%d" % i)) for i in range(4)])
    XP = Ring([(xpieces[i], buf("xp%d" % i), em.new_dsem(), em.new_dsem()) for i in range(4)])
    XR = Ring([(xrow[i], buf("xr%d" % i), em.new_dsem(), em.new_dsem()) for i in range(2)])
    XST = Ring([(xstage[i], buf("xst%d" % i), em.new_dsem()) for i in range(4)])
    xs_bufs = [buf("xs%d" % t) for t in range(16)]
    misc_ds = em.new_dsem()
    miscp_ds = em.new_dsem()
    rope_ds = em.new_dsem()
    memn_ds = em.new_dsem()
    out_buf = buf("out")
    out_ds = em.new_dsem()
    wout_ds = [em.new_dsem() for _ in range(4)]

    def MM(out, lhsT, rhs, start, stop, reads, writes):
        em.op("pe", lambda h: h.matmul(out, lhsT=lhsT, rhs=rhs, start=start, stop=stop), reads, writes)

    def TR(out, in_, ident, reads, writes):
        em.op("pe", lambda h: h.transpose(out, in_, ident), reads, writes)

    def ACT(out, in_, func, reads, writes, bias=0.0, scale=1.0, accum_out=None):
        if accum_out is None:
            em.op("act", lambda h: h.activation(out=out, in_=in_, func=func, bias=bias, scale=scale), reads, writes)
        else:
            em.op("act", lambda h: h.activation(out=out, in_=in_, func=func, bias=bias, scale=scale,
                                                accum_out=accum_out), reads, writes)

    def ACTMUL(out, in_, mul, reads, writes):
        em.op("act", lambda h: h.mul(out=out, in_=in_, mul=mul), reads, writes)

    def ACTCOPY(out, in_, reads, writes):
        em.op("act", lambda h: h.copy(out=out, in_=in_), reads, writes)

    def TT(eng, out, in0, in1, op, reads, writes):
        em.op(eng, lambda h: h.tensor_tensor(out=out, in0=in0, in1=in1, op=op), reads, writes)

    def TS(eng, out, in0, s1, s2, op0, op1, reads, writes):
        if op1 is None:
            em.op(eng, lambda h: h.tensor_scalar(out=out, in0=in0, scalar1=s1, scalar2=None, op0=op0), reads, writes)
        else:
            em.op(eng, lambda h: h.tensor_scalar(out=out, in0=in0, scalar1=s1, scalar2=s2, op0=op0, op1=op1),
                  reads, writes)

    def STT(eng, out, in0, scalar, in1, op0, op1, reads, writes):
        em.op(eng, lambda h: h.scalar_tensor_tensor(out=out, in0=in0, scalar=scalar, in1=in1, op0=op0, op1=op1),
              reads, writes)

    def CP(eng, out, in_, reads, writes):
        em.op(eng, lambda h: h.tensor_copy(out=out, in_=in_), reads, writes)

    def MEMSET(eng, ap, val, writes):
        em.op(eng, lambda h: h.memset(ap, val), (), writes)

    def RECIP(out, in_, reads, writes):
        em.op("dve", lambda h: h.reciprocal(out=out, in_=in_), reads, writes)

    def DMA(q, out, in_, ds, reads, writes):
        em.dma([(q, out, in_)], ds, reads, writes)

    cb = buf("consts")
    DMA("pool", cmask[:], cmask_d, miscp_ds, [], [cb])
    DMA("sp", identf[:], ident_d, misc_ds, [], [cb])
    DMA("pool", identb[:], ident_d, miscp_ds, [], [cb])
    DMA("pool", esel[:], esel_d, miscp_ds, [], [cb])
    DMA("sp", pastc[:], pastc_d.to_broadcast([128, 256]), misc_ds, [], [cb])
    DMA("sp", invf[:], invf_d, misc_ds, [], [cb])
    DMA("sp", ncols[:], ncols_d.rearrange("l p c -> p l c"), misc_ds, [], [cb])
    MEMSET("pool", ones[:], 1.0, [cb])
    MEMSET("pool", ones_lo[:], 0.0, [cb])
    MEMSET("pool", ones_hi[:], 0.0, [cb])
    em.fence()
    MEMSET("pool", ones_lo[:, 0:64], 1.0, [cb])
    MEMSET("pool", ones_hi[:, 64:128], 1.0, [cb])
    mA = cmask[:, 0:2432]
    mB = cmask[:, 2432:2944]
    mC = cmask[:, 2944:3840]
    em.fence()

    wring = [(wslots[i], buf("w%d" % i), em.new_dsem()) for i in range(3)]
    wq = []
    wstate = dict(loaded=0, used=0)
    loaded_items = []

    def w_issue():
        i = wstate["loaded"]
        spec = wq[i]
        ap, bf, ds = wring[i % 3]
        if spec["zero"]:
            MEMSET("pool", ap[:, :, :], 0.0, [bf])
        parts = []
        for (lo, hi, src) in spec["segs"]:
            parts.append(("pool", ap[:, :, lo:hi], src.rearrange("(c p) n -> p c n", p=128)))
        em.dma(parts, ds, [], [bf])
        wstate["loaded"] += 1

    def w_get(tag):
        i = wstate["used"]
        assert wq[i]["tag"] == tag, (wq[i]["tag"], tag)
        while wstate["loaded"] < min(len(wq), i + 3):
            w_issue()
        wstate["used"] += 1
        ap, bf, ds = wring[i % 3]
        return ap, bf

    def spec(tag, segs, zero=False):
        return dict(tag=tag, segs=segs, zero=zero)

    def layer_specs(l):
        par = l % 2
        li = l // 2
        win = w_in_d[par][li]
        wm = w_mem_d[par][li]
        sp_ = []
        for h in range(2):
            sp_.append(spec(("mk", l, h), [(0, 128, wm[:, h * 128:(h + 1) * 128])]))
            sp_.append(spec(("mv", l, h), [(0, 128, wm[:, 256 + h * 128:256 + (h + 1) * 128])]))
        if par == 0:
            E = EVEN_OFF
            for c in EVEN_PROC:
                sp_.append(spec(("g", l, c), [(0, 128, win[:, E["gate"] + c * 128:E["gate"] + (c + 1) * 128])]))
                if c < 3:
                    names = ("qa", "ka", "va")
                    hh = c
                elif c < 6:
                    names = ("qb", "kb", "vb")
                    hh = c - 3
                else:
                    names = ("qm",)
                    hh = c - 6
                for nm in names:
                    sp_.append(spec((nm, l, c), [(0, 128, win[:, E[nm] + hh * 128:E[nm] + (hh + 1) * 128])]))
        else:
            O = ODD_OFF
            for h in range(2):
                c = 6 + h
                sp_.append(spec(("g", l, c), [(0, 128, win[:, O["gate"] + c * 128:O["gate"] + (c + 1) * 128])]))
                sp_.append(spec(("qm", l, c), [(0, 128, win[:, O["qm"] + h * 128:O["qm"] + (h + 1) * 128])]))
            for g, chunks in ((0, (0, 1, 2, 3)), (1, (4, 5))):
                k0 = O["kc"] + g * 64
                v0 = O["vc"] + g * 64
                sp_.append(spec(("k0", l, g), [(0, 32, win[:, k0:k0 + 32]), (64, 96, win[:, k0 + 32:k0 + 64])], True))
                sp_.append(spec(("k1", l, g), [(32, 64, win[:, k0:k0 + 32]), (96, 128, win[:, k0 + 32:k0 + 64])], True))
                sp_.append(spec(("v0", l, g), [(0, 64, win[:, v0:v0 + 64])], True))
                sp_.append(spec(("v1", l, g), [(64, 128, win[:, v0:v0 + 64])], True))
                for c in chunks:
                    sp_.append(spec(("g", l, c), [(0, 128, win[:, O["gate"] + c * 128:O["gate"] + (c + 1) * 128])]))
                    q0 = O["qc"] + c * 128
                    sp_.append(spec(("qc", l, c), [(0, 32, win[:, q0:q0 + 32]), (64, 96, win[:, q0 + 32:q0 + 64]),
                                                   (32, 64, win[:, q0 + 64:q0 + 96]),
                                                   (96, 128, win[:, q0 + 96:q0 + 128])]))
        return sp_

    for l in range(n_layers):
        wq.extend(layer_specs(l))

    hTb = buf("R1")
    yTb = [buf("yT%d" % c) for c in range(16)]
    sm = buf("small")

    def norm_phase(src_rows, ntiles, gidx, dst, dst_buf, src_bufs):
        ngroups = (ntiles + NG - 1) // NG
        st = {}

        def stage_a(g):
            nt = min(NG, ntiles - g * NG)
            xg, xb, xds = XST.get()
            em.dma([("sp", xg[:, 0:nt, :], src_rows(g * NG, nt).rearrange("(j p) d -> p j d", p=128))], xds,
                   [src_bufs[g * NG + j] for j in range(nt)], [xb])
            sg = buf("nstat%d" % g)
            for j in range(nt):
                t = g * NG + j
                ACT(junk, xg[:, j, :], AF.Square, [xb], [buf("junk"), sg], accum_out=small[:, t:t + 1])
            ACT(small[:, 16 + g * NG:16 + g * NG + nt], small[:, g * NG:g * NG + nt], AF.Sqrt, [sg], [sg], bias=EPS,
                scale=1.0 / DM)
            RECIP(small[:, 16 + g * NG:16 + g * NG + nt], small[:, 16 + g * NG:16 + g * NG + nt], [sg], [sg])
            for j in range(nt):
                t = g * NG + j
                TS("dve", xg[:, j, :], xg[:, j, :], small[:, 16 + t:17 + t], None, ALU.mult, None, [sg, xb], [xb])
            st[g] = (xg, xb, nt)

        def stage_b(g):
            xg, xb, nt = st.pop(g)
            for c in range(16):
                ps, pb = PN.get()
                for j in range(nt):
                    TR(ps[:, j * 128:(j + 1) * 128], xg[:, j, c * 128:(c + 1) * 128], identf[:], [xb, cb], [pb])
                dcb = buf("%s_c%d" % (dst_buf.name, c))
                if c % 2 == 0:
                    ACTMUL(dst[:, c, g * NG * 128:g * NG * 128 + nt * 128], ps[:, 0:nt * 128],
                           ncols[:, gidx, c:c + 1], [pb, cb], [dcb])
                else:
                    TS("dve", dst[:, c, g * NG * 128:g * NG * 128 + nt * 128], ps[:, 0:nt * 128],
                       ncols[:, gidx, c:c + 1], None, ALU.mult, None, [pb, cb], [dcb])

        stage_a(0)
        for g in range(ngroups):
            if g + 1 < ngroups:
                stage_a(g + 1)
            stage_b(g)

    tb = buf("tables")

    def rope_tables(col):
        pb_ = buf("ropetmp")
        T0, T1, T2, T3 = r3f(8, 10), r3f(10, 12), r3f(12, 14), r3f(14, 16)
        for ch in range(4):
            cs = slice(ch * 512, (ch + 1) * 512)
            DMA("sp", T0.bitcast(I32), pos_in[:, cs].to_broadcast([128, 512]), rope_ds, [], [pb_])
            CP("dve", T1, T0.bitcast(I32), [pb_], [pb_])
            TS("dve", T1, T1, invf[:, col:col + 1], None, ALU.mult, None, [pb_, cb], [pb_])
            TS("dve", T2, T1, 1.0 / TWO_PI, None, ALU.mult, None, [pb_], [pb_])
            CP("dve", T3.bitcast(I32), T2, [pb_], [pb_])
            CP("dve", T2, T3.bitcast(I32), [pb_], [pb_])
            STT("dve", T1, T2, -CW1, T1, ALU.mult, ALU.add, [pb_], [pb_])
            STT("dve", T1, T2, -CW2, T1, ALU.mult, ALU.add, [pb_], [pb_])
            TS("dve", T2, T1, math.pi, -TWO_PI, ALU.is_gt, ALU.mult, [pb_], [pb_])
            TT("dve", T3, T1, T2, ALU.add, [pb_], [pb_])
            TS("dve", T2, T3, -math.pi, TWO_PI, ALU.is_lt, ALU.mult, [pb_], [pb_])
            TT("dve", T3, T3, T2, ALU.add, [pb_], [pb_])
            ACT(sinS[0:64, cs], T3[0:64, :], AF.Sin, [pb_], [tb])
            ACT(sinS[64:128, cs], T3[64:128, :], AF.Sin, [pb_], [tb], scale=-1.0)
            TS("dve", T0, T1, math.pi / 2, None, ALU.add, None, [pb_], [pb_])
            TS("dve", T2, T0, math.pi, -TWO_PI, ALU.is_gt, ALU.mult, [pb_], [pb_])
            TT("dve", T0, T0, T2, ALU.add, [pb_], [pb_])
            ACT(cosT[:, cs], T0, AF.Sin, [pb_], [tb])

    def proj_fm(tag, evac):
        wp, wb = w_get(tag)
        for tt in range(4):
            ps, pb = PA.get()
            for kc in range(16):
                MM(ps[:, :], wp[:, kc, :], hT[:, kc, tt * 512:(tt + 1) * 512], kc == 0, kc == 15, [wb, hTb], [pb])
            evac(ps, pb, tt)

    def rope_evac(dst, dst_buf):
        def f(ps, pb, tt):
            sl = slice(tt * 512, (tt + 1) * 512)
            TT("dve", t1, ps[:, :], cosT[:, sl], ALU.mult, [pb, tb], [buf("t1")])
            TT("dve", t2[0:64, :], ps[64:128, :], sinS[64:128, sl], ALU.mult, [pb, tb], [buf("t2")])
            TT("dve", t2[64:128, :], ps[0:64, :], sinS[0:64, sl], ALU.mult, [pb, tb], [buf("t2")])
            TT("pool", dst[:, sl], t1, t2, ALU.add, [buf("t1"), buf("t2")], [dst_buf])
        return f

    def copy_evac(dst, dst_buf):
        def f(ps, pb, tt):
            ACTCOPY(dst[:, tt * 512:(tt + 1) * 512], ps[:, :], [pb], [dst_buf])
        return f

    def silu_evac(ps, pb, tt):
        ACT(gT[:, tt * 512:(tt + 1) * 512], ps[:, :], AF.Silu, [pb], [buf("gT")])

    def v_evac(Vt, vbuf):
        def f(ps, pb, tt):
            ACTCOPY(vTtmp, ps[:, :], [pb], [buf("vTtmp")])
            p2, p2b = PA.get()
            p2v = p2[:, :].bitcast(BF16)
            for j in range(4):
                TR(p2v[:, j * 128:(j + 1) * 128], vTtmp[:, j * 128:(j + 1) * 128], identb[:], [buf("vTtmp"), cb], [p2b])
            CP("dve", Vt[:, tt * 4:(tt + 1) * 4, :], p2v[:, 0:512].rearrange("p (t d) -> p t d", t=4), [p2b], [vbuf])
        return f

    def attn_head(ranges):
        flat = [(ri, bj) for ri, r in enumerate(ranges) for bj in range(len(r[0]))]
        Ps = {}
        acc = {}

        def emit_qk(g):
            ri, bj = flat[g]
            blocks, Wq, Ws, scale, finish = ranges[ri]
            b = blocks[bj]
            Sx, Sb = PS.get()
            for (lhsT, rhs, lo, hi) in b["qk"]:
                MM(Sx[:, lo:hi], lhsT, rhs, True, b.get("bias") is None, b["reads"], [Sb])
                if b.get("bias") is not None:
                    bl, br = b["bias"]
                    MM(Sx[:, lo:hi], bl, br, False, True, [cb, buf("biasT")], [Sb])
            P, Pb = PP.get()
            ACT(P[:, 0:Ws], Sx[:, 0:Ws], AF.Exp, [Sb], [Pb], scale=scale)
            for (lo, hi, m) in b.get("masks", ()):
                TT("dve", P[:, lo:hi], P[:, lo:hi], m, ALU.mult, [Pb, cb], [Pb])
            Ps[g] = (P, Pb)

        def emit_pv(g):
            ri, bj = flat[g]
            blocks, Wq, Ws, scale, finish = ranges[ri]
            nb = len(blocks)
            b = blocks[bj]
            if bj == 0:
                Dn, Db = POD.get()
                O, Ob = POD.get()
                acc[ri] = (O, Ob, Dn, Db)
            O, Ob, Dn, Db = acc[ri]
            P, Pb = Ps.pop(g)
            n = len(b["pv"])
            for idx, (vl, ol, lo, hi) in enumerate(b["pv"]):
                first = (bj == 0 and idx == 0)
                last = (bj == nb - 1 and idx == n - 1)
                MM(O[:, 0:Wq], vl, P[:, lo:hi], first, last, [Pb] + b["vreads"], [Ob])
                MM(Dn[:, 0:Wq], ol, P[:, lo:hi], first, last, [Pb, cb], [Db])
            if bj == nb - 1:
                finish(O, Ob, Dn, Db)

        ng = len(flat)
        emit_qk(0)
        if ng > 1:
            emit_qk(1)
        for g in range(ng):
            if g + 2 < ng:
                emit_qk(g + 2)
            emit_pv(g)

    def finish_std(c, q0, Wq, sink_col=None):
        def f(O, Ob, Dn, Db):
            rb, yb = buf("t2"), buf("t1")
            if sink_col is None:
                ACT(rD[:, 0:Wq], Dn[:, 0:Wq], AF.Ln, [Db], [rb])
            else:
                ACT(rD[:, 0:Wq], Dn[:, 0:Wq], AF.Ln, [Db, cb], [rb], bias=sink_col)
            ACT(rD[:, 0:Wq], rD[:, 0:Wq], AF.Exp, [rb], [rb], scale=-1.0)
            TT("dve", ytmp[:, 0:Wq], O[:, 0:Wq], rD[:, 0:Wq], ALU.mult, [Ob, rb], [yb])
            TT("pool", yT[:, c, q0:q0 + Wq], ytmp[:, 0:Wq], gT[:, q0:q0 + Wq], ALU.mult, [yb, buf("gT")], [yTb[c]])
        return f

    SC128 = 128 ** -0.5
    SC64 = 64 ** -0.5
    qb_, kb0_, kb1_, vb0_, vb1_ = buf("qT"), buf("kT0"), buf("kT1"), buf("V0"), buf("V1")

    def mem_head(l, c, h, qtag):
        proj_fm(("g", l, c), silu_evac)
        proj_fm(qtag, copy_evac(qT, qb_))
        rngs = []
        for r in range(4):
            q0 = r * 512
            blocks = []
            for kb in range(2):
                blocks.append(dict(qk=[(mkT[:, h, kb * 128:(kb + 1) * 128], qT[:, q0:q0 + 512], 0, 512)],
                                   reads=[buf("mk"), qb_], pv=[(mvT[:, kb, h, :], ones[:], 0, 512)],
                                   vreads=[buf("mk")]))
            rngs.append((blocks, 512, 512, SC128, finish_std(c, q0, 512)))
        attn_head(rngs)

    def phase_M(l):
        DMA("sp", r3b(16, 24), memn_d, memn_ds, [buf("memn_d")], [buf("memnT")])
        mb = buf("mk")
        for h in range(2):
            wp, wb = w_get(("mk", l, h))
            ps, pb = PA.get()
            for kc in range(16):
                MM(ps[:, 0:256], wp[:, kc, :], memnT[:, kc, :], kc == 0, kc == 15, [wb, buf("memnT")], [pb])
            ACTCOPY(mkT[:, h, :], ps[:, 0:256], [pb], [mb])
            wp, wb = w_get(("mv", l, h))
            ps, pb = PA.get()
            for tl in range(2):
                for kc in range(16):
                    MM(ps[:, tl * 128:(tl + 1) * 128], memnT[:, kc, tl * 128:(tl + 1) * 128], wp[:, kc, :], kc == 0,
                       kc == 15, [wb, buf("memnT")], [pb])
            CP("dve", mvT[:, :, h, :], ps[:, 0:256].rearrange("p (t d) -> p t d", t=2), [pb], [mb])

    def even_layer(l):
        done = set()
        for c in EVEN_PROC:
            if c < 3:
                proj_fm(("g", l, c), silu_evac)
                proj_fm(("qa", l, c), rope_evac(qT, qb_))
                proj_fm(("ka", l, c), rope_evac(kT0, kb0_))
                proj_fm(("va", l, c), v_evac(V0, vb0_))
                rngs = []
                for r in range(4):
                    q0 = r * 512
                    blocks = []
                    for kb in range(4 * r + 4):
                        delta = q0 - 128 * kb
                        blocks.append(dict(qk=[(kT0[:, kb * 128:(kb + 1) * 128], qT[:, q0:q0 + 512], 0, 512)],
                                           reads=[kb0_, qb_], masks=[(0, 512, mA[:, delta + 384:delta + 384 + 512])],
                                           pv=[(V0[:, kb, :], ones[:], 0, 512)], vreads=[vb0_]))
                    rngs.append((blocks, 512, 512, SC128, finish_std(c, q0, 512)))
                attn_head(rngs)
            elif c < 6:
                proj_fm(("g", l, c), silu_evac)
                proj_fm(("qb", l, c), rope_evac(qT, qb_))
                proj_fm(("kb", l, c), rope_evac(kT0, kb0_))
                proj_fm(("vb", l, c), v_evac(V0, vb0_))
                if c == EVEN_PROC[-1]:
                    prefetch_wout(l)
                km = small[:, 32:40]
                kmt = small[:, 40:48]
                kmh = smallb[:, 0:8]
                kml = smallb[:, 8:16]
                gm = small[:, 64:192]
                mx = small[:, 192:320]
                bs = small[:, 320:448]
                bsb = smallb[:, 16:144]
                em.op("dve", lambda h: h.reduce_sum(out=km, in_=kT0.rearrange("p (b n) -> p b n", b=8), axis=AX.X),
                      [kb0_], [sm])
                TS("dve", km, km, 1.0 / 256.0, None, ALU.mult, None, [sm], [sm])
                CP("dve", kmh, km, [sm], [sm])
                TT("dve", kmt, km, kmh, ALU.subtract, [sm], [sm])
                CP("dve", kml, kmt, [sm], [sm])
                gp, gpb = PA.get()
                for t in range(16):
                    MM(gp[:, t * 8:(t + 1) * 8], qT[:, t * 128:(t + 1) * 128], kmh, True, False, [qb_, sm], [gpb])
                    MM(gp[:, t * 8:(t + 1) * 8], qT[:, t * 128:(t + 1) * 128], kml, False, True, [qb_, sm], [gpb])
                TT("dve", gm, gp[:, 0:128], pastc[:, 0:128], ALU.add, [gpb, cb], [sm])
                for t in range(16):
                    em.op("dve", (lambda t_: (lambda h: h.max(out=mx[:, t_ * 8:(t_ + 1) * 8],
                                                             in_=gm[:, t_ * 8:(t_ + 1) * 8])))(t), [sm], [sm])
                for t in range(16):
                    TS("dve", bs[:, t * 8:(t + 1) * 8], gm[:, t * 8:(t + 1) * 8], mx[:, t * 8 + 3:t * 8 + 4], None,
                       ALU.is_ge, None, [sm], [sm])
                STT("dve", bsb, bs, 30000.0, pastc[:, 128:256], ALU.mult, ALU.add, [sm, cb], [sm])
                for g4 in range(4):
                    p2, p2b = PA.get()
                    p2v = p2[:, :].bitcast(BF16)
                    for j in range(4):
                        t = g4 * 4 + j
                        TR(p2v[0:8, j * 128:(j + 1) * 128], bsb[:, t * 8:(t + 1) * 8], identb[:], [sm, cb], [p2b])
                    CP("dve", biasT[0:8, g4 * 512:(g4 + 1) * 512], p2v[0:8, 0:512], [p2b], [buf("biasT")])
                rngs = []
                for r in range(4):
                    q0 = r * 512
                    blocks = []
                    for kb in range(4 * r + 4):
                        b_ = dict(qk=[(kT0[:, kb * 128:(kb + 1) * 128], qT[:, q0:q0 + 512], 0, 512)],
                                  reads=[kb0_, qb_],
                                  bias=(esel[0:8, (kb // 2) * 128:(kb // 2 + 1) * 128], biasT[0:8, q0:q0 + 512]),
                                  pv=[(V0[:, kb, :], ones[:], 0, 512)], vreads=[vb0_])
                        if kb >= 4 * r:
                            delta = q0 - 128 * kb
                            b_["masks"] = [(0, 512, mC[:, delta + 384:delta + 384 + 512])]
                        blocks.append(b_)
                    rngs.append((blocks, 512, 512, SC128, finish_std(c, q0, 512)))
                attn_head(rngs)
            else:
                mem_head(l, c, c - 6, ("qm", l, c))
            done.add(c)
            if (c ^ 1) in done:
                exchange_part(l, c // 2)

    def odd_layer(l):
        li = l // 2
        DMA("sp", esink[0:64, :], sinks_d[li, 0:1, :].to_broadcast([64, 6]), misc_ds, [], [buf("esink")])
        DMA("sp", esink[64:128, :], sinks_d[li, 1:2, :].to_broadcast([64, 6]), misc_ds, [], [buf("esink")])
        ACT(esink[:, :], esink[:, :], AF.Exp, [buf("esink")], [cb])
        for h in range(2):
            mem_head(l, 6 + h, h, ("qm", l, 6 + h))
        exchange_part(l, 3)
        for g, chunks in ((0, (0, 1, 2, 3)), (1, (4, 5))):
            proj_fm(("k0", l, g), rope_evac(kT0, kb0_))
            proj_fm(("k1", l, g), rope_evac(kT1, kb1_))
            proj_fm(("v0", l, g), v_evac(V0, vb0_))
            proj_fm(("v1", l, g), v_evac(V1, vb1_))
            for c in chunks:
                proj_fm(("g", l, c), silu_evac)
                proj_fm(("qc", l, c), rope_evac(qT, qb_))
                if c == 5:
                    prefetch_wout(l)
                rngs = []
                for i in range(8):
                    q0 = i * 256
                    blocks = []
                    for kb in (2 * i - 1, 2 * i, 2 * i + 1):
                        if kb < 0:
                            continue
                        delta = q0 - 128 * kb
                        m = mB[:, delta + 128:delta + 128 + 256]
                        ks = slice(kb * 128, (kb + 1) * 128)
                        blocks.append(dict(qk=[(kT0[:, ks], qT[:, q0:q0 + 256], 0, 256),
                                               (kT1[:, ks], qT[:, q0:q0 + 256], 256, 512)],
                                           reads=[kb0_, kb1_, qb_], masks=[(0, 256, m), (256, 512, m)],
                                           pv=[(V0[:, kb, :], ones_lo[:], 0, 256), (V1[:, kb, :], ones_hi[:], 256, 512)],
                                           vreads=[vb0_, vb1_]))
                    rngs.append((blocks, 256, 512, SC64, finish_std(c, q0, 256, sink_col=esink[:, c:c + 1])))
                attn_head(rngs)
                if c % 2 == 1:
                    exchange_part(l, c // 2)

    def exchange_part(l, j):
        bb, gb = buf("ybounce%d_%d" % (l, j)), buf("ygath%d_%d" % (l, j))
        DMA("sp", ybounce_d[l][j].rearrange("(c p) n -> p c n", p=128), yT[:, 2 * j:2 * j + 2, :], gout_ds[j],
            yTb[2 * j:2 * j + 2], [bb])
        em.cc(ybounce_d[l][j], ygath_d[l][j], cc_ds[l][j], [bb], [gb])
        for r in range(2):
            DMA("sp", yT[:, r * 8 + 2 * j:r * 8 + 2 * j + 2, :],
                ygath_d[l][j][r * 256:(r + 1) * 256, :].rearrange("(c p) n -> p c n", p=128), gin_ds[j][r], [gb],
                yTb[r * 8 + 2 * j:r * 8 + 2 * j + 2])

    def prefetch_wout(l):
        par, li = l % 2, l // 2
        wo = w_out_d[par][li]
        for n in range(4):
            em.dma([("pool", wout[:, :, n * 512:(n + 1) * 512],
                     wo[:, n * 512:(n + 1) * 512].rearrange("(c p) n -> p c n", p=128))], wout_ds[n], [],
                   [hTb, buf("wout%d" % n)] if n == 0 else [buf("wout%d" % n)])

    def phase_O(l, ns=(0, 1, 2, 3)):
        par, li = l % 2, l // 2
        src = x_in if l == 0 else xs_d
        for n in ns:
            wb = buf("wout%d" % n)
            for t in range(16):
                xp, xb, ds_in, ds_out = XP.get()
                rows = slice(t * 128, (t + 1) * 128)
                cols = slice(n * 512, (n + 1) * 512)
                xsb = buf("xs%d_%d" % (t, n))
                DMA("sp", xp, src[rows, cols], ds_in, [xsb], [xb])
                ps, pb = PA.get()
                for kc in range(16):
                    MM(ps[:, :], yT[:, kc, rows], wout[:, kc, cols], kc == 0, kc == 15, [yTb[kc], wb], [pb])
                TT("dve", xp, ps[:, :], xp, ALU.add, [pb, xb], [xb])
                DMA("sp", xs_d[rows, cols], xp, ds_out, [xb], [xsb, xs_bufs[t]])

    def phase_F():
        DMA("sp", gfin, fin_d.to_broadcast([128, DM]), misc_ds, [], [buf("gfin")])
        for t in range(16):
            xr, xb, ds_in, ds_out = XR.get()
            rows = slice(t * 128, (t + 1) * 128)
            DMA("sp", xr, xs_d[rows, :], ds_in, [xs_bufs[t]], [xb])
            sg = buf("fstat%d" % t)
            ACT(junkO, xr, AF.Square, [xb], [buf("junkO"), sg], accum_out=small[:, t:t + 1])
            ACT(small[:, 16 + t:17 + t], small[:, t:t + 1], AF.Sqrt, [sg], [sg], bias=EPS, scale=1.0 / DM)
            RECIP(small[:, 16 + t:17 + t], small[:, 16 + t:17 + t], [sg], [sg])
            STT("dve", xr, xr, small[:, 16 + t:17 + t], gfin, ALU.mult, ALU.mult, [sg, xb, buf("gfin")], [xb])
            DMA("sp", out_d[rows, :], xr, ds_out, [xb], [out_buf])

    memb = buf("memn_sb")
    norm_phase(lambda t0, n: mem_in[t0 * 128:(t0 + n) * 128, :], 2, 4, memnT, memb, [buf("memsrc")] * 2)
    em.fence()
    DMA("sp", memn_d, r3b(16, 24), misc_ds, [memb], [buf("memn_d")])
    em.fence()

    xin_bufs = [buf("xin")] * 16
    rope_tables(0)
    phase_M(0)
    em.fence()
    for l in range(n_layers):
        par = l % 2
        src = x_in if l == 0 else xs_d
        norm_phase(lambda t0, n, s=src: s[t0 * 128:(t0 + n) * 128, :], 16, l, hT, hTb,
                   xin_bufs if l == 0 else xs_bufs)
        em.fence()
        if par == 0:
            even_layer(l)
        else:
            odd_layer(l)
        em.fence(exclude=wout_ds + [d for j in range(4) for d in gin_ds[j]] + [d for dl in cc_ds for d in dl]
                 + gout_ds)
        phase_O(l, (0, 1, 2))
        if l + 1 < n_layers:
            rope_tables((l + 1) % 2)
            phase_M(l + 1)
        phase_O(l, (3,))
        em.fence()
    phase_F()
    em.fence()
    em.replay(nc)
    es.close()
    return nc


_CACHE = {}


def _slice_even(w, r):
    ids = [3 * r + j for j in range(3)]
    cols = []
    for base in (0, 768, 1536, 2304, 3072, 3840):
        for h in ids:
            cols.append(np.arange(base + h * 128, base + (h + 1) * 128))
    for h in (2 * r, 2 * r + 1):
        cols.append(np.arange(4608 + h * 128, 4608 + (h + 1) * 128))
    for gc in EVEN_ORDER[8 * r:8 * r + 8]:
        cols.append(np.arange(5120 + gc * 128, 5120 + (gc + 1) * 128))
    return np.ascontiguousarray(w[:, :, np.concatenate(cols)])


def _slice_odd(w, r):
    chunks = ODD_ORDER[8 * r:8 * r + 8]
    groups = [0, 1] if r == 0 else [2, 1]
    cols = []
    for gc in chunks[:6]:
        cols.append(np.arange(gc * 128, (gc + 1) * 128))
    for g in groups:
        cols.append(np.arange(1536 + g * 64, 1536 + (g + 1) * 64))
    for g in groups:
        cols.append(np.arange(1728 + g * 64, 1728 + (g + 1) * 64))
    for h in (2 * r, 2 * r + 1):
        cols.append(np.arange(1920 + h * 128, 1920 + (h + 1) * 128))
    for gc in chunks:
        cols.append(np.arange(2432 + gc * 128, 2432 + (gc + 1) * 128))
    return np.ascontiguousarray(w[:, :, np.concatenate(cols)])


def _slice_mem(w, r):
    cols = []
    for base in (0, 512):
        for h in (2 * r, 2 * r + 1):
            cols.append(np.arange(base + h * 128, base + (h + 1) * 128))
    return np.ascontiguousarray(w[:, :, np.concatenate(cols)])


def _perm_rows(w, order):
    rows = np.concatenate([np.arange(gc * 128, (gc + 1) * 128) for gc in order])
    return np.ascontiguousarray(w[:, rows, :])


def kernel(x, mem, positions, even_norm, even_w_in, even_w_mem_kv, even_w_out,
           odd_norm, odd_w_in, odd_w_mem_kv, odd_w_out, odd_sinks, mem_norm, final_norm, _n_layers=4):
    f32 = np.float32
    x = np.asarray(x, f32)
    mem = np.asarray(mem, f32)
    positions = np.asarray(positions, np.int32)
    even_w_in = np.asarray(even_w_in, f32)
    odd_w_in = np.asarray(odd_w_in, f32)
    even_w_mem_kv = np.asarray(even_w_mem_kv, f32)
    odd_w_mem_kv = np.asarray(odd_w_mem_kv, f32)
    consts = host_consts()
    norms = [np.asarray(even_norm, f32)[0], np.asarray(odd_norm, f32)[0], np.asarray(even_norm, f32)[1],
             np.asarray(odd_norm, f32)[1], np.asarray(mem_norm, f32)]
    ncols = np.stack([n.reshape(16, 128).T for n in norms]).astype(f32)
    sk = np.asarray(odd_sinks, f32)
    common = dict(even_w_out=_perm_rows(np.asarray(even_w_out, f32), EVEN_ORDER),
                  odd_w_out=_perm_rows(np.asarray(odd_w_out, f32), ODD_ORDER),
                  ncols=ncols, final_norm=np.asarray(final_norm, f32).reshape(1, DM), **consts)
    role = []
    for r in range(2):
        chunks = ODD_ORDER[8 * r:8 * r + 6]
        sinks2 = np.stack([sk[:, [2 * gc for gc in chunks]], sk[:, [2 * gc + 1 for gc in chunks]]], axis=1)
        role.append(dict(even_w_in=_slice_even(even_w_in, r), odd_w_in=_slice_odd(odd_w_in, r),
                         even_w_mem_kv=_slice_mem(even_w_mem_kv, r), odd_w_mem_kv=_slice_mem(odd_w_mem_kv, r),
                         sinks2=np.ascontiguousarray(sinks2.astype(f32))))
    if _n_layers not in _CACHE:
        _CACHE[_n_layers] = build_program(_n_layers)
    nc = _CACHE[_n_layers]
    in_maps = []
    for core in range(8):
        b, r = core // 2, core % 2
        m = dict(common)
        m.update(role[r])
        m["x"] = np.ascontiguousarray(x[b])
        m["mem"] = np.ascontiguousarray(mem[b])
        m["pos"] = np.ascontiguousarray(positions[b].reshape(1, S))
        in_maps.append(m)
    res = run_bass_kernel_spmd(nc, in_maps, core_ids=list(range(8)))
    out = np.stack([res.results[2 * b]["out"] for b in range(4)]).astype(f32)
    return out
```

```python
import math
from contextlib import ExitStack
import numpy as np
import concourse.bass as bass
import concourse.mybir as mybir
from concourse.bass_utils import run_bass_kernel_spmd

F32 = mybir.dt.float32
BF16 = mybir.dt.bfloat16
I32 = mybir.dt.int32
AF = mybir.ActivationFunctionType
ALU = mybir.AluOpType
AX = mybir.AxisListType

S = 2048
DM = 2048
EPS = 1e-6
EVEN_OFF = dict(qa=0, ka=384, va=768, qb=1152, kb=1536, vb=1920, qm=2304, gate=2560)
ODD_OFF = dict(qc=0, kc=768, vc=896, qm=1024, gate=1280)
EVEN_ORDER = [0, 1, 2, 6, 7, 8, 12, 13, 3, 4, 5, 9, 10, 11, 14, 15]
ODD_ORDER = [0, 1, 2, 3, 4, 5, 12, 13, 8, 9, 10, 11, 6, 7, 14, 15]
PAIRS = [[0, 1], [2, 3], [4, 5], [6, 7]]
EVEN_PROC = [6, 7, 0, 1, 2, 3, 4, 5]
TWO_PI = 2.0 * math.pi
CW1 = 6.28125
CW2 = TWO_PI - CW1


class Buf:
    __slots__ = ("name", "w", "r")

    def __init__(self, name=""):
        self.name = name
        self.w = None
        self.r = {}


class DSem:
    def __init__(self, key):
        self.key = key
        self.count = 0


class Eng:
    def __init__(self, name):
        self.name = name
        self.ops = []
        self.count = 0
        self.waited = {}
        self.key = "E_" + name


class Emitter:
    def __init__(self):
        self.engs = {n: Eng(n) for n in ("pe", "act", "dve", "pool", "sp")}
        self.dsems = []

    def new_dsem(self):
        d = DSem("D_%d" % len(self.dsems))
        self.dsems.append(d)
        return d

    def _deps(self, eng, reads, writes, is_dma):
        deps = {}

        def add(semkey, val, ename, kind):
            if (not is_dma) and ename == eng.name and kind != "raw":
                return
            if eng.waited.get(semkey, 0) >= val:
                return
            if deps.get(semkey, 0) < val:
                deps[semkey] = val

        for b in reads:
            if b.w is not None:
                add(b.w[0], b.w[1], b.w[2], "raw")
        for b in writes:
            if b.w is not None:
                add(b.w[0], b.w[1], b.w[2], "waw")
            for k, (v, en) in b.r.items():
                add(k, v, en, "war")
        return deps

    def _wait(self, eng, deps):
        for k, v in deps.items():
            eng.ops.append(("wait", k, v))
            eng.waited[k] = v

    def _commit(self, tok, reads, writes):
        k, v, en = tok
        for b in reads:
            b.r[k] = (v, en)
        for b in writes:
            b.w = tok
            b.r = {}

    def op(self, engname, fn, reads=(), writes=()):
        eng = self.engs[engname]
        self._wait(eng, self._deps(eng, reads, writes, False))
        eng.count += 1
        eng.ops.append(("op", fn))
        self._commit((eng.key, eng.count, eng.name), reads, writes)

    def dma(self, parts, dsem, reads=(), writes=()):
        for (q, o, i) in parts:
            eng = self.engs[q]
            self._wait(eng, self._deps(eng, reads, writes, True))
            eng.ops.append(("dma", o, i, dsem.key))
            dsem.count += 16
        self._commit((dsem.key, dsem.count, "dma"), reads, writes)

    def cc(self, ins, outs, dsem, reads=(), writes=()):
        eng = self.engs["pool"]
        self._wait(eng, self._deps(eng, reads, writes, True))
        eng.ops.append(("cc", ins, outs, dsem.key))
        dsem.count += 1
        self._commit((dsem.key, dsem.count, "dma"), reads, writes)

    def fence(self, exclude=()):
        ex = set(d.key for d in exclude)
        for e in self.engs.values():
            deps = {}
            for e2 in self.engs.values():
                if e2 is not e and e2.count > e.waited.get(e2.key, 0):
                    deps[e2.key] = e2.count
            for d in self.dsems:
                if d.key not in ex and d.count > e.waited.get(d.key, 0):
                    deps[d.key] = d.count
            self._wait(e, deps)

    def replay(self, nc):
        with ExitStack() as es:
            sems = {}
            for e in self.engs.values():
                sems[e.key] = es.enter_context(nc.semaphore("s_" + e.name))
            for d in self.dsems:
                sems[d.key] = es.enter_context(nc.semaphore("s_" + d.key))
            block = es.enter_context(nc.Block())

            def run(eng, h):
                for o in eng.ops:
                    if o[0] == "wait":
                        h.wait_ge(sems[o[1]], o[2])
                    elif o[0] == "op":
                        o[1](h).then_inc(sems[eng.key], 1)
                    elif o[0] == "cc":
                        h.collective_compute("AllGather", ALU.bypass, replica_groups=PAIRS, ins=[o[1]],
                                             outs=[o[2]]).then_inc(sems[o[3]], 1)
                    else:
                        h.dma_start(out=o[1], in_=o[2]).then_inc(sems[o[3]], 16)

            @block.tensor
            def _(h):
                run(self.engs["pe"], h)

            @block.scalar
            def _(h):
                run(self.engs["act"], h)

            @block.vector
            def _(h):
                run(self.engs["dve"], h)

            @block.gpsimd
            def _(h):
                run(self.engs["pool"], h)

            @block.sync
            def _(h):
                run(self.engs["sp"], h)


class Ring:
    def __init__(self, items):
        self.items = items
        self.i = 0

    def get(self):
        it = self.items[self.i % len(self.items)]
        self.i += 1
        return it


def host_consts():
    kk = np.arange(128)[:, None]
    c = np.arange(2432)[None, :]
    d = c - kk - 384
    mA = (((d >= 0) & (d <= 128)).astype(np.float32) + ((d % 4 == 0) & (d >= 0) & (d <= 512)).astype(np.float32)
          + ((d % 16 == 0) & (d >= 0) & (d <= 2048)).astype(np.float32))
    c = np.arange(512)[None, :]
    d = c - 128 - kk
    mB = ((d >= 0) & (d <= 127)).astype(np.float32)
    c = np.arange(896)[None, :]
    d = c - 384 - kk
    mC = (d >= 0).astype(np.float32)
    cmask = np.concatenate([mA, mB, mC], axis=1).astype(np.float32)
    ident = np.eye(128, dtype=np.float32)
    esel = np.zeros((8, 8, 128), np.float32)
    for b in range(8):
        esel[b, b, :] = 1.0
    esel = esel.reshape(8, 1024)
    t = np.arange(16)[:, None]
    blk = np.arange(8)[None, :]
    past = blk < (t // 2)
    own = blk == (t // 2)
    pn = np.stack([np.where(past, 0.0, np.where(own, 1e30, -1e30)),
                   np.where(past | own, -30000.0, -60000.0)]).astype(np.float32).reshape(1, 256)
    p = np.arange(128)
    inv128 = np.exp(np.arange(64, dtype=np.float32) * np.float32(-2.0 * math.log(10000.0) / 128)).astype(np.float32)
    inv64 = np.exp(np.arange(32, dtype=np.float32) * np.float32(-2.0 * math.log(10000.0) / 64)).astype(np.float32)
    invf = np.stack([inv128[p % 64], inv64[p % 32]], axis=1).astype(np.float32)
    return dict(cmask=cmask, ident=ident, esel=esel, pastc=pn, invf=invf)


def build_program(n_layers=4):
    nc = bass.Bass("TRN2", target_bir_lowering=False)
    em = Emitter()
    es = ExitStack()

    def dram(name, shape, dt, kind="ExternalInput"):
        return nc.dram_tensor(name, shape, dt, kind=kind).ap()

    x_in = dram("x", [S, DM], F32)
    mem_in = dram("mem", [256, DM], F32)
    pos_in = dram("pos", [1, S], I32)
    w_in_d = [dram("even_w_in", [2, DM, 3584], F32), dram("odd_w_in", [2, DM, 2304], F32)]
    w_mem_d = [dram("even_w_mem_kv", [2, DM, 512], F32), dram("odd_w_mem_kv", [2, DM, 512], F32)]
    w_out_d = [dram("even_w_out", [2, DM, DM], F32), dram("odd_w_out", [2, DM, DM], F32)]
    ncols_d = dram("ncols", [5, 128, 16], F32)
    fin_d = dram("final_norm", [1, DM], F32)
    sinks_d = dram("sinks2", [2, 2, 6], F32)
    cmask_d = dram("cmask", [128, 3840], F32)
    ident_d = dram("ident", [128, 128], F32)
    esel_d = dram("esel", [8, 1024], F32)
    pastc_d = dram("pastc", [1, 256], F32)
    invf_d = dram("invf", [128, 2], F32)
    out_d = dram("out", [S, DM], F32, kind="ExternalOutput")
    xs_d = dram("xs_scratch", [S, DM], F32, kind="Internal")
    memn_d = dram("memn_scratch", [128, 4096], BF16, kind="Internal")
    ybounce_d = [[nc.dram_tensor("ybounce%d_%d" % (i, j), [256, S], BF16).ap() for j in range(4)]
                 for i in range(n_layers)]
    ygath_d = [[nc.dram_tensor("ygath%d_%d" % (i, j), [512, S], BF16).ap() for j in range(4)]
               for i in range(n_layers)]
    cc_ds = [[em.new_dsem() for j in range(4)] for _ in range(n_layers)]
    gin_ds = [[em.new_dsem() for r in range(2)] for j in range(4)]
    gout_ds = [em.new_dsem() for j in range(4)]

    def sb(name, shape, dt):
        return es.enter_context(nc.sbuf_tensor(name, shape, dt))

    R1 = sb("R1", [128, 32768], BF16)
    R2 = sb("R2", [128, 32768], BF16)
    R3 = sb("R3", [128, 16896], BF16)
    cosT = sb("cosT", [128, S], F32)
    sinS = sb("sinS", [128, S], F32)
    mkT = sb("mkT", [128, 2, 256], BF16)
    mvT = sb("mvT", [128, 2, 2, 128], BF16)
    wslots = [sb("wslot%d" % i, [128, 16, 128], BF16) for i in range(3)]
    cmask = sb("cmaskb", [128, 3840], BF16)
    identf = sb("identf", [128, 128], F32)
    identb = sb("identb", [128, 128], BF16)
    ones = sb("ones", [128, 128], BF16)
    ones_lo = sb("ones_lo", [128, 128], BF16)
    ones_hi = sb("ones_hi", [128, 128], BF16)
    esel = sb("eselb", [8, 1024], BF16)
    pastc = sb("pastcb", [128, 256], F32)
    invf = sb("invfb", [128, 2], F32)
    ncols = sb("ncolsb", [128, 5, 16], F32)
    esink = sb("esink", [128, 6], F32)
    small = sb("small", [128, 640], F32)
    smallb = sb("smallb", [128, 160], BF16)

    banks = [es.enter_context(nc.psum_tensor("bank%d" % i, [128, 512], F32)) for i in range(8)]
    PA = Ring([(banks[i], Buf("pa%d" % i)) for i in (0, 1)])
    PS = Ring([(banks[i], Buf("ps%d" % i)) for i in (2, 3, 4)])
    POD = Ring([(banks[i], Buf("pod%d" % i)) for i in (5, 6, 7)])
    PN = Ring(PA.items + PS.items + POD.items)

    hT = R1[:, :].rearrange("p (c n) -> p c n", c=16)
    wout = hT
    yT = R2[:, :].rearrange("p (c n) -> p c n", c=16)
    R2f = R2[:, :].bitcast(F32)
    NG = 2
    xstage = [R2f[:, g * 4096:(g + 1) * 4096].rearrange("p (j d) -> p j d", j=NG) for g in range(4)]
    R3f = R3[:, :].bitcast(F32)

    def r3b(kib_lo, kib_hi):
        return R3[:, kib_lo * 512:kib_hi * 512]

    def r3f(kib_lo, kib_hi):
        return R3f[:, kib_lo * 256:kib_hi * 256]

    qT = r3b(0, 4)
    kT0 = r3b(4, 8)
    kT1 = r3b(8, 12)
    biasT = r3b(8, 12)
    V0 = r3b(12, 16).rearrange("p (t d) -> p t d", t=16)
    V1 = r3b(16, 20).rearrange("p (t d) -> p t d", t=16)
    gT = r3b(20, 24)
    t1 = r3f(24, 26)
    t2 = r3f(26, 28)
    rD = t2
    ytmp = t1
    Pt = [r3b(28 + i, 29 + i) for i in range(3)] + [r3b(32, 33)]
    vTtmp = r3b(31, 32)
    junk = r3b(28, 32)
    posi = R2f[:, 0:2048].bitcast(I32)
    angf = R2f[:, 2048:4096]
    kff = R2f[:, 4096:6144]
    rr = R2f[:, 6144:8192]
    memnT = r3b(16, 24).rearrange("p (c n) -> p c n", c=16)
    xpieces = [r3f(2 * i, 2 * i + 2) for i in range(4)]
    xrow = [r3f(8, 16), r3f(16, 24)]
    gfin = r3f(24, 32)
    junkO = r3b(0, 4)

    B = {}

    def buf(name):
        if name not in B:
            B[name] = Buf(name)
        return B[name]

    PP = Ring([(Pt[i], buf("P%d" % i)) for i in range(4)])
    XP = Ring([(xpieces[i], buf("xp%d" % i), em.new_dsem(), em.new_dsem()) for i in range(4)])
    XR = Ring([(xrow[i], buf("xr%d" % i), em.new_dsem(), em.new_dsem()) for i in range(2)])
    XST = Ring([(xstage[i], buf("xst%d" % i), em.new_dsem()) for i in range(4)])
    XH = Ring([(r3b(8 * i, 8 * i + 8).rearrange("p (j d) -> p j d", j=NG), buf("xh%d" % i)) for i in range(2)])
    xs_bufs = [buf("xs%d" % t) for t in range(16)]
    misc_ds = em.new_dsem()
    miscp_ds = em.new_dsem()
    rope_ds = em.new_dsem()
    memn_ds = em.new_dsem()
    out_buf = buf("out")
    out_ds = em.new_dsem()
    wout_ds = [em.new_dsem() for _ in range(4)]

    def MM(out, lhsT, rhs, start, stop, reads, writes):
        em.op("pe", lambda h: h.matmul(out, lhsT=lhsT, rhs=rhs, start=start, stop=stop), reads, writes)

    def TR(out, in_, ident, reads, writes):
        em.op("pe", lambda h: h.transpose(out, in_, ident), reads, writes)

    def ACT(out, in_, func, reads, writes, bias=0.0, scale=1.0, accum_out=None):
        if accum_out is None:
            em.op("act", lambda h: h.activation(out=out, in_=in_, func=func, bias=bias, scale=scale), reads, writes)
        else:
            em.op("act", lambda h: h.activation(out=out, in_=in_, func=func, bias=bias, scale=scale,
                                                accum_out=accum_out), reads, writes)

    def ACTMUL(out, in_, mul, reads, writes):
        em.op("act", lambda h: h.mul(out=out, in_=in_, mul=mul), reads, writes)

    def ACTCOPY(out, in_, reads, writes):
        em.op("act", lambda h: h.copy(out=out, in_=in_), reads, writes)

    def TT(eng, out, in0, in1, op, reads, writes):
        em.op(eng, lambda h: h.tensor_tensor(out=out, in0=in0, in1=in1, op=op), reads, writes)

    def TS(eng, out, in0, s1, s2, op0, op1, reads, writes):
        if op1 is None:
            em.op(eng, lambda h: h.tensor_scalar(out=out, in0=in0, scalar1=s1, scalar2=None, op0=op0), reads, writes)
        else:
            em.op(eng, lambda h: h.tensor_scalar(out=out, in0=in0, scalar1=s1, scalar2=s2, op0=op0, op1=op1),
                  reads, writes)

    def STT(eng, out, in0, scalar, in1, op0, op1, reads, writes):
        em.op(eng, lambda h: h.scalar_tensor_tensor(out=out, in0=in0, scalar=scalar, in1=in1, op0=op0, op1=op1),
              reads, writes)

    def CP(eng, out, in_, reads, writes):
        em.op(eng, lambda h: h.tensor_copy(out=out, in_=in_), reads, writes)

    def MEMSET(eng, ap, val, writes):
        em.op(eng, lambda h: h.memset(ap, val), (), writes)

    def RECIP(out, in_, reads, writes):
        em.op("dve", lambda h: h.reciprocal(out=out, in_=in_), reads, writes)

    def DMA(q, out, in_, ds, reads, writes):
        em.dma([(q, out, in_)], ds, reads, writes)

    cb = buf("consts")
    DMA("pool", cmask[:], cmask_d, miscp_ds, [], [cb])
    DMA("sp", identf[:], ident_d, misc_ds, [], [cb])
    DMA("pool", identb[:], ident_d, miscp_ds, [], [cb])
    DMA("pool", esel[:], esel_d, miscp_ds, [], [cb])
    DMA("sp", pastc[:], pastc_d.to_broadcast([128, 256]), misc_ds, [], [cb])
    DMA("sp", invf[:], invf_d, misc_ds, [], [cb])
    DMA("sp", ncols[:], ncols_d.rearrange("l p c -> p l c"), misc_ds, [], [cb])
    MEMSET("pool", ones[:], 1.0, [cb])
    MEMSET("pool", ones_lo[:], 0.0, [cb])
    MEMSET("pool", ones_hi[:], 0.0, [cb])
    em.fence()
    MEMSET("pool", ones_lo[:, 0:64], 1.0, [cb])
    MEMSET("pool", ones_hi[:, 64:128], 1.0, [cb])
    mA = cmask[:, 0:2432]
    mB = cmask[:, 2432:2944]
    mC = cmask[:, 2944:3840]
    em.fence()

    wring = [(wslots[i], buf("w%d" % i), em.new_dsem()) for i in range(3)]
    wq = []
    wstate = dict(loaded=0, used=0)
    loaded_items = []

    def w_issue():
        i = wstate["loaded"]
        spec = wq[i]
        ap, bf, ds = wring[i % 3]
        if spec["zero"]:
            MEMSET("pool", ap[:, :, :], 0.0, [bf])
        parts = []
        for (lo, hi, src) in spec["segs"]:
            parts.append(("pool", ap[:, :, lo:hi], src.rearrange("(c p) n -> p c n", p=128)))
        em.dma(parts, ds, [], [bf])
        wstate["loaded"] += 1

    def w_get(tag):
        i = wstate["used"]
        assert wq[i]["tag"] == tag, (wq[i]["tag"], tag)
        while wstate["loaded"] < min(len(wq), i + 3):
            w_issue()
        wstate["used"] += 1
        ap, bf, ds = wring[i % 3]
        return ap, bf

    def spec(tag, segs, zero=False):
        return dict(tag=tag, segs=segs, zero=zero)

    def layer_specs(l):
        par = l % 2
        li = l // 2
        win = w_in_d[par][li]
        wm = w_mem_d[par][li]
        sp_ = []
        for h in range(2):
            sp_.append(spec(("mk", l, h), [(0, 128, wm[:, h * 128:(h + 1) * 128])]))
            sp_.append(spec(("mv", l, h), [(0, 128, wm[:, 256 + h * 128:256 + (h + 1) * 128])]))
        if par == 0:
            E = EVEN_OFF
            for c in EVEN_PROC:
                sp_.append(spec(("g", l, c), [(0, 128, win[:, E["gate"] + c * 128:E["gate"] + (c + 1) * 128])]))
                if c < 3:
                    names = ("qa", "ka", "va")
                    hh = c
                elif c < 6:
                    names = ("qb", "kb", "vb")
                    hh = c - 3
                else:
                    names = ("qm",)
                    hh = c - 6
                for nm in names:
                    sp_.append(spec((nm, l, c), [(0, 128, win[:, E[nm] + hh * 128:E[nm] + (hh + 1) * 128])]))
        else:
            O = ODD_OFF
            for h in range(2):
                c = 6 + h
                sp_.append(spec(("g", l, c), [(0, 128, win[:, O["gate"] + c * 128:O["gate"] + (c + 1) * 128])]))
                sp_.append(spec(("qm", l, c), [(0, 128, win[:, O["qm"] + h * 128:O["qm"] + (h + 1) * 128])]))
            for g, chunks in ((0, (0, 1, 2, 3)), (1, (4, 5))):
                k0 = O["kc"] + g * 64
                v0 = O["vc"] + g * 64
                sp_.append(spec(("k0", l, g), [(0, 32, win[:, k0:k0 + 32]), (64, 96, win[:, k0 + 32:k0 + 64])], True))
                sp_.append(spec(("k1", l, g), [(32, 64, win[:, k0:k0 + 32]), (96, 128, win[:, k0 + 32:k0 + 64])], True))
                sp_.append(spec(("v0", l, g), [(0, 64, win[:, v0:v0 + 64])], True))
                sp_.append(spec(("v1", l, g), [(64, 128, win[:, v0:v0 + 64])], True))
                for c in chunks:
                    sp_.append(spec(("g", l, c), [(0, 128, win[:, O["gate"] + c * 128:O["gate"] + (c + 1) * 128])]))
                    q0 = O["qc"] + c * 128
                    sp_.append(spec(("qc", l, c), [(0, 32, win[:, q0:q0 + 32]), (64, 96, win[:, q0 + 32:q0 + 64]),
                                                   (32, 64, win[:, q0 + 64:q0 + 96]),
                                                   (96, 128, win[:, q0 + 96:q0 + 128])]))
        return sp_

    for l in range(n_layers):
        wq.extend(layer_specs(l))

    hTb = buf("R1")
    yTb = [buf("yT%d" % c) for c in range(16)]
    sm = buf("small")

    def norm_phase(src_rows, ntiles, gidx, dst, dst_buf, src_bufs):
        ngroups = (ntiles + NG - 1) // NG
        st = {}

        def stage_a(g):
            nt = min(NG, ntiles - g * NG)
            xg, xb, xds = XST.get()
            em.dma([("sp", xg[:, 0:nt, :], src_rows(g * NG, nt).rearrange("(j p) d -> p j d", p=128))], xds,
                   [src_bufs[g * NG + j] for j in range(nt)], [xb])
            sg = buf("nstat%d" % g)
            for j in range(nt):
                t = g * NG + j
                ACT(junk, xg[:, j, :], AF.Square, [xb], [buf("junk"), sg], accum_out=small[:, t:t + 1])
            ACT(small[:, 16 + g * NG:16 + g * NG + nt], small[:, g * NG:g * NG + nt], AF.Sqrt, [sg], [sg], bias=EPS,
                scale=1.0 / DM)
            RECIP(small[:, 16 + g * NG:16 + g * NG + nt], small[:, 16 + g * NG:16 + g * NG + nt], [sg], [sg])
            xh, xhb = XH.get()
            for j in range(nt):
                t = g * NG + j
                TS("dve", xh[:, j, :], xg[:, j, :], small[:, 16 + t:17 + t], None, ALU.mult, None, [sg, xb], [xhb])
            st[g] = (xh, xhb, nt)

        def stage_b(g):
            xg, xb, nt = st.pop(g)
            for c in range(16):
                ps32, pb = PN.get()
                ps = ps32[:, :].bitcast(BF16)
                for j in range(nt):
                    TR(ps[:, j * 128:(j + 1) * 128], xg[:, j, c * 128:(c + 1) * 128], identb[:], [xb, cb], [pb])
                dcb = buf("%s_c%d" % (dst_buf.name, c))
                if c % 2 == 0:
                    ACTMUL(dst[:, c, g * NG * 128:g * NG * 128 + nt * 128], ps[:, 0:nt * 128],
                           ncols[:, gidx, c:c + 1], [pb, cb], [dcb])
                else:
                    TS("dve", dst[:, c, g * NG * 128:g * NG * 128 + nt * 128], ps[:, 0:nt * 128],
                       ncols[:, gidx, c:c + 1], None, ALU.mult, None, [pb, cb], [dcb])

        stage_a(0)
        for g in range(ngroups):
            if g + 1 < ngroups:
                stage_a(g + 1)
            stage_b(g)

    tb = buf("tables")

    def rope_tables(col):
        pb_ = buf("ropetmp")
        T0, T1, T2, T3 = r3f(8, 10), r3f(10, 12), r3f(12, 14), r3f(14, 16)
        for ch in range(4):
            cs = slice(ch * 512, (ch + 1) * 512)
            DMA("sp", T0.bitcast(I32), pos_in[:, cs].to_broadcast([128, 512]), rope_ds, [], [pb_])
            CP("dve", T1, T0.bitcast(I32), [pb_], [pb_])
            TS("dve", T1, T1, invf[:, col:col + 1], None, ALU.mult, None, [pb_, cb], [pb_])
            TS("dve", T2, T1, 1.0 / TWO_PI, None, ALU.mult, None, [pb_], [pb_])
            CP("dve", T3.bitcast(I32), T2, [pb_], [pb_])
            CP("dve", T2, T3.bitcast(I32), [pb_], [pb_])
            STT("dve", T1, T2, -CW1, T1, ALU.mult, ALU.add, [pb_], [pb_])
            STT("dve", T1, T2, -CW2, T1, ALU.mult, ALU.add, [pb_], [pb_])
            TS("dve", T2, T1, math.pi, -TWO_PI, ALU.is_gt, ALU.mult, [pb_], [pb_])
            TT("dve", T3, T1, T2, ALU.add, [pb_], [pb_])
            TS("dve", T2, T3, -math.pi, TWO_PI, ALU.is_lt, ALU.mult, [pb_], [pb_])
            TT("dve", T3, T3, T2, ALU.add, [pb_], [pb_])
            ACT(sinS[0:64, cs], T3[0:64, :], AF.Sin, [pb_], [tb])
            ACT(sinS[64:128, cs], T3[64:128, :], AF.Sin, [pb_], [tb], scale=-1.0)
            TS("dve", T0, T1, math.pi / 2, None, ALU.add, None, [pb_], [pb_])
            TS("dve", T2, T0, math.pi, -TWO_PI, ALU.is_gt, ALU.mult, [pb_], [pb_])
            TT("dve", T0, T0, T2, ALU.add, [pb_], [pb_])
            ACT(cosT[:, cs], T0, AF.Sin, [pb_], [tb])

    def proj_fm(tag, evac):
        wp, wb = w_get(tag)
        for tt in range(4):
            ps, pb = PA.get()
            for kc in range(16):
                MM(ps[:, :], wp[:, kc, :], hT[:, kc, tt * 512:(tt + 1) * 512], kc == 0, kc == 15, [wb, hTb], [pb])
            evac(ps, pb, tt)

    def rope_evac(dst, dst_buf):
        def f(ps, pb, tt):
            sl = slice(tt * 512, (tt + 1) * 512)
            TT("dve", t1, ps[:, :], cosT[:, sl], ALU.mult, [pb, tb], [buf("t1")])
            TT("dve", t2[0:64, :], ps[64:128, :], sinS[64:128, sl], ALU.mult, [pb, tb], [buf("t2")])
            TT("dve", t2[64:128, :], ps[0:64, :], sinS[0:64, sl], ALU.mult, [pb, tb], [buf("t2")])
            TT("pool", dst[:, sl], t1, t2, ALU.add, [buf("t1"), buf("t2")], [dst_buf])
        return f

    def copy_evac(dst, dst_buf):
        def f(ps, pb, tt):
            ACTCOPY(dst[:, tt * 512:(tt + 1) * 512], ps[:, :], [pb], [dst_buf])
        return f

    def silu_evac(ps, pb, tt):
        ACT(gT[:, tt * 512:(tt + 1) * 512], ps[:, :], AF.Silu, [pb], [buf("gT")])

    def v_evac(Vt, vbuf):
        def f(ps, pb, tt):
            ACTCOPY(vTtmp, ps[:, :], [pb], [buf("vTtmp")])
            p2, p2b = PA.get()
            p2v = p2[:, :].bitcast(BF16)
            for j in range(4):
                TR(p2v[:, j * 128:(j + 1) * 128], vTtmp[:, j * 128:(j + 1) * 128], identb[:], [buf("vTtmp"), cb], [p2b])
            CP("dve", Vt[:, tt * 4:(tt + 1) * 4, :], p2v[:, 0:512].rearrange("p (t d) -> p t d", t=4), [p2b], [vbuf])
        return f

    def attn_head(ranges):
        flat = [(ri, bj) for ri, r in enumerate(ranges) for bj in range(len(r[0]))]
        Ps = {}
        acc = {}

        def emit_qk(g):
            ri, bj = flat[g]
            blocks, Wq, Ws, scale, finish = ranges[ri]
            b = blocks[bj]
            Sx, Sb = PS.get()
            for (lhsT, rhs, lo, hi) in b["qk"]:
                MM(Sx[:, lo:hi], lhsT, rhs, True, b.get("bias") is None, b["reads"], [Sb])
                if b.get("bias") is not None:
                    bl, br = b["bias"]
                    MM(Sx[:, lo:hi], bl, br, False, True, [cb, buf("biasT")], [Sb])
            P, Pb = PP.get()
            ACT(P[:, 0:Ws], Sx[:, 0:Ws], AF.Exp, [Sb], [Pb], scale=scale)
            for (lo, hi, m) in b.get("masks", ()):
                TT("dve", P[:, lo:hi], P[:, lo:hi], m, ALU.mult, [Pb, cb], [Pb])
            Ps[g] = (P, Pb)

        def emit_pv(g):
            ri, bj = flat[g]
            blocks, Wq, Ws, scale, finish = ranges[ri]
            nb = len(blocks)
            b = blocks[bj]
            if b.get("packed"):
                if bj == 0:
                    O, Ob = POD.get()
                    acc[ri] = (O, Ob, None, None)
                O, Ob, Dn, Db = acc[ri]
                P, Pb = Ps.pop(g)
                for (vl, ol, lo, hi) in b["pv"]:
                    MM(O[:, lo:hi], vl, P[:, lo:hi], bj == 0, bj == nb - 1, [Pb] + b["vreads"], [Ob])
                if bj == nb - 1:
                    finish(O, Ob, Dn, Db)
                return
            if bj == 0:
                Dn, Db = POD.get()
                O, Ob = POD.get()
                acc[ri] = (O, Ob, Dn, Db)
            O, Ob, Dn, Db = acc[ri]
            P, Pb = Ps.pop(g)
            n = len(b["pv"])
            for idx, (vl, ol, lo, hi) in enumerate(b["pv"]):
                first = (bj == 0 and idx == 0)
                last = (bj == nb - 1 and idx == n - 1)
                MM(O[:, 0:Wq], vl, P[:, lo:hi], first, last, [Pb] + b["vreads"], [Ob])
                MM(Dn[:, 0:Wq], ol, P[:, lo:hi], first, last, [Pb, cb], [Db])
            if bj == nb - 1:
                finish(O, Ob, Dn, Db)

        ng = len(flat)
        emit_qk(0)
        if ng > 1:
            emit_qk(1)
        for g in range(ng):
            if g + 2 < ng:
                emit_qk(g + 2)
            emit_pv(g)

    def finish_std(c, q0, Wq, sink_col=None):
        def f(O, Ob, Dn, Db):
            rb, yb = buf("t2"), buf("t1")
            if sink_col is None:
                ACT(rD[:, 0:Wq], Dn[:, 0:Wq], AF.Ln, [Db], [rb])
            else:
                ACT(rD[:, 0:Wq], Dn[:, 0:Wq], AF.Ln, [Db, cb], [rb], bias=sink_col)
            ACT(rD[:, 0:Wq], rD[:, 0:Wq], AF.Exp, [rb], [rb], scale=-1.0)
            TT("dve", ytmp[:, 0:Wq], O[:, 0:Wq], rD[:, 0:Wq], ALU.mult, [Ob, rb], [yb])
            TT("pool", yT[:, c, q0:q0 + Wq], ytmp[:, 0:Wq], gT[:, q0:q0 + Wq], ALU.mult, [yb, buf("gT")], [yTb[c]])
        return f

    def finish_swa(c, q0):
        def f(O, Ob, Dn, Db):
            rb, yb = buf("t2"), buf("t1")
            ACTCOPY(rD[0:64, 0:256], O[64:128, 0:256], [Ob], [rb])
            ACTCOPY(rD[64:128, 0:256], O[0:64, 256:512], [Ob], [rb])
            ACT(rD[:, 0:256], rD[:, 0:256], AF.Ln, [rb, cb], [rb], bias=esink[:, c:c + 1])
            ACT(rD[:, 0:256], rD[:, 0:256], AF.Exp, [rb], [rb], scale=-1.0)
            TT("dve", ytmp[0:64, 0:256], O[0:64, 0:256], rD[0:64, 0:256], ALU.mult, [Ob, rb], [yb])
            TT("dve", ytmp[64:128, 0:256], O[64:128, 256:512], rD[64:128, 0:256], ALU.mult, [Ob, rb], [yb])
            TT("pool", yT[:, c, q0:q0 + 256], ytmp[:, 0:256], gT[:, q0:q0 + 256], ALU.mult, [yb, buf("gT")], [yTb[c]])
        return f

    SC128 = 128 ** -0.5
    SC64 = 64 ** -0.5
    qb_, kb0_, kb1_, vb0_, vb1_ = buf("qT"), buf("kT0"), buf("kT1"), buf("V0"), buf("V1")

    def mem_head(l, c, h, qtag):
        proj_fm(("g", l, c), silu_evac)
        proj_fm(qtag, copy_evac(qT, qb_))
        rngs = []
        for r in range(4):
            q0 = r * 512
            blocks = []
            for kb in range(2):
                blocks.append(dict(qk=[(mkT[:, h, kb * 128:(kb + 1) * 128], qT[:, q0:q0 + 512], 0, 512)],
                                   reads=[buf("mk"), qb_], pv=[(mvT[:, kb, h, :], ones[:], 0, 512)],
                                   vreads=[buf("mk")]))
            rngs.append((blocks, 512, 512, SC128, finish_std(c, q0, 512)))
        attn_head(rngs)

    def phase_M(l):
        DMA("sp", r3b(16, 24), memn_d, memn_ds, [buf("memn_d")], [buf("memnT")])
        mb = buf("mk")
        for h in range(2):
            wp, wb = w_get(("mk", l, h))
            ps, pb = PA.get()
            for kc in range(16):
                MM(ps[:, 0:256], wp[:, kc, :], memnT[:, kc, :], kc == 0, kc == 15, [wb, buf("memnT")], [pb])
            ACTCOPY(mkT[:, h, :], ps[:, 0:256], [pb], [mb])
            wp, wb = w_get(("mv", l, h))
            ps, pb = PA.get()
            for tl in range(2):
                for kc in range(16):
                    MM(ps[:, tl * 128:(tl + 1) * 128], memnT[:, kc, tl * 128:(tl + 1) * 128], wp[:, kc, :], kc == 0,
                       kc == 15, [wb, buf("memnT")], [pb])
            CP("dve", mvT[:, :, h, :], ps[:, 0:256].rearrange("p (t d) -> p t d", t=2), [pb], [mb])

    def even_layer(l):
        done = set()
        for c in EVEN_PROC:
            if c < 3:
                proj_fm(("g", l, c), silu_evac)
                proj_fm(("qa", l, c), rope_evac(qT, qb_))
                proj_fm(("ka", l, c), rope_evac(kT0, kb0_))
                proj_fm(("va", l, c), v_evac(V0, vb0_))
                rngs = []
                for r in range(4):
                    q0 = r * 512
                    blocks = []
                    for kb in range(4 * r + 4):
                        delta = q0 - 128 * kb
                        blocks.append(dict(qk=[(kT0[:, kb * 128:(kb + 1) * 128], qT[:, q0:q0 + 512], 0, 512)],
                                           reads=[kb0_, qb_], masks=[(0, 512, mA[:, delta + 384:delta + 384 + 512])],
                                           pv=[(V0[:, kb, :], ones[:], 0, 512)], vreads=[vb0_]))
                    rngs.append((blocks, 512, 512, SC128, finish_std(c, q0, 512)))
                attn_head(rngs)
            elif c < 6:
                proj_fm(("g", l, c), silu_evac)
                proj_fm(("qb", l, c), rope_evac(qT, qb_))
                proj_fm(("kb", l, c), rope_evac(kT0, kb0_))
                proj_fm(("vb", l, c), v_evac(V0, vb0_))
                if c == EVEN_PROC[-1]:
                    prefetch_wout(l)
                km = small[:, 32:40]
                kmt = small[:, 40:48]
                kmh = smallb[:, 0:8]
                kml = smallb[:, 8:16]
                gm = small[:, 64:192]
                mx = small[:, 192:320]
                bs = small[:, 320:448]
                bsb = smallb[:, 16:144]
                em.op("dve", lambda h: h.reduce_sum(out=km, in_=kT0.rearrange("p (b n) -> p b n", b=8), axis=AX.X),
                      [kb0_], [sm])
                TS("dve", km, km, 1.0 / 256.0, None, ALU.mult, None, [sm], [sm])
                CP("dve", kmh, km, [sm], [sm])
                TT("dve", kmt, km, kmh, ALU.subtract, [sm], [sm])
                CP("dve", kml, kmt, [sm], [sm])
                gp, gpb = PA.get()
                for t in range(16):
                    MM(gp[:, t * 8:(t + 1) * 8], qT[:, t * 128:(t + 1) * 128], kmh, True, False, [qb_, sm], [gpb])
                    MM(gp[:, t * 8:(t + 1) * 8], qT[:, t * 128:(t + 1) * 128], kml, False, True, [qb_, sm], [gpb])
                TT("dve", gm, gp[:, 0:128], pastc[:, 0:128], ALU.add, [gpb, cb], [sm])
                for t in range(16):
                    em.op("dve", (lambda t_: (lambda h: h.max(out=mx[:, t_ * 8:(t_ + 1) * 8],
                                                             in_=gm[:, t_ * 8:(t_ + 1) * 8])))(t), [sm], [sm])
                for t in range(16):
                    TS("dve", bs[:, t * 8:(t + 1) * 8], gm[:, t * 8:(t + 1) * 8], mx[:, t * 8 + 3:t * 8 + 4], None,
                       ALU.is_ge, None, [sm], [sm])
                STT("dve", bsb, bs, 30000.0, pastc[:, 128:256], ALU.mult, ALU.add, [sm, cb], [sm])
                for g4 in range(4):
                    p2, p2b = PA.get()
                    p2v = p2[:, :].bitcast(BF16)
                    for j in range(4):
                        t = g4 * 4 + j
                        TR(p2v[0:8, j * 128:(j + 1) * 128], bsb[:, t * 8:(t + 1) * 8], identb[:], [sm, cb], [p2b])
                    CP("dve", biasT[0:8, g4 * 512:(g4 + 1) * 512], p2v[0:8, 0:512], [p2b], [buf("biasT")])
                rngs = []
                for r in range(4):
                    q0 = r * 512
                    blocks = []
                    for kb in range(4 * r + 4):
                        b_ = dict(qk=[(kT0[:, kb * 128:(kb + 1) * 128], qT[:, q0:q0 + 512], 0, 512)],
                                  reads=[kb0_, qb_],
                                  bias=(esel[0:8, (kb // 2) * 128:(kb // 2 + 1) * 128], biasT[0:8, q0:q0 + 512]),
                                  pv=[(V0[:, kb, :], ones[:], 0, 512)], vreads=[vb0_])
                        if kb >= 4 * r:
                            delta = q0 - 128 * kb
                            b_["masks"] = [(0, 512, mC[:, delta + 384:delta + 384 + 512])]
                        blocks.append(b_)
                    rngs.append((blocks, 512, 512, SC128, finish_std(c, q0, 512)))
                attn_head(rngs)
            else:
                mem_head(l, c, c - 6, ("qm", l, c))
            done.add(c)
            if (c ^ 1) in done:
                exchange_part(l, c // 2)

    def odd_layer(l):
        li = l // 2
        DMA("sp", esink[0:64, :], sinks_d[li, 0:1, :].to_broadcast([64, 6]), misc_ds, [], [buf("esink")])
        DMA("sp", esink[64:128, :], sinks_d[li, 1:2, :].to_broadcast([64, 6]), misc_ds, [], [buf("esink")])
        ACT(esink[:, :], esink[:, :], AF.Exp, [buf("esink")], [cb])
        for h in range(2):
            mem_head(l, 6 + h, h, ("qm", l, 6 + h))
        exchange_part(l, 3)
        for g, chunks in ((0, (0, 1, 2, 3)), (1, (4, 5))):
            proj_fm(("k0", l, g), rope_evac(kT0, kb0_))
            proj_fm(("k1", l, g), rope_evac(kT1, kb1_))
            proj_fm(("v0", l, g), v_evac(V0, vb0_))
            proj_fm(("v1", l, g), v_evac(V1, vb1_))
            for c in chunks:
                proj_fm(("g", l, c), silu_evac)
                proj_fm(("qc", l, c), rope_evac(qT, qb_))
                if c == 5:
                    prefetch_wout(l)
                rngs = []
                for i in range(8):
                    q0 = i * 256
                    blocks = []
                    for kb in (2 * i - 1, 2 * i, 2 * i + 1):
                        if kb < 0:
                            continue
                        delta = q0 - 128 * kb
                        m = mB[:, delta + 128:delta + 128 + 256]
                        ks = slice(kb * 128, (kb + 1) * 128)
                        blocks.append(dict(qk=[(kT0[:, ks], qT[:, q0:q0 + 256], 0, 256),
                                               (kT1[:, ks], qT[:, q0:q0 + 256], 256, 512)],
                                           reads=[kb0_, kb1_, qb_], masks=[(0, 256, m), (256, 512, m)],
                                           pv=[(V0[:, kb, :], ones_lo[:], 0, 256), (V1[:, kb, :], ones_hi[:], 256, 512)],
                                           vreads=[vb0_, vb1_]))
                    rngs.append((blocks, 256, 512, SC64, finish_std(c, q0, 256, sink_col=esink[:, c:c + 1])))
                attn_head(rngs)
                if c % 2 == 1:
                    exchange_part(l, c // 2)

    def exchange_part(l, j):
        bb, gb = buf("ybounce%d_%d" % (l, j)), buf("ygath%d_%d" % (l, j))
        DMA("sp", ybounce_d[l][j].rearrange("(c p) n -> p c n", p=128), yT[:, 2 * j:2 * j + 2, :], gout_ds[j],
            yTb[2 * j:2 * j + 2], [bb])
        em.cc(ybounce_d[l][j], ygath_d[l][j], cc_ds[l][j], [bb], [gb])
        for r in range(2):
            DMA("sp", yT[:, r * 8 + 2 * j:r * 8 + 2 * j + 2, :],
                ygath_d[l][j][r * 256:(r + 1) * 256, :].rearrange("(c p) n -> p c n", p=128), gin_ds[j][r], [gb],
                yTb[r * 8 + 2 * j:r * 8 + 2 * j + 2])

    def prefetch_wout(l):
        par, li = l % 2, l // 2
        wo = w_out_d[par][li]
        for n in range(4):
            em.dma([("pool", wout[:, :, n * 512:(n + 1) * 512],
                     wo[:, n * 512:(n + 1) * 512].rearrange("(c p) n -> p c n", p=128))], wout_ds[n], [],
                   [hTb, buf("wout%d" % n)] if n == 0 else [buf("wout%d" % n)])

    def phase_O(l, ns=(0, 1, 2, 3)):
        par, li = l % 2, l // 2
        src = x_in if l == 0 else xs_d
        for n in ns:
            wb = buf("wout%d" % n)
            for t in range(16):
                xp, xb, ds_in, ds_out = XP.get()
                rows = slice(t * 128, (t + 1) * 128)
                cols = slice(n * 512, (n + 1) * 512)
                xsb = buf("xs%d_%d" % (t, n))
                DMA("sp", xp, src[rows, cols], ds_in, [xsb], [xb])
                ps, pb = PA.get()
                for kc in range(16):
                    MM(ps[:, :], yT[:, kc, rows], wout[:, kc, cols], kc == 0, kc == 15, [yTb[kc], wb], [pb])
                TT("dve", xp, ps[:, :], xp, ALU.add, [pb, xb], [xb])
                DMA("sp", xs_d[rows, cols], xp, ds_out, [xb], [xsb, xs_bufs[t]])

    def phase_F():
        DMA("sp", gfin, fin_d.to_broadcast([128, DM]), misc_ds, [], [buf("gfin")])
        for t in range(16):
            xr, xb, ds_in, ds_out = XR.get()
            rows = slice(t * 128, (t + 1) * 128)
            DMA("sp", xr, xs_d[rows, :], ds_in, [xs_bufs[t]], [xb])
            sg = buf("fstat%d" % t)
            ACT(junkO, xr, AF.Square, [xb], [buf("junkO"), sg], accum_out=small[:, t:t + 1])
            ACT(small[:, 16 + t:17 + t], small[:, t:t + 1], AF.Sqrt, [sg], [sg], bias=EPS, scale=1.0 / DM)
            RECIP(small[:, 16 + t:17 + t], small[:, 16 + t:17 + t], [sg], [sg])
            STT("dve", xr, xr, small[:, 16 + t:17 + t], gfin, ALU.mult, ALU.mult, [sg, xb, buf("gfin")], [xb])
            DMA("sp", out_d[rows, :], xr, ds_out, [xb], [out_buf])

    memb = buf("memn_sb")
    norm_phase(lambda t0, n: mem_in[t0 * 128:(t0 + n) * 128, :], 2, 4, memnT, memb, [buf("memsrc")] * 2)
    em.fence()
    DMA("sp", memn_d, r3b(16, 24), misc_ds, [memb], [buf("memn_d")])
    em.fence()

    xin_bufs = [buf("xin")] * 16
    rope_tables(0)
    phase_M(0)
    em.fence()
    for l in range(n_layers):
        par = l % 2
        src = x_in if l == 0 else xs_d
        norm_phase(lambda t0, n, s=src: s[t0 * 128:(t0 + n) * 128, :], 16, l, hT, hTb,
                   xin_bufs if l == 0 else xs_bufs)
        em.fence()
        if par == 0:
            even_layer(l)
        else:
            odd_layer(l)
        em.fence(exclude=wout_ds + [d for j in range(4) for d in gin_ds[j]] + [d for dl in cc_ds for d in dl]
                 + gout_ds)
        phase_O(l, (0, 1, 2))
        if l + 1 < n_layers:
            rope_tables((l + 1) % 2)
            phase_M(l + 1)
        phase_O(l, (3,))
        em.fence()
    phase_F()
    em.fence()
    em.replay(nc)
    es.close()
    return nc


_CACHE = {}


def _slice_even(w, r):
    ids = [3 * r + j for j in range(3)]
    cols = []
    for base in (0, 768, 1536, 2304, 3072, 3840):
        for h in ids:
            cols.append(np.arange(base + h * 128, base + (h + 1) * 128))
    for h in (2 * r, 2 * r + 1):
        cols.append(np.arange(4608 + h * 128, 4608 + (h + 1) * 128))
    for gc in EVEN_ORDER[8 * r:8 * r + 8]:
        cols.append(np.arange(5120 + gc * 128, 5120 + (gc + 1) * 128))
    return np.ascontiguousarray(w[:, :, np.concatenate(cols)])


def _slice_odd(w, r):
    chunks = ODD_ORDER[8 * r:8 * r + 8]
    groups = [0, 1] if r == 0 else [2, 1]
    cols = []
    for gc in chunks[:6]:
        cols.append(np.arange(gc * 128, (gc + 1) * 128))
    for g in groups:
        cols.append(np.arange(1536 + g * 64, 1536 + (g + 1) * 64))
    for g in groups:
        cols.append(np.arange(1728 + g * 64, 1728 + (g + 1) * 64))
    for h in (2 * r, 2 * r + 1):
        cols.append(np.arange(1920 + h * 128, 1920 + (h + 1) * 128))
    for gc in chunks:
        cols.append(np.arange(2432 + gc * 128, 2432 + (gc + 1) * 128))
    return np.ascontiguousarray(w[:, :, np.concatenate(cols)])


def _slice_mem(w, r):
    cols = []
    for base in (0, 512):
        for h in (2 * r, 2 * r + 1):
            cols.append(np.arange(base + h * 128, base + (h + 1) * 128))
    return np.ascontiguousarray(w[:, :, np.concatenate(cols)])


def _perm_rows(w, order):
    rows = np.concatenate([np.arange(gc * 128, (gc + 1) * 128) for gc in order])
    return np.ascontiguousarray(w[:, rows, :])


def kernel(x, mem, positions, even_norm, even_w_in, even_w_mem_kv, even_w_out,
           odd_norm, odd_w_in, odd_w_mem_kv, odd_w_out, odd_sinks, mem_norm, final_norm, _n_layers=4):
    f32 = np.float32
    x = np.asarray(x, f32)
    mem = np.asarray(mem, f32)
    positions = np.asarray(positions, np.int32)
    even_w_in = np.asarray(even_w_in, f32)
    odd_w_in = np.asarray(odd_w_in, f32)
    even_w_mem_kv = np.asarray(even_w_mem_kv, f32)
    odd_w_mem_kv = np.asarray(odd_w_mem_kv, f32)
    consts = host_consts()
    norms = [np.asarray(even_norm, f32)[0], np.asarray(odd_norm, f32)[0], np.asarray(even_norm, f32)[1],
             np.asarray(odd_norm, f32)[1], np.asarray(mem_norm, f32)]
    ncols = np.stack([n.reshape(16, 128).T for n in norms]).astype(f32)
    sk = np.asarray(odd_sinks, f32)
    common = dict(even_w_out=_perm_rows(np.asarray(even_w_out, f32), EVEN_ORDER),
                  odd_w_out=_perm_rows(np.asarray(odd_w_out, f32), ODD_ORDER),
                  ncols=ncols, final_norm=np.asarray(final_norm, f32).reshape(1, DM), **consts)
    role = []
    for r in range(2):
        chunks = ODD_ORDER[8 * r:8 * r + 6]
        sinks2 = np.stack([sk[:, [2 * gc for gc in chunks]], sk[:, [2 * gc + 1 for gc in chunks]]], axis=1)
        role.append(dict(even_w_in=_slice_even(even_w_in, r), odd_w_in=_slice_odd(odd_w_in, r),
                         even_w_mem_kv=_slice_mem(even_w_mem_kv, r), odd_w_mem_kv=_slice_mem(odd_w_mem_kv, r),
                         sinks2=np.ascontiguousarray(sinks2.astype(f32))))
    if _n_layers not in _CACHE:
        _CACHE[_n_layers] = build_program(_n_layers)
    nc = _CACHE[_n_layers]
    in_maps = []
    for core in range(8):
        b, r = core // 2, core % 2
        m = dict(common)
        m.update(role[r])
        m["x"] = np.ascontiguousarray(x[b])
        m["mem"] = np.ascontiguousarray(mem[b])
        m["pos"] = np.ascontiguousarray(positions[b].reshape(1, S))
        in_maps.append(m)
    res = run_bass_kernel_spmd(nc, in_maps, core_ids=list(range(8)))
    out = np.stack([res.results[2 * b]["out"] for b in range(4)]).astype(f32)
    return out
```
